# Optimizing a Trainium2 kernel written in Bass

```python
import math
import jax
import jax.numpy as jnp
from jax import lax
import numpy as np

D_MODEL = 1024
BATCH = 16
SEQ = 2048
DEPTH = 2

CTX_LEN = 256
GRID_W = 64
ROPE_THETA = 10000.0
NORM_EPS = 1e-6
QBLOCK = 128

HEAD_DIM = 64
GROUP_W = D_MODEL // 4
MIX_W = 4 * GROUP_W

A_HEADS = GROUP_W // HEAD_DIM
A_KV_HEADS = A_HEADS // 2
SSM_HEAD_DIM = 64
SSM_HEADS = GROUP_W // SSM_HEAD_DIM
SSM_STATE = 128
SSM_GROUPS = 2
SSM_CONV = 3
SSM_CHUNK = 64
SSM_CONV_CH = GROUP_W + 2 * SSM_GROUPS * SSM_STATE
C_HEADS = GROUP_W // HEAD_DIM
C_KV_HEADS = C_HEADS // 2
WINDOW = 128
MLA_HEADS = 4
MLA_V_DIM = GROUP_W // MLA_HEADS
MLA_NOPE = 64
MLA_ROPE = 32
MLA_Q_RANK = 192
MLA_KV_RANK = 128
D_FF = 2816
FFN_CONV = 3

A_COLS = (A_HEADS + 2 * A_KV_HEADS) * HEAD_DIM
B_COLS = GROUP_W + SSM_CONV_CH + 2 * SSM_HEADS
C_COLS = (C_HEADS + 2 * C_KV_HEADS) * HEAD_DIM
D_COLS = MLA_Q_RANK + MLA_KV_RANK + MLA_ROPE
IN_COLS = A_COLS + B_COLS + C_COLS + D_COLS
IN_SPLITS = [A_COLS, A_COLS + B_COLS, A_COLS + B_COLS + C_COLS]

kernel_name = 'hybrid_parallel_group_dit_block'


def rmsnorm(x, w):
    xf = x.astype(jnp.float32)
    y = xf * lax.rsqrt(jnp.mean(xf * xf, axis=-1, keepdims=True) + NORM_EPS)
    return (y * w.astype(jnp.float32)).astype(x.dtype)


def axial_rope_tables(n, rot_dim):
    rows = n // GRID_W
    row = jnp.repeat(jnp.arange(rows, dtype=jnp.float32), GRID_W)
    col = jnp.tile(jnp.arange(GRID_W, dtype=jnp.float32), rows)
    half = rot_dim // 2
    inv_freq = jnp.power(ROPE_THETA, -jnp.arange(0, half, 2, dtype=jnp.float32) / half)
    ang_r = row[:, None] * inv_freq[None, :]
    ang_c = col[:, None] * inv_freq[None, :]
    ang = jnp.concatenate([ang_r, ang_r, ang_c, ang_c], axis=-1)
    return jnp.cos(ang), jnp.sin(ang)


def rotate_half(u):
    u1, u2 = jnp.split(u, 2, axis=-1)
    return jnp.concatenate([-u2, u1], axis=-1)


def apply_axial_rope(x, cos, sin):
    bshape = (1, cos.shape[0]) + (1,) * (x.ndim - 3) + (cos.shape[-1],)
    cos = cos.reshape(bshape)
    sin = sin.reshape(bshape)
    xf = x.astype(jnp.float32)
    x_row, x_col = jnp.split(xf, 2, axis=-1)
    rot = jnp.concatenate([rotate_half(x_row), rotate_half(x_col)], axis=-1)
    return (xf * cos + rot * sin).astype(x.dtype)


def dwconv_centred(u, w, b):
    k_w = w.shape[1]
    pad = k_w // 2
    t_len = u.shape[1]
    up = jnp.pad(u, ((0, 0), (pad, pad), (0, 0)))
    out = b
    for k in range(k_w):
        out = out + up[:, k:k + t_len, :] * w[:, k]
    return out


def softmax_sink(s, sink):
    m = jnp.maximum(jnp.max(s, axis=-1, keepdims=True), sink)
    e = jnp.exp(s - m)
    return e / (jnp.sum(e, axis=-1, keepdims=True) + jnp.exp(sink - m))


def attend(q, k, v, scale, sink=None):
    s = jnp.einsum('bqkgd,bnkd->bkgqn', q, k).astype(jnp.float32) * scale
    if sink is None:
        p = jax.nn.softmax(s, axis=-1)
    else:
        p = softmax_sink(s, sink)
    return jnp.einsum('bkgqn,bnkd->bqkgd', p.astype(v.dtype), v)


def dense_latent_attention(q, k_all, v_all, scale):
    b_sz, s_len = q.shape[:2]
    nb = s_len // QBLOCK
    qb = jnp.moveaxis(q.reshape((b_sz, nb, QBLOCK) + q.shape[2:]), 1, 0)
    out = lax.map(lambda blk: attend(blk, k_all, v_all, scale), qb)
    return jnp.moveaxis(out, 0, 1).reshape(b_sz, s_len, -1)


def split_gqa(p, n_heads, n_kv):
    b_sz, t_len = p.shape[:2]
    q, k, v = jnp.split(p, [n_heads * HEAD_DIM, (n_heads + n_kv) * HEAD_DIM], axis=-1)
    q = q.reshape(b_sz, t_len, n_kv, n_heads // n_kv, HEAD_DIM)
    k = k.reshape(b_sz, t_len, n_kv, HEAD_DIM)
    v = v.reshape(b_sz, t_len, n_kv, HEAD_DIM)
    return q, k, v


def mixer_gqa(p, pc, q_norm, k_norm, cos, sin, with_ctx):
    ql, kl, vl = split_gqa(p, A_HEADS, A_KV_HEADS)
    qc, kc, vc = split_gqa(pc, A_HEADS, A_KV_HEADS)
    ql = apply_axial_rope(rmsnorm(ql, q_norm), cos, sin)
    kl = apply_axial_rope(rmsnorm(kl, k_norm), cos, sin)
    kc = rmsnorm(kc, k_norm)
    scale = HEAD_DIM ** -0.5
    k_all = jnp.concatenate([kl, kc], axis=1)
    v_all = jnp.concatenate([vl, vc], axis=1)
    y = dense_latent_attention(ql, k_all, v_all, scale)
    yc = None
    if with_ctx:
        qc = rmsnorm(qc, q_norm)
        yc = attend(qc, kc, vc, scale).reshape(pc.shape[0], pc.shape[1], GROUP_W)
    return y, yc


def ssd_scan(xh, dt, a, bm, cm, h0):
    b_sz, t_len, n_h, p_dim = xh.shape
    n_st = bm.shape[-1]
    L = SSM_CHUNK
    nc = t_len // L
    xr = (xh * dt[..., None]).reshape(b_sz, nc, L, n_h, p_dim)
    br = bm.reshape(b_sz, nc, L, n_h, n_st)
    cr = cm.reshape(b_sz, nc, L, n_h, n_st)
    cs = jnp.cumsum(jnp.moveaxis((dt * a).reshape(b_sz, nc, L, n_h), 3, 1), axis=-1)
    seg = cs[..., :, None] - cs[..., None, :]
    lower = jnp.tril(jnp.ones((L, L), dtype=bool))
    decay = jnp.exp(jnp.where(lower, seg, -jnp.inf))
    scores = jnp.einsum('bclhn,bcshn->bhcls', cr, br) * decay
    y_diag = jnp.einsum('bhcls,bcshp->bclhp', scores, xr)
    w_state = jnp.exp(cs[..., -1:] - cs)
    states = jnp.einsum('bclhn,bhcl,bclhp->bchpn', br, w_state, xr)
    chunk_decay = jnp.exp(cs[..., -1])

    def step(h, inp):
        s_c, d_c = inp
        return h * d_c[..., None, None] + s_c, h

    h_final, h_in = lax.scan(step, h0, (jnp.moveaxis(states, 1, 0), jnp.moveaxis(chunk_decay, 2, 0)))
    h_in = jnp.moveaxis(h_in, 0, 1)
    y_off = jnp.einsum('bclhn,bchpn,bhcl->bclhp', cr, h_in, jnp.exp(cs))
    return (y_diag + y_off).reshape(b_sz, t_len, n_h, p_dim), h_final


def mixer_ssd(p, pc, conv_w, conv_b, dt_bias, a_log, d_skip, norm_w, with_ctx):
    rep = SSM_HEADS // SSM_GROUPS

    def prep(u):
        b_sz, t_len = u.shape[:2]
        z, xbc, dtr = jnp.split(u, [GROUP_W, GROUP_W + SSM_CONV_CH], axis=-1)
        xbc = jax.nn.silu(dwconv_centred(xbc, conv_w, conv_b))
        xs, bm, cm = jnp.split(xbc, [GROUP_W, GROUP_W + SSM_GROUPS * SSM_STATE], axis=-1)
        xs = xs.reshape(b_sz, t_len, SSM_HEADS, SSM_HEAD_DIM).astype(jnp.float32)
        bm = jnp.repeat(bm.reshape(b_sz, t_len, SSM_GROUPS, SSM_STATE), rep, axis=2).astype(jnp.float32)
        cm = jnp.repeat(cm.reshape(b_sz, t_len, SSM_GROUPS, SSM_STATE), rep, axis=2).astype(jnp.float32)
        dt = jax.nn.softplus(dtr.reshape(b_sz, t_len, 2, SSM_HEADS).astype(jnp.float32)
                             + dt_bias.astype(jnp.float32))
        return z, xs, bm, cm, dt

    flip = lambda t: jnp.flip(t, axis=1)
    a = -jnp.exp(a_log.astype(jnp.float32))
    zc, xc, bc, cc, dtc = prep(pc)
    zl, xl, bl, cl, dtl = prep(p)
    h0 = jnp.zeros((pc.shape[0], SSM_HEADS, SSM_HEAD_DIM, SSM_STATE), jnp.float32)
    yc_f, hc_f = ssd_scan(xc, dtc[:, :, 0], a[0], bc, cc, h0)
    yc_b, hc_b = ssd_scan(flip(xc), flip(dtc[:, :, 1]), a[1], flip(bc), flip(cc), h0)
    yl_f, _ = ssd_scan(xl, dtl[:, :, 0], a[0], bl, cl, hc_f)
    yl_b, _ = ssd_scan(flip(xl), flip(dtl[:, :, 1]), a[1], flip(bl), flip(cl), hc_b)
    d_f = d_skip.astype(jnp.float32)[:, None]

    def finish(y_f, y_b_rev, xs, z):
        y = y_f + flip(y_b_rev) + xs * d_f
        y = y.reshape(z.shape[0], z.shape[1], GROUP_W).astype(z.dtype)
        return rmsnorm(y * jax.nn.silu(z), norm_w)

    y = finish(yl_f, yl_b, xl, zl)
    yc = finish(yc_f, yc_b, xc, zc) if with_ctx else None
    return y, yc


def band_blocks(t, nb):
    tb = t.reshape((t.shape[0], nb, QBLOCK) + t.shape[2:])
    tp = jnp.pad(tb, ((0, 0), (1, 1), (0, 0), (0, 0), (0, 0)))
    return jnp.concatenate([tp[:, :-2], tp[:, 1:-1], tp[:, 2:]], axis=2)


def mixer_window(p, pc, sink, cos, sin, with_ctx):
    ql, kl, vl = split_gqa(p, C_HEADS, C_KV_HEADS)
    qc, kc, vc = split_gqa(pc, C_HEADS, C_KV_HEADS)
    ql = apply_axial_rope(ql, cos, sin)
    kl = apply_axial_rope(kl, cos, sin)
    b_sz, s_len = ql.shape[:2]
    nb = s_len // QBLOCK
    n_g = C_HEADS // C_KV_HEADS
    scale = HEAD_DIM ** -0.5
    sk = sink.astype(jnp.float32).reshape(C_KV_HEADS, n_g, 1, 1)
    qb = ql.reshape(b_sz, nb, QBLOCK, C_KV_HEADS, n_g, HEAD_DIM)
    kb = band_blocks(kl, nb)
    vb = band_blocks(vl, nb)
    blk = jnp.arange(nb)[:, None]
    qpos = blk * QBLOCK + jnp.arange(QBLOCK)[None, :]
    kpos = (blk - 1) * QBLOCK + jnp.arange(3 * QBLOCK)[None, :]
    kp = kpos[:, None, :]
    mask = (jnp.abs(kp - qpos[:, :, None]) <= WINDOW) & (kp >= 0) & (kp < s_len)
    s_band = jnp.einsum('bnqkgd,bnrkd->bnkgqr', qb, kb).astype(jnp.float32) * scale
    s_band = jnp.where(mask[None, :, None, None], s_band, -jnp.inf)
    s_ctx = jnp.einsum('bnqkgd,bckd->bnkgqc', qb, kc).astype(jnp.float32) * scale
    prob = softmax_sink(jnp.concatenate([s_band, s_ctx], axis=-1), sk).astype(vl.dtype)
    y = (jnp.einsum('bnkgqr,bnrkd->bnqkgd', prob[..., :3 * QBLOCK], vb)
         + jnp.einsum('bnkgqc,bckd->bnqkgd', prob[..., 3 * QBLOCK:], vc))
    y = y.reshape(b_sz, s_len, GROUP_W)
    yc = None
    if with_ctx:
        yc = attend(qc, kc, vc, scale, sk).reshape(pc.shape[0], pc.shape[1], GROUP_W)
    return y, yc


def mla_heads(p, q_norm, w_uq, kv_norm, w_ukv, cos, sin, use_rope):
    b_sz, t_len = p.shape[:2]
    cq, ckv, k_rot = jnp.split(p, [MLA_Q_RANK, MLA_Q_RANK + MLA_KV_RANK], axis=-1)
    q = (rmsnorm(cq, q_norm) @ w_uq).reshape(b_sz, t_len, MLA_HEADS, MLA_NOPE + MLA_ROPE)
    kv = (rmsnorm(ckv, kv_norm) @ w_ukv).reshape(b_sz, t_len, MLA_HEADS, MLA_NOPE + MLA_V_DIM)
    q_nope, q_rot = jnp.split(q, [MLA_NOPE], axis=-1)
    k_nope, v = jnp.split(kv, [MLA_NOPE], axis=-1)
    k_rot = k_rot[:, :, None, :]
    if use_rope:
        q_rot = apply_axial_rope(q_rot, cos, sin)
        k_rot = apply_axial_rope(k_rot, cos, sin)
    k = jnp.concatenate([k_nope, jnp.broadcast_to(k_rot, k_nope.shape[:-1] + (MLA_ROPE,))], axis=-1)
    q = jnp.concatenate([q_nope, q_rot], axis=-1)
    return q[:, :, :, None, :], k, v


def mixer_mla(p, pc, q_norm, w_uq, kv_norm, w_ukv, cos, sin, with_ctx):
    ql, kl, vl = mla_heads(p, q_norm, w_uq, kv_norm, w_ukv, cos, sin, True)
    qc, kc, vc = mla_heads(pc, q_norm, w_uq, kv_norm, w_ukv, cos, sin, False)
    scale = (MLA_NOPE + MLA_ROPE) ** -0.5
    k_all = jnp.concatenate([kl, kc], axis=1)
    v_all = jnp.concatenate([vl, vc], axis=1)
    y = dense_latent_attention(ql, k_all, v_all, scale)
    yc = attend(qc, kc, vc, scale).reshape(pc.shape[0], pc.shape[1], GROUP_W) if with_ctx else None
    return y, yc


def conv_glu(h, w_up, conv_w, conv_b, w_down):
    a, g = jnp.split(h @ w_up, 2, axis=-1)
    g = dwconv_centred(g, conv_w, conv_b)
    return (a * jax.nn.silu(g)) @ w_down


def setup_inputs(seed: int = 0) -> dict:
    key = jax.random.key(seed)
    ks = jax.random.split(key, 28)
    f32 = jnp.float32

    def nrm(i, shape, scale):
        return jax.random.normal(ks[i], shape, f32) * scale

    def gain(i, shape):
        return 1.0 + 0.05 * jax.random.normal(ks[i], shape, f32)

    L = DEPTH
    dt0 = jnp.exp(jax.random.uniform(ks[12], (L, 2, SSM_HEADS), f32, math.log(1e-3), math.log(1e-1)))
    dt_bias = dt0 + jnp.log(-jnp.expm1(-dt0))
    a_log = jnp.log(jax.random.uniform(ks[13], (L, 2, SSM_HEADS), f32, 1.0, 16.0))
    return {
        'x': nrm(0, (BATCH, SEQ, D_MODEL), 1.0),
        'c': nrm(1, (BATCH, D_MODEL), 1.0),
        'ctx': nrm(2, (BATCH, CTX_LEN, D_MODEL), 1.0),
        'c_ctx': nrm(3, (D_MODEL,), 1.0),
        'norm1_w': gain(4, (L, D_MODEL)),
        'w_mod': nrm(5, (L, D_MODEL, 6 * D_MODEL), D_MODEL ** -0.5),
        'b_mod': nrm(6, (L, 6 * D_MODEL), 0.02),
        'w_in': nrm(7, (L, D_MODEL, IN_COLS), D_MODEL ** -0.5),
        'attn_q_norm': gain(8, (L, HEAD_DIM)),
        'attn_k_norm': gain(9, (L, HEAD_DIM)),
        'ssm_conv_w': nrm(10, (L, SSM_CONV_CH, SSM_CONV), SSM_CONV ** -0.5),
        'ssm_conv_b': nrm(11, (L, SSM_CONV_CH), 0.02),
        'ssm_dt_bias': dt_bias,
        'ssm_a_log': a_log,
        'ssm_d': gain(14, (L, SSM_HEADS)),
        'ssm_norm_w': gain(15, (L, GROUP_W)),
        'win_sink': nrm(16, (L, C_HEADS), 0.5),
        'mla_q_norm': gain(17, (L, MLA_Q_RANK)),
        'mla_w_uq': nrm(18, (L, MLA_Q_RANK, MLA_HEADS * (MLA_NOPE + MLA_ROPE)), MLA_Q_RANK ** -0.5),
        'mla_kv_norm': gain(19, (L, MLA_KV_RANK)),
        'mla_w_ukv': nrm(20, (L, MLA_KV_RANK, MLA_HEADS * (MLA_NOPE + MLA_V_DIM)), MLA_KV_RANK ** -0.5),
        'w_out': nrm(21, (L, MIX_W, D_MODEL), MIX_W ** -0.5),
        'norm2_w': gain(22, (L, D_MODEL)),
        'ffn_w_up': nrm(23, (L, D_MODEL, 2 * D_FF), D_MODEL ** -0.5),
        'ffn_conv_w': nrm(24, (L, D_FF, FFN_CONV), FFN_CONV ** -0.5),
        'ffn_conv_b': nrm(25, (L, D_FF), 0.02),
        'ffn_w_down': nrm(26, (L, D_FF, D_MODEL), D_FF ** -0.5),
        'final_norm_w': gain(27, (D_MODEL,)),
    }


def reference(x, c, ctx, c_ctx, norm1_w, w_mod, b_mod, w_in, attn_q_norm, attn_k_norm,
              ssm_conv_w, ssm_conv_b, ssm_dt_bias, ssm_a_log, ssm_d, ssm_norm_w, win_sink,
              mla_q_norm, mla_w_uq, mla_kv_norm, mla_w_ukv, w_out, norm2_w,
              ffn_w_up, ffn_conv_w, ffn_conv_b, ffn_w_down, final_norm_w):
    s_len = x.shape[1]
    cos_a, sin_a = axial_rope_tables(s_len, HEAD_DIM)
    cos_m, sin_m = axial_rope_tables(s_len, MLA_ROPE)
    c_act = jax.nn.silu(c)
    cc_act = jax.nn.silu(c_ctx)
    xc = ctx
    for l in range(DEPTH):
        with_ctx = l < DEPTH - 1
        mod = c_act @ w_mod[l] + b_mod[l]
        mod_c = cc_act @ w_mod[l] + b_mod[l]
        sh1, sc1, g1, sh2, sc2, g2 = jnp.split(mod[:, None, :], 6, axis=-1)
        csh1, csc1, cg1, csh2, csc2, cg2 = jnp.split(mod_c, 6, axis=-1)

        h = rmsnorm(x, norm1_w[l]) * (1 + sc1) + sh1
        hc = rmsnorm(xc, norm1_w[l]) * (1 + csc1) + csh1
        la, lb, lw, lm = jnp.split(h @ w_in[l], IN_SPLITS, axis=-1)
        ca, cb, cw, cm = jnp.split(hc @ w_in[l], IN_SPLITS, axis=-1)
        ya, yca = mixer_gqa(la, ca, attn_q_norm[l], attn_k_norm[l], cos_a, sin_a, with_ctx)
        yb, ycb = mixer_ssd(lb, cb, ssm_conv_w[l], ssm_conv_b[l], ssm_dt_bias[l], ssm_a_log[l],
                            ssm_d[l], ssm_norm_w[l], with_ctx)
        yw, ycw = mixer_window(lw, cw, win_sink[l], cos_a, sin_a, with_ctx)
        ym, ycm = mixer_mla(lm, cm, mla_q_norm[l], mla_w_uq[l], mla_kv_norm[l], mla_w_ukv[l],
                            cos_m, sin_m, with_ctx)
        x = x + g1 * (jnp.concatenate([ya, yb, yw, ym], axis=-1) @ w_out[l])
        x = x + g2 * conv_glu(rmsnorm(x, norm2_w[l]) * (1 + sc2) + sh2,
                              ffn_w_up[l], ffn_conv_w[l], ffn_conv_b[l], ffn_w_down[l])
        if with_ctx:
            xc = xc + cg1 * (jnp.concatenate([yca, ycb, ycw, ycm], axis=-1) @ w_out[l])
            xc = xc + cg2 * conv_glu(rmsnorm(xc, norm2_w[l]) * (1 + csc2) + csh2,
                                     ffn_w_up[l], ffn_conv_w[l], ffn_conv_b[l], ffn_w_down[l])
    return rmsnorm(x, final_norm_w)
```

```python
import numpy as np
import concourse.bass as bass
import concourse.mybir as mybir
from concourse.bass_utils import run_bass_kernel_spmd

F32 = mybir.dt.float32
BF16 = mybir.dt.bfloat16
AF = mybir.ActivationFunctionType
ALU = mybir.AluOpType
AX = mybir.AxisListType

GRAN = 64
EPOCH = 30000
_ESZ = {F32: 4, BF16: 2}


def _esize(dt):
    return _ESZ.get(dt, 4)


class DmaSem:
    def __init__(self, handle):
        self.h = handle
        self.count = 0
        self.last_waited = 0


class Op:
    __slots__ = ("eng", "fn", "deps", "pdeps", "is_dma", "sem", "semval", "signal",
                 "sigdim", "sigval", "waits", "clock", "tag")

    def __init__(self, eng, fn):
        self.eng = eng
        self.fn = fn
        self.deps = {}
        self.pdeps = []
        self.tag = None
        self.is_dma = False
        self.sem = None
        self.semval = 0
        self.signal = False
        self.sigdim = None
        self.sigval = 0
        self.waits = None
        self.clock = None


class Prog:
    ENGS = ("pe", "act", "dve", "pool", "sp")

    def __init__(self, nc):
        self.nc = nc
        self.allops = []
        self.lastw = {}
        self.readers = {}
        self.pseudo = set()
        self.nops = 0
        self.cur_tag = None

    @staticmethod
    def keys_of(ap):
        t = ap.tensor
        pat = ap.ap
        es = _esize(ap.dtype)
        rowlen = pat[0][0]
        off = ap.offset
        start = off % rowlen if rowlen > 0 else off
        ext = 0
        for st, cnt in pat[1:]:
            ext += (cnt - 1) * abs(st)
        b0 = (start * es) // GRAN
        b1 = ((start + ext + 1) * es - 1) // GRAN
        name = t.name
        if name.startswith("ps"):
            return [(name, 0)]
        return [(name, b) for b in range(b0, b1 + 1)]

    def _collect(self, items):
        ks = []
        for it in items:
            if it is None:
                continue
            if isinstance(it, tuple):
                ks.append(it)
            else:
                ks.extend(self.keys_of(it))
        return ks

    def _record(self, o, reads, writes):
        rk = self._collect(reads)
        wk = self._collect(writes)
        deps = o.deps
        prk = [k for k in rk if k[0].startswith("ps") and k not in wk]
        for k in rk:
            w = self.lastw.get(k)
            if w is not None:
                deps[w] = "rar" if (k in self.pseudo and deps.get(w) != "raw") else "raw"
        for k in wk:
            w = self.lastw.get(k)
            if w is not None and w not in deps:
                deps[w] = "waw"
            for r in self.readers.get(k, ()):
                if r not in deps:
                    deps[r] = "war"
        for k in rk:
            self.readers.setdefault(k, []).append(o)
        for k in wk:
            self.lastw[k] = o
            self.readers[k] = []
            self.pseudo.discard(k)
        for k in prk:
            self.lastw[k] = o
            self.readers[k] = []
            self.pseudo.add(k)
        keep = {}
        for d, kind in deps.items():
            if d is o:
                continue
            if d.eng == o.eng and not d.is_dma and not o.is_dma:
                if o.eng == "pe":
                    continue
                if kind == "rar":
                    continue
            keep[d] = kind
        o.deps = keep
        o.tag = self.cur_tag
        for d in keep:
            if d.is_dma:
                sm = d.sem
                o.pdeps.append((sm, sm.count))
                if sm.count > sm.last_waited:
                    sm.last_waited = sm.count
        self.allops.append(o)

    def op(self, eng, fn, reads=(), writes=()):
        o = Op(eng, fn)
        self._record(o, reads, writes)
        return o

    def dma(self, eng, out, in_, sem, reads=None, writes=None, **kw):
        o = Op(eng, None)
        o.is_dma = True
        o.sem = sem
        if sem.last_waited > 0:
            o.pdeps.append((sem, sem.last_waited))
        o.fn = lambda e: e.dma_start(out=out, in_=in_, **kw)
        self._record(o, reads if reads is not None else [in_],
                     writes if writes is not None else [out])
        sem.count += 16
        o.semval = sem.count
        return o

    def finalize(self, sem_alloc):
        for o in self.allops:
            for d in o.deps:
                if not d.is_dma:
                    d.signal = True
        cnt = {e: 0 for e in self.ENGS}
        ep = {e: 0 for e in self.ENGS}
        self.engsem = {}
        for o in self.allops:
            if o.is_dma:
                o.sigdim = o.sem
                o.sigval = o.semval
                continue
            if o.signal:
                e = o.eng
                if cnt[e] >= EPOCH:
                    cnt[e] = 0
                    ep[e] += 1
                cnt[e] += 1
                dim = (e, ep[e])
                if dim not in self.engsem:
                    self.engsem[dim] = sem_alloc()
                o.sigdim = dim
                o.sigval = cnt[e]
        known = {e: {} for e in self.ENGS}
        nwaits = 0
        for o in self.allops:
            K = known[o.eng]
            need = {}
            for d in o.deps:
                if d.is_dma:
                    continue
                dim, val = d.sigdim, d.sigval
                if K.get(dim, 0) < val and need.get(dim, 0) < val:
                    need[dim] = val
            for sem, val in o.pdeps:
                if K.get(sem, 0) < val and need.get(sem, 0) < val:
                    need[sem] = val
            for d in o.deps:
                ck = d.clock
                for dim, val in ck.items():
                    if K.get(dim, 0) < val:
                        K[dim] = val
            for dim, val in need.items():
                if K.get(dim, 0) < val:
                    K[dim] = val
            o.waits = list(need.items())
            nwaits += len(o.waits)
            ck = dict(K)
            if o.is_dma or o.signal:
                ck[o.sigdim] = max(ck.get(o.sigdim, 0), o.sigval)
            o.clock = ck
        self.nwaits = nwaits

    def emit(self, block):
        per = {e: [] for e in self.ENGS}
        for o in self.allops:
            per[o.eng].append(o)
        engsem = self.engsem

        def run(eng_obj, ops):
            for o in ops:
                for dim, val in o.waits:
                    h = dim.h if isinstance(dim, DmaSem) else engsem[dim]
                    eng_obj.wait_ge(h, val)
                ins = o.fn(eng_obj)
                if o.is_dma:
                    ins.then_inc(o.sem.h, 16)
                elif o.signal:
                    ins.then_inc(engsem[o.sigdim], 1)

        @block.tensor
        def _(e):
            run(e, per["pe"])

        @block.scalar
        def _(e):
            run(e, per["act"])

        @block.vector
        def _(e):
            run(e, per["dve"])

        @block.gpsimd
        def _(e):
            run(e, per["pool"])

        @block.sync
        def _(e):
            run(e, per["sp"])
from contextlib import ExitStack

D = 1024
T = 2304
NLAT = 2048
NTILE = 18
TBS = [(0, 512), (512, 512), (1024, 512), (1536, 512), (2048, 256)]
DFF = 2816
NFC = 22
EPS = 1e-6
import os as _os
ARENA = int(_os.environ.get("KARENA", "53000"))
BSTOP = int(_os.environ.get("BSTOP", "0"))
ALIGN = int(_os.environ.get("KALIGN", "16"))
MARK = int(_os.environ.get("KMARK", "0"))
GW = 2310
NEG = -30000.0


def host_consts():
    f = np.float32
    idn = np.eye(128, dtype=f)
    k = np.arange(128)
    tri_f = (k[:, None] <= k[None, :]).astype(f)
    tri_b = (k[:, None] >= k[None, :]).astype(f)
    nm_f = np.where(k[None, :] >= k[:, None], 0.0, NEG).astype(f)
    nm_b = np.where(k[:, None] >= k[None, :], 0.0, NEG).astype(f)
    cF = np.concatenate([idn, tri_f, tri_b, nm_f, nm_b], axis=1)

    def perm(base, half):
        Pm = np.zeros((128, 128), f)
        for h0 in (base, base + half):
            q = half // 2
            for jj in range(q):
                Pm[h0 + jj + q, h0 + jj] = -1.0
                Pm[h0 + jj, h0 + q + jj] = 1.0
        return Pm
    PA = perm(0, 32) + perm(64, 32)
    PD = perm(64, 16)
    BD = np.zeros((128, 128), f)
    BD[0:64, 0:64] = 1.0
    BD[64:128, 64:128] = 1.0
    a = np.arange(128)[:, None]
    bq = np.arange(128)[None, :]
    masks = []
    for r in range(-1, 5):
        m = np.zeros((128, 512), f)
        for c in range(4):
            if r == c:
                m[:, c * 128:(c + 1) * 128] = 1.0
            elif r - c == -1:
                m[:, c * 128:(c + 1) * 128] = (bq <= a)
            elif r - c == 1:
                m[:, c * 128:(c + 1) * 128] = (a <= bq)
        masks.append(m)
    cB = np.concatenate([PA, PD, BD] + masks, axis=1)

    def tables(rot_dim):
        rows = NLAT // 64
        row = np.repeat(np.arange(rows, dtype=f), 64)
        col = np.tile(np.arange(64, dtype=f), rows)
        half = rot_dim // 2
        inv = np.power(f(10000.0), -np.arange(0, half, 2, dtype=f) / f(half)).astype(f)
        ar = (row[:, None] * inv[None, :]).astype(f)
        ac = (col[:, None] * inv[None, :]).astype(f)
        ang = np.concatenate([ar, ar, ac, ac], axis=-1)
        return np.cos(ang).astype(f), np.sin(ang).astype(f)
    cA, sA = tables(64)
    ropeA = np.zeros((2, 128, T), f)
    ropeA[0] = 1.0
    ropeA[0, 0:64, 0:NLAT] = cA.T
    ropeA[0, 64:128, 0:NLAT] = cA.T
    ropeA[1, 0:64, 0:NLAT] = sA.T
    ropeA[1, 64:128, 0:NLAT] = sA.T
    cD, sD = tables(32)
    ropeD = np.zeros((2, 128, T), f)
    ropeD[0] = 1.0
    ropeD[0, 64:96, 0:NLAT] = cD.T
    ropeD[1, 64:96, 0:NLAT] = sD.T
    return dict(cF=cF, cB=cB, ropeA=ropeA, ropeD=ropeD)


WSPEC = [
    ("x", [2, 2048, 1024]), ("c", [2, 1024]), ("ctx", [2, 256, 1024]), ("c_ctx", [1024]),
    ("norm1_w", [2, 1024]), ("w_mod", [2, 1024, 6144]), ("b_mod", [2, 6144]),
    ("w_in", [2, 1024, 2408]), ("attn_q_norm", [2, 64]), ("attn_k_norm", [2, 64]),
    ("ssm_conv_w", [2, 768, 3]), ("ssm_conv_b", [2, 768]), ("ssm_dt_bias", [2, 2, 4]),
    ("ssm_a_log", [2, 2, 4]), ("ssm_d", [2, 4]), ("ssm_norm_w", [2, 256]), ("win_sink", [2, 4]),
    ("mla_q_norm", [2, 192]), ("mla_w_uq", [2, 192, 384]), ("mla_kv_norm", [2, 128]),
    ("mla_w_ukv", [2, 128, 512]), ("w_out", [2, 1024, 1024]), ("norm2_w", [2, 1024]),
    ("ffn_w_up", [2, 1024, 5632]), ("ffn_conv_w", [2, 2816, 3]), ("ffn_conv_b", [2, 2816]),
    ("ffn_w_down", [2, 2816, 1024]), ("final_norm_w", [1024]),
    ("cF", [128, 640]), ("cB", [128, 384 + 6 * 512]), ("ropeA", [2, 128, T]), ("ropeD", [2, 128, T]),
]


class KB:
    def __init__(self, nc, es, dumps=()):
        self.nc = nc
        self.es = es
        self.P = Prog(nc)
        self.dumps = set(dumps)
        self.I = {}
        for name, shp in WSPEC:
            self.I[name] = nc.dram_tensor(name, shp, F32, kind="ExternalInput").ap()
        self.out = nc.dram_tensor("out", [2, 2048, 1024], F32, kind="ExternalOutput").ap()
        self.xt = nc.dram_tensor("xt", [2, 8, 128, T], F32, kind=("ExternalOutput" if "xt" in self.dumps else "Internal")).ap()
        self.usc = nc.dram_tensor("usc", [NFC, 128, T], BF16, kind="Internal").ap()
        self.arena = es.enter_context(nc.sbuf_tensor("arena", [128, ARENA], F32))
        self.pss = [es.enter_context(nc.psum_tensor("ps%d" % i, [128, 512], F32)) for i in range(8)]
        self.top = 0
        self.nsem = 0
        self.rotc = {}
        self.dsems = {}
        self.dump_aps = {}
        self.pe_slices = {}

    def sem(self):
        s = self.es.enter_context(self.nc.semaphore("s%d" % self.nsem))
        self.nsem += 1
        return s

    def dsem(self, name):
        if name not in self.dsems:
            self.dsems[name] = DmaSem(self.sem())
        return self.dsems[name]

    def alloc(self, n, dt=F32):
        es_ = _esize(dt)
        n4 = (n * es_ + 3) // 4
        off = (self.top + ALIGN - 1) // ALIGN * ALIGN
        self.top = off + n4
        assert self.top <= ARENA, ("arena overflow", self.top)
        a = self.arena[:, off:off + n4]
        if dt == F32:
            return a
        return a.bitcast(dt)[:, 0:n]

    def mark(self):
        return self.top

    def release(self, m):
        self.top = m

    def rot(self, name, items):
        i = self.rotc.get(name, 0)
        self.rotc[name] = i + 1
        return items[i % len(items)]

    BANKS_DEFAULT = {"pj": (0, 1), "s": (2, 3), "acc": (4, 5), "aux": (6,), "aux2": (7,)}
    BANKS_PROJ = {"pj": (0, 1, 2, 3, 4, 5), "s": (2, 3), "acc": (4, 5), "aux": (6,), "aux2": (7,)}
    BANKS_ATT = {"pj": (0, 1), "s": (0, 1, 2, 3, 6, 7), "acc": (4, 5), "aux": (6,), "aux2": (7,)}

    def bank(self, grp):
        banks = getattr(self, "bankmap", self.BANKS_DEFAULT)[grp]
        return self.pss[self.rot("bank_" + grp, banks)]

    def mm(self, out, lhsT, rhs, start=True, stop=True, tp=None):
        kw = dict(start=start, stop=stop)
        if tp is not None:
            kw["tile_position"] = tp
        self.P.op("pe", lambda e: e.matmul(out, lhsT, rhs, **kw), [lhsT, rhs], [out])
        t_ = self.P.cur_tag
        self.pe_slices[t_] = self.pe_slices.get(t_, 0) + (2 if lhsT.dtype == F32 else 1)

    def tr(self, out, in_, ident):
        self.P.op("pe", lambda e: e.transpose(out, in_, ident), [in_, ident], [out])
        t_ = self.P.cur_tag
        self.pe_slices[t_] = self.pe_slices.get(t_, 0) + 1

    def act(self, out, in_, func, bias=None, scale=None, accum=None):
        kw = {}
        rd = [in_]
        wr = [out]
        if bias is not None:
            kw["bias"] = bias
            if not isinstance(bias, (int, float)):
                rd.append(bias)
        if scale is not None:
            kw["scale"] = scale
            if not isinstance(scale, (int, float)):
                rd.append(scale)
        if accum is not None:
            kw["accum_out"] = accum
            wr.append(accum)
        self.P.op("act", lambda e: e.activation(out, in_, func, **kw), rd, wr)

    def tt(self, eng, out, in0, in1, op):
        self.P.op(eng, lambda e: e.tensor_tensor(out, in0, in1, op), [in0, in1], [out])

    def ts(self, eng, out, in0, s1, s2, op0, op1=None):
        rd = [in0]
        for s in (s1, s2):
            if s is not None and not isinstance(s, (int, float)):
                rd.append(s)
        if op1 is None:
            self.P.op(eng, lambda e: e.tensor_scalar(out, in0, s1, None, op0), rd, [out])
        else:
            self.P.op(eng, lambda e: e.tensor_scalar(out, in0, s1, s2, op0, op1), rd, [out])

    def stt(self, out, in0, scalar, in1, op0, op1):
        rd = [in0, in1]
        if not isinstance(scalar, (int, float)):
            rd.append(scalar)
        self.P.op("dve", lambda e: e.scalar_tensor_tensor(out, in0, scalar, in1, op0, op1), rd, [out])

    def cp(self, eng, out, in_):
        if eng == "act":
            self.act(out, in_, AF.Copy)
        else:
            self.P.op(eng, lambda e: e.tensor_copy(out, in_), [in_], [out])

    def ms(self, eng, out, val):
        self.P.op(eng, lambda e: e.memset(out, val), [], [out])

    def recip(self, out, in_):
        self.P.op("dve", lambda e: e.reciprocal(out, in_), [in_], [out])

    def ld(self, out, in_, sem, rk=()):
        self.P.dma("sp", out, in_, self.dsem(sem), reads=list(rk), writes=[out])

    def st(self, out, in_, sem, wk=()):
        self.P.dma("sp", out, in_, self.dsem(sem), reads=[in_], writes=list(wk))

    def rstd(self, out, ss, inv_n):
        self.act(out, ss, AF.Ln, bias=EPS, scale=inv_n)
        self.act(out, out, AF.Exp, scale=-0.5)

    def dump(self, name, src, n):
        if name not in self.dumps:
            return
        dst = self.nc.dram_tensor("dbg_" + name, [128, n], F32, kind="ExternalOutput").ap()
        pn = src.shape[0]
        p0 = src.base_partition()
        m = self.mark()
        tmp = self.alloc(512)
        for c0 in range(0, n, 512):
            w = min(512, n - c0)
            self.cp("dve", tmp[p0:p0 + pn, 0:w], src[:, c0:c0 + w])
            self.st(dst[p0:p0 + pn, c0:c0 + w], tmp[p0:p0 + pn, 0:w], "dbg", wk=[("dbg", name, c0)])
        self.release(m)
        self.dump_aps[name] = 1

    def load_w(self, dst, src, K, n, rows_last=128):
        per = max(1, self.wst_cap // n)
        k0 = 0
        while k0 < K:
            kg = min(per, K - k0)
            slot = self.rot("wst", (0, 1))
            stg = self.wst[slot]
            full = kg if not (k0 + kg == K and rows_last != 128) else kg - 1
            v = stg[:, 0:kg * n].rearrange("p (k n) -> p k n", k=kg)
            if full > 0:
                self.ld(v[:, 0:full, :], src[k0 * 128:(k0 + full) * 128, :].rearrange("(k p) n -> p k n", p=128),
                        "wst%d" % slot)
                self.cp("dve", dst[:, k0:k0 + full, :], v[:, 0:full, :])
            if full < kg:
                r0 = (k0 + full) * 128
                self.ld(v[0:rows_last, full, :], src[r0:r0 + rows_last, :], "wst%d" % slot)
                self.cp("dve", dst[0:rows_last, k0 + full, :], v[0:rows_last, full, :])
            k0 += kg

    def setup_consts(self):
        I = self.I
        cf = self.alloc(640)
        self.ld(cf, I["cF"], "cst")
        self.identF = cf[:, 0:128]
        self.triF = cf[:, 128:256]
        self.triB = cf[:, 256:384]
        self.nmF = cf[:, 384:512]
        self.nmB = cf[:, 512:640]
        self.onesF = self.alloc(128)
        self.ms("pool", self.onesF, 1.0)
        self.onesB = self.alloc(128, BF16)
        self.ms("pool", self.onesB, 1.0)
        self.identB = self.alloc(128, BF16)
        self.cp("pool", self.identB, self.identF)
        self.nmFb = self.alloc(128, BF16)
        self.nmBb = self.alloc(128, BF16)
        self.cp("pool", self.nmFb, self.nmF)
        self.cp("pool", self.nmBb, self.nmB)
        self.nmb4 = [self.alloc(512, BF16), self.alloc(512, BF16)]
        for d_, src_ in enumerate((self.nmF, self.nmB)):
            for h_ in range(4):
                self.cp("pool", self.nmb4[d_][:, h_ * 128:(h_ + 1) * 128], src_)
        self.PA = self.alloc(128, BF16)
        self.PD = self.alloc(128, BF16)
        self.BD = self.alloc(128, BF16)
        self.wmask = self.alloc(6 * 512, BF16)
        self.LV = self.alloc(128)
        self.scw = self.alloc(18)
        self.fcw = self.alloc(66)
        self.rows = self.alloc(24)
        self.arow = self.alloc(8)
        self.sinkexp = self.alloc(4)
        self.Drow = self.alloc(256)
        self.SC = self.alloc(24)
        self.modT = self.alloc(144)
        self.A1 = self.alloc(24)
        self.A2 = self.alloc(24)
        self.wst = None
        self.wst_cap = 4096
        self.mk = self.alloc(2)
        self.ms("pool", self.mk, 0.0)
        m = self.mark()
        stg = self.alloc(384 + 3072)
        self.ld(stg, I["cB"], "cst")
        self.cp("pool", self.PA, stg[:, 0:128])
        self.cp("pool", self.PD, stg[:, 128:256])
        self.cp("pool", self.BD, stg[:, 256:384])
        self.cp("pool", self.wmask, stg[:, 384:384 + 3072])
        cst = self.alloc(128)
        self.ms("dve", cst, 0.0)
        self.ld(cst[0:16, :], I["c"].rearrange("b (k p) -> (b k) p", p=128), "cst")
        self.ld(cst[16:24, :], I["c_ctx"].rearrange("(k p) -> k p", p=128), "cst")
        ps = self.bank("aux")
        self.tr(ps[:, 0:128], cst, self.identF)
        self.act(self.SC, ps[:, 0:24], AF.Silu)
        self.release(m)
        self.persist_top = self.top

    def layer_setup(self, l):
        I = self.I
        m = self.mark()
        stg = self.alloc(128)
        self.ms("dve", stg, 0.0)

        def rows(r0, src2d):
            n = src2d.shape[0]
            self.ld(stg[r0:r0 + n, 0:src2d.shape[1]], src2d, "cst")
        rows(0, I["norm1_w"][l].rearrange("(k p) -> k p", p=128))
        rows(8, I["norm2_w"][l].rearrange("(k p) -> k p", p=128))
        rows(16, I["b_mod"][l].rearrange("(k p) -> k p", p=128))
        rows(64, I["ssm_conv_b"][l].rearrange("(k p) -> k p", p=128))
        rows(70, I["ssm_norm_w"][l].rearrange("(k p) -> k p", p=128))
        rows(72, I["ffn_conv_b"][l].rearrange("(k p) -> k p", p=128))
        rows(94, I["mla_kv_norm"][l].rearrange("(k p) -> k p", p=128))
        aq = I["attn_q_norm"][l].rearrange("(k p) -> k p", p=64)
        ak = I["attn_k_norm"][l].rearrange("(k p) -> k p", p=64)
        self.ld(stg[95:96, 0:64], aq, "cst")
        self.ld(stg[95:96, 64:128], aq, "cst")
        self.ld(stg[96:97, 0:64], ak, "cst")
        self.ld(stg[96:97, 64:128], ak, "cst")
        mq = I["mla_q_norm"][l]
        self.ld(stg[97:98, :], mq[0:128].rearrange("(k p) -> k p", p=128), "cst")
        self.ld(stg[98:99, 0:64], mq[128:192].rearrange("(k p) -> k p", p=64), "cst")
        rows(99, I["final_norm_w"].rearrange("(k p) -> k p", p=128))
        ps = self.bank("aux")
        self.tr(ps[:, 0:128], stg, self.identF)
        self.cp("dve", self.LV, ps[:, 0:128])
        LV = self.LV
        self.n1w = LV[:, 0:8]
        self.n2w = LV[:, 8:16]
        self.bmodT = LV[:, 16:64]
        self.scb = LV[:, 64:70]
        self.snw = LV[:, 70:72]
        self.fcb = LV[:, 72:94]
        self.kvn = LV[:, 94:95]
        self.aqn = LV[:, 95:96]
        self.akn = LV[:, 96:97]
        self.mqn0 = LV[:, 97:98]
        self.mqn1 = LV[:, 98:99]
        self.fnw = LV[:, 99:107]
        self.ld(self.scw.rearrange("p (j k) -> p j k", k=3),
                I["ssm_conv_w"][l].rearrange("(j p) k -> p j k", p=128), "cst")
        self.ld(self.fcw.rearrange("p (j k) -> p j k", k=3),
                I["ffn_conv_w"][l].rearrange("(j p) k -> p j k", p=128), "cst")

        def bro(dst, src1d, n):
            self.ld(dst, src1d.rearrange("(o n) -> o n", o=1).to_broadcast([128, n]), "cst")
        bro(self.rows[:, 0:8], I["ssm_dt_bias"][l].rearrange("d h -> (d h)"), 8)
        bro(self.rows[:, 8:16], I["ssm_a_log"][l].rearrange("d h -> (d h)"), 8)
        bro(self.rows[:, 16:20], I["ssm_d"][l], 4)
        bro(self.rows[:, 20:24], I["win_sink"][l], 4)
        self.dtb = self.rows[:, 0:8]
        self.act(self.arow, self.rows[:, 8:16], AF.Exp)
        self.ts("dve", self.arow, self.arow, -1.0, None, ALU.mult)
        self.act(self.sinkexp, self.rows[:, 20:24], AF.Exp)
        for h in range(4):
            self.cp("dve", self.Drow[:, h * 64:(h + 1) * 64], self.rows[:, 16 + h:17 + h].to_broadcast([128, 64]))
        wm = [self.alloc(8192), self.alloc(8192)]
        wmb = [self.alloc(8192, BF16), self.alloc(8192, BF16)]
        SCb = self.alloc(24, BF16)
        self.cp("dve", SCb, self.SC)
        SCv = SCb.rearrange("p (r k) -> p k r", r=3)
        modv = self.modT.rearrange("p (j d r) -> p j d r", j=6, d=8)
        for j6 in range(6):
            w = wm[j6 % 2]
            wv = w.rearrange("p (k n) -> p k n", k=8)
            wb = wmb[j6 % 2].rearrange("p (k n) -> p k n", k=8)
            self.ld(wv, I["w_mod"][l][:, j6 * 1024:(j6 + 1) * 1024].rearrange("(k p) n -> p k n", p=128),
                    "wm%d" % (j6 % 2))
            self.cp("dve", wb[:, 0:3, :], wv[:, 0:3, :])
            self.cp("act", wb[:, 3:6, :], wv[:, 3:6, :])
            self.cp("pool", wb[:, 6:8, :], wv[:, 6:8, :])
            pm = self.bank("pj")
            for dc in range(8):
                for k_ in range(8):
                    self.mm(pm[:, dc * 3:dc * 3 + 3], wb[:, k_, dc * 128:(dc + 1) * 128], SCv[:, k_, :],
                            start=(k_ == 0), stop=(k_ == 7))
            self.tt("dve", modv[:, j6], pm[:, 0:24].rearrange("p (d r) -> p d r", r=3),
                    self.bmodT[:, j6 * 8:(j6 + 1) * 8].unsqueeze(2).to_broadcast([128, 8, 3]), ALU.add)
        A1v = self.A1.rearrange("p (r k) -> p r k", r=3)
        A2v = self.A2.rearrange("p (r k) -> p r k", r=3)
        for r in range(3):
            self.stt(A1v[:, r, :], modv[:, 1, :, r], 1.0, self.n1w, ALU.add, ALU.mult)
            self.stt(A2v[:, r, :], modv[:, 4, :, r], 1.0, self.n2w, ALU.add, ALU.mult)
        self.modv = modv
        self.A1v = A1v
        self.A2v = A2v
        self.release(m)

    def tb_keys(self, b, ti):
        t0, n = TBS[ti]
        return [("xt", b, i) for i in range(t0 // 128, (t0 + n) // 128)]

    def xt_view(self, b, t0, n):
        return self.xt[b][:, :, t0:t0 + n].rearrange("k p t -> p k t")

    def phase_x0(self):
        I = self.I
        m = self.mark()
        xin = [self.alloc(1024), self.alloc(1024)]
        xo = [self.alloc(1024), self.alloc(1024)]
        def src_of(g):
            b, i = divmod(g, NTILE)
            return I["x"][b, i * 128:(i + 1) * 128, :] if i < 16 else I["ctx"][b, (i - 16) * 128:(i - 15) * 128, :]
        self.ld(xin[0], src_of(0), "xin0")
        for b in range(2):
            for i in range(NTILE):
                g = b * NTILE + i
                s = g % 2
                if g + 1 < 2 * NTILE:
                    self.ld(xin[1 - s], src_of(g + 1), "xin%d" % (1 - s))
                p0 = self.bank("pj")
                p1 = self.bank("pj")
                for k in range(8):
                    pp = p0 if k < 4 else p1
                    self.tr(pp[:, (k % 4) * 128:(k % 4 + 1) * 128], xin[s][:, k * 128:(k + 1) * 128], self.identF)
                self.cp("act", xo[s][:, 0:512], p0[:, :])
                self.cp("dve", xo[s][:, 512:1024], p1[:, :])
                self.st(self.xt_view(b, i * 128, 128), xo[s].rearrange("p (k t) -> p k t", k=8),
                        "xo%d" % s, wk=[("xt", b, i)])
        self.release(m)

    def norm_mod_block(self, xb, n, t0, Av, shift_j, r, tmp):
        ss = self.bank("aux")
        for k in range(8):
            sq = self.rot("sqb", tmp["sq"])
            self.act(sq[:, 0:n], xb[:, k, 0:n], AF.Square)
            self.mm(ss[:, 0:n], self.onesB, sq[:, 0:n], start=(k == 0), stop=(k == 7))
        rs = self.rot("rsb", tmp["rs"])
        self.rstd(rs[:, 0:n], ss[:, 0:n], 1.0 / D)
        for k in range(8):
            t = self.rot("nmt", tmp["t"])
            self.stt(t[:, 0:n], xb[:, k, 0:n], Av[:, r, k:k + 1], rs[:, 0:n], ALU.mult, ALU.mult)
            self.act(self.hT[:, k, t0:t0 + n], t[:, 0:n], AF.Identity, bias=self.modv[:, shift_j, k, r:r + 1])

    def norm_tmp(self):
        return dict(sq=[self.alloc(512, BF16) for _ in range(2)], rs=[self.alloc(512) for _ in range(2)],
                    t=[self.alloc(512) for _ in range(2)])

    def p1(self, l, b):
        m = self.mark()
        xbs = [self.alloc(4096), self.alloc(4096)]
        tmp = self.norm_tmp()
        for ti, (t0, n) in enumerate(TBS):
            s = ti % 2
            xb = xbs[s].rearrange("p (k t) -> p k t", k=8)
            self.ld(xb[:, :, 0:n], self.xt_view(b, t0, n), "xb%d" % s, rk=self.tb_keys(b, ti))
            r = b if ti < 4 else 2
            self.norm_mod_block(xb, n, t0, self.A1v, 0, r, tmp)
        self.release(m)


    def proj_fm(self, out, w, c0, M, t0, n, tp=None):
        for k in range(8):
            self.mm(out, w[:, k, c0:c0 + M], self.hT[:, k, t0:t0 + n], start=(k == 0), stop=(k == 7), tp=tp)

    def proj_tm(self, out, w, c0, N, i):
        for k in range(8):
            self.mm(out, self.hT[:, k, i * 128:(i + 1) * 128], w[:, k, c0:c0 + N], start=(k == 0), stop=(k == 7))

    def rope_tmp(self):
        return dict(raw=[self.alloc(512, BF16) for _ in range(3)], sq=[self.alloc(512, BF16) for _ in range(3)],
                    rs=[self.alloc(512) for _ in range(3)], qn=[self.alloc(512, BF16) for _ in range(3)],
                    t1=[self.alloc(512) for _ in range(3)], t2=[self.alloc(512) for _ in range(3)])

    def norm_rope_gen(self, src, p0, p1, n, t0, dst, tmp, perm, cos, sin, gain=None, ones=None, nfeat=64):
        raw = self.rot("rp_raw", tmp["raw"])[p0:p1, 0:n]
        self.act(raw, src, AF.Copy)
        if gain is not None:
            sq = self.rot("rp_sq", tmp["sq"])[p0:p1, 0:n]
            self.act(sq, src, AF.Square)
        yield
        if gain is not None:
            ssp = self.bank("aux")[p0:p1, 0:n]
            self.mm(ssp, ones[p0:p1, p0:p1], sq)
            rs = self.rot("rp_rs", tmp["rs"])[p0:p1, 0:n]
            self.rstd(rs, ssp, 1.0 / nfeat)
            qn = self.rot("rp_qn", tmp["qn"])[p0:p1, 0:n]
            self.stt(qn, raw, gain[p0:p1, :], rs, ALU.mult, ALU.mult)
        else:
            qn = raw
        yield
        rotp = self.bank("aux2")[p0:p1, 0:n]
        self.mm(rotp, perm[p0:p1, p0:p1], qn)
        t1 = self.rot("rp_t1", tmp["t1"])[p0:p1, 0:n]
        t2 = self.rot("rp_t2", tmp["t2"])[p0:p1, 0:n]
        self.tt("pool", t1, qn, cos[p0:p1, t0:t0 + n], ALU.mult)
        self.tt("dve", t2, rotp, sin[p0:p1, t0:t0 + n], ALU.mult)
        if isinstance(dst, list):
            for (a0, a1, d) in dst:
                self.tt("pool", d, t1[a0 - p0:a1 - p0, :], t2[a0 - p0:a1 - p0, :], ALU.add)
        else:
            self.tt("pool", dst, t1, t2, ALU.add)

    def norm_rope(self, *a, **kw):
        for _ in self.norm_rope_gen(*a, **kw):
            pass

    def pipe_step(self):
        for g in list(self.active):
            try:
                next(g)
            except StopIteration:
                self.active.remove(g)

    def pipe_add(self, g):
        next(g)
        self.active.append(g)

    def pipe_drain(self):
        while self.active:
            self.pipe_step()

    def load_rope(self, name):
        cos = self.alloc(T)
        sin = self.alloc(T)
        self.ld(cos, self.I[name][0], "cst")
        self.ld(sin, self.I[name][1], "cst")
        return cos, sin

    def attend(self, qT, kT, K, pb, vaug, ychunk, ob, scale, window=False, sinkcol=None, pts=None, recs=None):
        so = 64 - ob
        for qi, (q0, n) in enumerate(TBS):
            if qi == 4:
                tiles = [(16, None), (17, None)]
            elif not window:
                tiles = [(j, None) for j in range(NTILE)]
            else:
                i0 = 4 * qi
                tiles = [(16, None), (17, None)] + [(j, j - i0 + 1) for j in range(max(0, i0 - 1), min(15, i0 + 4) + 1)]
            acc = self.bank("acc")
            sts = {}
            LA = 3
            for idx in range(len(tiles) + LA):
                if idx < len(tiles):
                    j = tiles[idx][0]
                    st = self.bank("s")
                    self.mm(st[:, 0:n], kT[pb:pb + K, j * 128:(j + 1) * 128], qT[pb:pb + K, q0:q0 + n])
                    sts[idx] = st
                if idx >= LA:
                    i2 = idx - LA
                    j, mi = tiles[i2]
                    st = sts.pop(i2)
                    pt = self.rot("pt", pts)
                    self.act(pt[:, 0:n], st[:, 0:n], AF.Exp, scale=scale)
                    if mi is not None:
                        self.tt("pool", pt[:, 0:n], pt[:, 0:n], self.wmask[:, mi * 512:mi * 512 + n], ALU.mult)
                    self.mm(acc[:, 0:n], vaug(j), pt[:, 0:n], start=(i2 == 0), stop=(i2 == len(tiles) - 1))
            rec = self.rot("rec", recs)
            if sinkcol is not None:
                self.ts("dve", rec[ob:ob + 64, 0:n], acc[so:so + 64, 0:n], sinkcol[so:so + 64, :], None, ALU.add)
                self.recip(rec[ob:ob + 64, 0:n], rec[ob:ob + 64, 0:n])
            else:
                self.recip(rec[ob:ob + 64, 0:n], acc[so:so + 64, 0:n])
            self.tt("dve", ychunk[ob:ob + 64, q0:q0 + n], acc[ob:ob + 64, 0:n], rec[ob:ob + 64, 0:n], ALU.mult)

    def gqa_mixer(self, l, b, c0, ych0, qgain, kgain, window, sink):
        I = self.I
        m = self.mark()
        w = self.alloc(8 * 512, BF16).rearrange("p (k n) -> p k n", k=8)
        cos, sin = self.load_rope("ropeA")
        qT = self.alloc(4 * T, BF16).rearrange("p (c t) -> p c t", c=4)
        kT = self.alloc(2 * T, BF16).rearrange("p (c t) -> p c t", c=2)
        wk2 = self.alloc(8 * 256, BF16).rearrange("p (k n) -> p k n", k=8)
        VW = 320
        V = self.alloc(NTILE * VW, BF16).rearrange("p (i v) -> p i v", i=NTILE)
        m2 = self.mark()
        self.wst = [self.alloc(4096), self.alloc(4096)]
        self.load_w(w, I["w_in"][l][:, c0:c0 + 512], 8, 512)
        self.release(m2)
        pts = [self.alloc(512, BF16) for _ in range(4)]
        recs = [self.alloc(512) for _ in range(2)]
        tmp = self.rope_tmp()
        for c in range(2):
            self.cp("pool", wk2[:, :, c * 128:c * 128 + 64], w[:, :, 256 + c * 64:320 + c * 64])
            self.cp("pool", wk2[:, :, c * 128 + 64:c * 128 + 128], w[:, :, 256 + c * 64:320 + c * 64])
        self.ms("pool", V[:, :, 0:64], 1.0)
        self.ms("pool", V[:, :, 128:192], 1.0)
        self.ms("pool", V[:, :, 256:320], 1.0)
        for h in range(4):
            o = 64 - (h % 2) * 64
            self.ms("pool", qT[o:o + 64, h, :], 0.0)
        self.bankmap = self.BANKS_PROJ
        self.active = []
        for ti, (t0, n) in enumerate(TBS):
            for ci in range(4):
                pr = self.bank("pj")
                self.proj_fm(pr[:, 0:n], w if ci < 2 else wk2, (ci % 2) * 128, 128, t0, n)
                if ci < 2:
                    dst = [(0, 64, qT[0:64, 2 * ci, t0:t0 + n]), (64, 128, qT[64:128, 2 * ci + 1, t0:t0 + n])]
                else:
                    dst = kT[:, ci - 2, t0:t0 + n]
                g = None
                if qgain is not None:
                    g = qgain if ci < 2 else kgain
                self.pipe_step()
                self.pipe_add(self.norm_rope_gen(pr[:, 0:n], 0, 128, n, t0, dst, tmp, self.PA, cos, sin, gain=g,
                                                 ones=self.BD, nfeat=64))
            pv = self.bank("pj")
            nt = n // 128
            for ii in range(nt):
                self.proj_tm(pv[:, ii * 128:(ii + 1) * 128], w, 384, 128, t0 // 128 + ii)
            pvv = pv[:, 0:nt * 128].rearrange("p (i c) -> p i c", c=128)
            i0 = t0 // 128
            self.pipe_step()
            self.act(V[:, i0:i0 + nt, 64:128], pvv[:, :, 0:64], AF.Copy)
            self.act(V[:, i0:i0 + nt, 192:256], pvv[:, :, 64:128], AF.Copy)
        self.pipe_drain()
        scale = 64 ** -0.5
        self.bankmap = self.BANKS_ATT
        for h in range(4):
            kv = h // 2
            ob = (h % 2) * 64
            voff = (64 if ob == 0 else 0) + kv * 128
            self.attend(qT[:, h, :], kT[:, kv, :], 128, 0, lambda j, vo=voff: V[:, j, vo:vo + 128],
                        self.Y[:, ych0 + h // 2, :], ob, scale, window=window,
                        sinkcol=(self.sinkexp[:, h:h + 1] if sink else None), pts=pts, recs=recs)
        self.bankmap = self.BANKS_DEFAULT
        self.release(m)

    def pA(self, l, b):
        self.gqa_mixer(l, b, 0, 0, self.aqn, self.akn, False, False)

    def pC(self, l, b):
        self.gqa_mixer(l, b, 1544, 4, None, None, True, True)


    def pD(self, l, b):
        I = self.I
        m = self.mark()
        wD = self.alloc(8 * 352, BF16).rearrange("p (k n) -> p k n", k=8)
        wuq = self.alloc(2 * 384, BF16).rearrange("p (k n) -> p k n", k=2)
        wukv = self.alloc(512, BF16).rearrange("p (k n) -> p k n", k=1)
        wv = self.alloc(256, BF16)
        cos, sin = self.load_rope("ropeD")
        qT = self.alloc(4 * T, BF16).rearrange("p (c t) -> p c t", c=4)
        kT = self.alloc(4 * T, BF16).rearrange("p (c t) -> p c t", c=4)
        VW = 384
        V = self.alloc(NTILE * VW, BF16).rearrange("p (i v) -> p i v", i=NTILE)
        m2 = self.mark()
        self.wst = [self.alloc(4096), self.alloc(4096)]
        self.load_w(wD, I["w_in"][l][:, 2056:2408], 8, 352)
        self.load_w(wuq, I["mla_w_uq"][l], 2, 384, rows_last=64)
        self.load_w(wukv, I["mla_w_ukv"][l], 1, 512)
        self.release(m2)
        for h in range(4):
            self.cp("pool", wv[:, h * 64:(h + 1) * 64], wukv[:, 0, h * 128 + 64:(h + 1) * 128])
        pts = [self.alloc(512, BF16) for _ in range(4)]
        recs = [self.alloc(512) for _ in range(2)]
        tmp = self.rope_tmp()
        cq0s = [self.alloc(512, BF16) for _ in range(2)]
        cq1s = [self.alloc(512, BF16) for _ in range(2)]
        ckvs = [self.alloc(512, BF16) for _ in range(2)]
        sqks = [self.alloc(512, BF16) for _ in range(2)]
        self.ms("pool", V[:, :, 64:128], 1.0)
        self.ms("pool", V[:, :, 256:320], 1.0)
        self.ms("pool", qT[64:128, :, :], 0.0)
        self.ms("pool", kT[64:128, :, :], 0.0)
        self.bankmap = self.BANKS_PROJ
        self.active = []
        voffs = (0, 128, 192, 320)
        for ti, (t0, n) in enumerate(TBS):
            p0 = self.bank("pj")
            self.proj_fm(p0[:, 0:n], wD, 0, 128, t0, n)
            p1 = self.bank("pj")
            self.proj_fm(p1[0:64, 0:n], wD, 128, 64, t0, n)
            p2 = self.bank("pj")
            self.proj_fm(p2[:, 0:n], wD, 192, 128, t0, n)
            p3 = self.bank("pj")
            self.proj_fm(p3[64:96, 0:n], wD, 320, 32, t0, n, tp=(0, 64))
            self.pipe_step()
            sq0 = self.rot("rp_sq", tmp["sq"])
            self.act(sq0[:, 0:n], p0[:, 0:n], AF.Square)
            sq1 = self.rot("rp_sq", tmp["sq"])
            self.act(sq1[0:64, 0:n], p1[0:64, 0:n], AF.Square)
            sqk = self.rot("dsqk", sqks)
            self.act(sqk[:, 0:n], p2[:, 0:n], AF.Square)
            ss = self.bank("aux")
            self.mm(ss[:, 0:n], self.onesB, sq0[:, 0:n], start=True, stop=False)
            self.mm(ss[:, 0:n], self.onesB[0:64, :], sq1[0:64, 0:n], start=False, stop=True)
            ssk = self.bank("aux2")
            self.mm(ssk[:, 0:n], self.onesB, sqk[:, 0:n])
            rs = self.rot("rp_rs", tmp["rs"])
            self.rstd(rs[:, 0:n], ss[:, 0:n], 1.0 / 192)
            rsk = self.rot("rp_rs", tmp["rs"])
            self.rstd(rsk[:, 0:n], ssk[:, 0:n], 1.0 / 128)
            cq0 = self.rot("cq0", cq0s)
            cq1 = self.rot("cq1", cq1s)
            self.stt(cq0[:, 0:n], p0[:, 0:n], self.mqn0, rs[:, 0:n], ALU.mult, ALU.mult)
            self.stt(cq1[0:64, 0:n], p1[0:64, 0:n], self.mqn1[0:64, :], rs[0:64, 0:n], ALU.mult, ALU.mult)
            ckv = self.rot("ckv", ckvs)
            self.stt(ckv[:, 0:n], p2[:, 0:n], self.kvn, rsk[:, 0:n], ALU.mult, ALU.mult)
            def krot_gen(p3=p3, t0=t0, n=n):
                yield from self.norm_rope_gen(p3[64:96, 0:n], 64, 96, n, t0, kT[64:96, 0, t0:t0 + n], tmp, self.PD,
                                              cos, sin)
                for h in range(1, 4):
                    self.cp("pool", kT[64:96, h, t0:t0 + n], kT[64:96, 0, t0:t0 + n])
            self.pipe_step()
            self.pipe_add(krot_gen())
            for h in range(4):
                pq = self.bank("pj")
                self.mm(pq[0:96, 0:n], wuq[:, 0, h * 96:(h + 1) * 96], cq0[:, 0:n], start=True, stop=False)
                self.mm(pq[0:96, 0:n], wuq[0:64, 1, h * 96:(h + 1) * 96], cq1[0:64, 0:n], start=False, stop=True)
                self.pipe_step()
                self.pipe_add(self.norm_rope_gen(pq[0:96, 0:n], 0, 96, n, t0, qT[0:96, h, t0:t0 + n], tmp, self.PD,
                                                 cos, sin))
            for h in range(4):
                pk = self.bank("pj")
                self.mm(pk[0:64, 0:n], wukv[:, 0, h * 128:h * 128 + 64], ckv[:, 0:n])
                self.cp("act", kT[0:64, h, t0:t0 + n], pk[0:64, 0:n])
            nt = n // 128
            i0 = t0 // 128
            for pr in range(nt // 2):
                pv = self.bank("pj")
                for ii in range(2):
                    tok = (pr * 2 + ii) * 128
                    self.mm(pv[:, ii * 256:(ii + 1) * 256], ckv[:, tok:tok + 128], wv)
                pvv = pv[:, :].rearrange("p (i c) -> p i c", c=256)
                for h in range(4):
                    self.cp("act" if h % 2 == 0 else "dve", V[:, i0 + pr * 2:i0 + pr * 2 + 2, voffs[h]:voffs[h] + 64],
                            pvv[:, :, h * 64:(h + 1) * 64])
        self.pipe_drain()
        scale = 96 ** -0.5
        self.bankmap = self.BANKS_ATT
        for h in range(4):
            ob = (h % 2) * 64
            vo = voffs[h] - ob
            self.attend(qT[:, h, :], kT[:, h, :], 128, 0, lambda j, vo=vo: V[:, j, vo:vo + 128],
                        self.Y[:, 6 + h // 2, :], ob, scale, pts=pts, recs=recs)
        self.bankmap = self.BANKS_DEFAULT
        self.release(m)

    def pB(self, l, b):
        I = self.I
        m = self.mark()
        BT = self.alloc(2 * T, BF16).rearrange("p (g t) -> p g t", g=2)
        CT = self.alloc(2 * T, BF16).rearrange("p (g t) -> p g t", g=2)
        xs_tok = self.alloc(NTILE * 256, BF16).rearrange("p (i c) -> p i c", i=NTILE)
        B_tok = self.alloc(NTILE * 256, BF16).rearrange("p (i c) -> p i c", i=NTILE)
        z_tok = self.alloc(NTILE * 256, BF16).rearrange("p (i c) -> p i c", i=NTILE)
        dt = self.alloc(144)
        da = self.alloc(144)
        cs = self.alloc(144)
        tot = self.alloc(144)
        ecs = self.alloc(144)
        dtw = self.alloc(144)
        cdec = self.alloc(144)
        ncs = self.alloc(144)
        v3 = lambda a: a.rearrange("p (i c) -> p i c", c=8)
        m2 = self.mark()
        wB = self.alloc(8 * 1032, BF16).rearrange("p (k n) -> p k n", k=8)
        m3 = self.mark()
        self.wst_cap = 2064
        self.wst = [self.alloc(2064), self.alloc(2064)]
        self.load_w(wB, I["w_in"][l][:, 512:1544], 8, 1032)
        self.wst_cap = 4096
        self.release(m3)
        Gpl = [self.alloc(GW), self.alloc(GW)]
        accl = [self.alloc(GW), self.alloc(GW)]
        xsT = self.alloc(2 * T, BF16).rearrange("p (c t) -> p c t", c=2)
        pdt = self.bank("aux")
        for i in range(NTILE):
            self.proj_tm(pdt[:, i * 8:(i + 1) * 8], wB, 1024, 8, i)
        xr = ecs
        self.tt("dve", v3(xr), v3(pdt[:, 0:144]), self.dtb.unsqueeze(1).to_broadcast([128, NTILE, 8]), ALU.add)
        self.ts("dve", cs, xr, -1.0, None, ALU.mult)
        self.tt("dve", cs, cs, xr, ALU.max)
        self.act(cs, cs, AF.Exp, scale=-1.0)
        self.act(cs, cs, AF.Ln, bias=1.0)
        self.ts("dve", xr, xr, 0.0, None, ALU.max)
        self.tt("dve", dt, xr, cs, ALU.add)
        self.tt("dve", v3(da), v3(dt), self.arow.unsqueeze(1).to_broadcast([128, NTILE, 8]), ALU.mult)
        if BSTOP == 1:
            self.top = m
            return
        for ip in range(NTILE // 2):
            pz = self.bank("pj")
            for ii in range(2):
                self.proj_tm(pz[:, ii * 256:(ii + 1) * 256], wB, 0, 256, ip * 2 + ii)
            self.act(z_tok[:, ip * 2:ip * 2 + 2, :], pz[:, :].rearrange("p (i c) -> p i c", c=256), AF.Silu)
        if BSTOP == 2:
            self.top = m
            return
        for Gp in Gpl:
            for c0 in (0, 2049, 2050, 2307):
                self.ms("pool", Gp[:, c0:c0 + 1], 0.0)
        self.bankmap = self.BANKS_PROJ
        for cidx in range(6):
            Gp = Gpl[cidx % 2]
            acc = accl[cidx % 2]
            for ti, (t0, n) in enumerate(TBS):
                pr = self.bank("pj")
                self.proj_fm(pr[:, 0:n], wB, 256 + cidx * 128, 128, t0, n)
                off = 1 + t0 if ti < 4 else 2051
                self.cp("dve" if ti % 2 else "act", Gp[:, off:off + n], pr[:, 0:n])
            W = 2306
            self.act(acc[:, 0:W], Gp[:, 1:1 + W], AF.Identity, bias=self.scb[:, cidx:cidx + 1],
                     scale=self.scw[:, cidx * 3 + 1:cidx * 3 + 2])
            self.stt(acc[:, 0:W], Gp[:, 0:W], self.scw[:, cidx * 3:cidx * 3 + 1], acc[:, 0:W], ALU.mult, ALU.add)
            self.stt(acc[:, 0:W], Gp[:, 2:2 + W], self.scw[:, cidx * 3 + 2:cidx * 3 + 3], acc[:, 0:W], ALU.mult, ALU.add)
            dstT = xsT[:, cidx, :] if cidx < 2 else (BT[:, cidx - 2, :] if cidx < 4 else CT[:, cidx - 4, :])
            self.act(dstT[:, 0:NLAT], acc[:, 0:NLAT], AF.Silu)
            self.act(dstT[:, NLAT:T], acc[:, 2050:2306], AF.Silu)
        self.bankmap = self.BANKS_DEFAULT
        if BSTOP == 3:
            self.top = m
            return
        for i in range(NTILE):
            ptr = self.bank("aux2").bitcast(BF16)
            for c in range(2):
                self.tr(ptr[:, c * 128:(c + 1) * 128], xsT[:, c, i * 128:(i + 1) * 128], self.identB)
                self.tr(ptr[:, 256 + c * 128:256 + (c + 1) * 128], BT[:, c, i * 128:(i + 1) * 128], self.identB)
            self.cp("act", xs_tok[:, i, :], ptr[:, 0:256])
            self.cp("dve", B_tok[:, i, :], ptr[:, 256:512])
        self.release(m2)
        if BSTOP == 4:
            self.top = m
            return
        ybuf = self.alloc(NTILE * 256).rearrange("p (i c) -> p i c", i=NTILE)
        H = [self.alloc(256), self.alloc(256)]
        Hb = [self.alloc(256, BF16), self.alloc(256, BF16)]
        dtr4 = [[self.alloc(512) for _ in range(2)] for _ in range(2)]
        decs = [self.alloc(128) for _ in range(8)]
        scs = [self.alloc(128, BF16) for _ in range(8)]
        tmpos = [self.alloc(256) for _ in range(4)]
        xsws = [self.alloc(256, BF16) for _ in range(4)]
        t1s = [self.alloc(256) for _ in range(2)]
        t2s = [self.alloc(256) for _ in range(2)]
        yns = [self.alloc(256, BF16) for _ in range(2)]
        junk = self.alloc(256, BF16)
        ssq = self.alloc(NTILE)
        pcs = self.bank("aux")
        ptot = self.bank("aux2")
        dav = v3(da)
        for j in range(NTILE):
            self.mm(pcs[:, j * 8:j * 8 + 4], self.triF, dav[:, j, 0:4])
            self.mm(pcs[:, j * 8 + 4:j * 8 + 8], self.triB, dav[:, j, 4:8])
            self.mm(ptot[:, j * 8:(j + 1) * 8], self.onesF, dav[:, j, :])
        self.cp("dve", cs, pcs[:, 0:144])
        self.cp("dve", tot, ptot[:, 0:144])
        self.act(ecs, cs, AF.Exp)
        self.tt("dve", dtw, tot, cs, ALU.subtract)
        self.act(dtw, dtw, AF.Exp)
        self.tt("dve", dtw, dtw, dt, ALU.mult)
        self.act(cdec, tot, AF.Exp)
        self.ts("dve", ncs, cs, -1.0, None, ALU.mult)
        if BSTOP == 5:
            self.top = m
            return
        self.ms("pool", ybuf.rearrange("p i c -> p (i c)"), 0.0)
        for d in range(2):
            self.ms("pool", H[d], 0.0)
            self.ms("pool", Hb[d], 0.0)
        forder = [16, 17] + list(range(16))
        border = [17, 16] + list(range(15, -1, -1))
        tri = (self.triF, self.triB)
        h3 = lambda a: a.rearrange("p (h c) -> p h c", h=4)
        hd = [(h, d) for h in range(4) for d in range(2)]
        pEb = (((self.pss[0], self.pss[1])), ((self.pss[5], self.pss[7])))
        pGb = self.pss[2]
        pSb = self.pss[3]
        pYb = self.pss[4]
        pOb = self.pss[6]

        def stage_a(i):
            p = i % 2
            for d in range(2):
                for h in range(4):
                    col = i * 8 + d * 4 + h
                    self.ts("pool", dtr4[p][d][:, h * 128:(h + 1) * 128], tri[d], da[:, col:col + 1], 1.0,
                            ALU.mult, ALU.mult)
                self.mm(pEb[p][d][:, :], self.onesF, dtr4[p][d], start=True, stop=False)
                self.mm(pEb[p][d][:, :], self.identB, self.nmb4[d], start=False, stop=True)

        def stage_g(i):
            for g in range(2):
                self.mm(pGb[:, g * 128:(g + 1) * 128], BT[:, g, i * 128:(i + 1) * 128], CT[:, g, i * 128:(i + 1) * 128])

        def stage_bc(i):
            p = i % 2
            for q_, (h, d) in enumerate(hd):
                col = i * 8 + d * 4 + h
                self.act(decs[q_], pEb[p][d][:, h * 128:(h + 1) * 128], AF.Exp, bias=ncs[:, col:col + 1])
            for q_, (h, d) in enumerate(hd):
                col = i * 8 + d * 4 + h
                g = h // 2
                self.stt(scs[q_], decs[q_], dt[:, col:col + 1], pGb[:, g * 128:(g + 1) * 128], ALU.mult, ALU.mult)

        def stage_d(i):
            for q_, (h, d) in enumerate(hd):
                self.mm(pYb[:, h * 64:(h + 1) * 64], scs[q_], xs_tok[:, i, h * 64:(h + 1) * 64],
                        start=(d == 0), stop=(d == 1))

        def scan(i):
            dj = ((0, forder[i]), (1, border[i]))
            xsw = {}
            for d, jj in dj:
                c4 = jj * 8 + d * 4
                xsw[d] = self.rot("xsw", xsws)
                self.tt("pool", h3(xsw[d]), h3(xs_tok[:, jj, :]), dtw[:, c4:c4 + 4].unsqueeze(2).to_broadcast([128, 4, 64]),
                        ALU.mult)
            for d, jj in dj:
                for h in range(4):
                    self.mm(pOb[:, d * 256 + h * 64:d * 256 + (h + 1) * 64], CT[:, h // 2, jj * 128:(jj + 1) * 128],
                            Hb[d][:, h * 64:(h + 1) * 64])
            for d, jj in dj:
                for h in range(4):
                    g = h // 2
                    self.mm(pSb[:, d * 256 + h * 64:d * 256 + (h + 1) * 64], B_tok[:, jj, g * 128:(g + 1) * 128],
                            xsw[d][:, h * 64:(h + 1) * 64])
            tm = {}
            for d, jj in dj:
                c4 = jj * 8 + d * 4
                tm[d] = self.rot("tmpo", tmpos)
                self.tt("dve", h3(tm[d]), h3(pOb[:, d * 256:(d + 1) * 256]),
                        ecs[:, c4:c4 + 4].unsqueeze(2).to_broadcast([128, 4, 64]), ALU.mult)
            for d, jj in dj:
                c4 = jj * 8 + d * 4
                self.tt("dve", h3(H[d]), h3(H[d]), cdec[:, c4:c4 + 4].unsqueeze(2).to_broadcast([128, 4, 64]), ALU.mult)
                self.tt("dve", H[d], H[d], pSb[:, d * 256:(d + 1) * 256], ALU.add)
                self.cp("act", Hb[d], H[d])
            self.tt("dve", ybuf[:, i, :], ybuf[:, i, :], pYb[:, 0:256], ALU.add)
            for d, jj in dj:
                self.tt("pool", ybuf[:, jj, :], ybuf[:, jj, :], tm[d], ALU.add)
        stage_a(0)
        stage_g(0)
        for i in range(NTILE):
            stage_bc(i)
            if i + 1 < NTILE:
                stage_a(i + 1)
            stage_d(i)
            if i + 1 < NTILE:
                stage_g(i + 1)
            scan(i)
        if BSTOP == 6:
            self.top = m
            return
        for j in range(NTILE):
            t1 = self.rot("bt1", t1s)
            self.tt("dve", t1, xs_tok[:, j, :], self.Drow, ALU.mult)
            self.tt("dve", ybuf[:, j, :], t1, ybuf[:, j, :], ALU.add)
            self.tt("pool", ybuf[:, j, :], ybuf[:, j, :], z_tok[:, j, :], ALU.mult)
            self.act(junk, ybuf[:, j, :], AF.Square, accum=ssq[:, j:j + 1])
        self.rstd(ssq, ssq, 1.0 / 256)
        for j in range(NTILE):
            yn = self.rot("byn", yns)
            self.ts("dve", yn, ybuf[:, j, :], ssq[:, j:j + 1], None, ALU.mult)
            ptr = self.bank("acc").bitcast(BF16)
            for c in range(2):
                self.tr(ptr[:, c * 128:(c + 1) * 128], yn[:, c * 128:(c + 1) * 128], self.identB)
            self.act(self.Y[:, 2, j * 128:(j + 1) * 128], ptr[:, 0:128], AF.Identity, scale=self.snw[:, 0:1])
            self.ts("dve", self.Y[:, 3, j * 128:(j + 1) * 128], ptr[:, 128:256], self.snw[:, 1:2], None, ALU.mult)
        self.release(m)

    def pO(self, l, b):
        I = self.I
        m = self.mark()
        wo = self.alloc(8 * 1024, BF16).rearrange("p (k n) -> p k n", k=8)
        m2 = self.mark()
        self.wst = [self.alloc(4096), self.alloc(4096)]
        self.load_w(wo, I["w_out"][l], 8, 1024)
        self.release(m2)
        xbs = [self.alloc(4096), self.alloc(4096)]
        tmp = self.norm_tmp()
        def ldx(ti):
            t0_, n_ = TBS[ti]
            v = xbs[ti % 2].rearrange("p (k t) -> p k t", k=8)
            self.ld(v[:, :, 0:n_], self.xt_view(b, t0_, n_), "xb%d" % (ti % 2), rk=self.tb_keys(b, ti))
        ldx(0)
        for ti, (t0, n) in enumerate(TBS):
            s = ti % 2
            xb = xbs[s].rearrange("p (k t) -> p k t", k=8)
            if ti + 1 < len(TBS):
                ldx(ti + 1)
            r = b if ti < 4 else 2
            for dc in range(8):
                po = self.bank("pj")
                for k in range(8):
                    self.mm(po[:, 0:n], wo[:, k, dc * 128:(dc + 1) * 128], self.Y[:, k, t0:t0 + n],
                            start=(k == 0), stop=(k == 7))
                self.stt(xb[:, dc, 0:n], po[:, 0:n], self.modv[:, 2, dc, r:r + 1], xb[:, dc, 0:n], ALU.mult, ALU.add)
            self.st(self.xt_view(b, t0, n), xb[:, :, 0:n], "xst%d" % s, wk=self.tb_keys(b, ti))
            self.norm_mod_block(xb, n, t0, self.A2v, 3, r, tmp)
        self.release(m)

    def pF(self, l, b):
        I = self.I
        self.release(self.y_mark)
        m = self.mark()
        wd = self.alloc(NFC * 1024, BF16).rearrange("p (k n) -> p k n", k=NFC)
        wd_mark = self.mark()
        wup = [self.alloc(8 * 256, BF16).rearrange("p (k n) -> p k n", k=8) for _ in range(2)]
        stgs = [self.alloc(2048).rearrange("p (k n) -> p k n", k=8) for _ in range(2)]
        dstg = [self.alloc(1024) for _ in range(2)]
        Gps = [self.alloc(GW) for _ in range(2)]
        accs = [self.alloc(GW) for _ in range(2)]
        abufs = [self.alloc(T, BF16) for _ in range(2)]
        sgs = [self.alloc(T, BF16) for _ in range(2)]
        us = [self.alloc(T, BF16) for _ in range(2)]
        for Gp in Gps:
            for c0 in (0, 2049, 2050, 2307):
                self.ms("pool", Gp[:, c0:c0 + 1], 0.0)
        wupd = I["ffn_w_up"][l]
        wdd = I["ffn_w_down"][l]
        W = 2306

        def ldw(fc_):
            s_ = fc_ % 2
            self.ld(stgs[s_][:, :, 0:128], wupd[:, fc_ * 128:(fc_ + 1) * 128].rearrange("(k p) n -> p k n", p=128),
                    "wup%d" % s_)
            self.ld(stgs[s_][:, :, 128:256],
                    wupd[:, DFF + fc_ * 128:DFF + (fc_ + 1) * 128].rearrange("(k p) n -> p k n", p=128), "wup%d" % s_)
            self.ld(dstg[s_], wdd[fc_ * 128:(fc_ + 1) * 128, :], "wdn%d" % s_)

        def conv_stage(fc_, stage):
            s_ = fc_ % 2
            Gp, acc, abuf, sg, u = Gps[s_], accs[s_], abufs[s_], sgs[s_], us[s_]
            if stage == 0:
                self.act(acc[:, 0:W], Gp[:, 1:1 + W], AF.Identity, bias=self.fcb[:, fc_:fc_ + 1],
                         scale=self.fcw[:, fc_ * 3 + 1:fc_ * 3 + 2])
            elif stage == 1:
                self.stt(acc[:, 0:W], Gp[:, 0:W], self.fcw[:, fc_ * 3:fc_ * 3 + 1], acc[:, 0:W], ALU.mult, ALU.add)
            elif stage == 2:
                self.stt(acc[:, 0:W], Gp[:, 2:2 + W], self.fcw[:, fc_ * 3 + 2:fc_ * 3 + 3], acc[:, 0:W],
                         ALU.mult, ALU.add)
            elif stage == 3:
                self.act(sg[:, 0:NLAT], acc[:, 0:NLAT], AF.Silu)
                self.act(sg[:, NLAT:T], acc[:, 2050:2306], AF.Silu)
            else:
                self.tt("dve", u, abuf, sg, ALU.mult)
                self.st(self.usc[fc_], u, "ust%d" % s_, wk=[("usc", fc_)])
        ldw(0)
        ldw(1)
        self.cp("pool", wup[0], stgs[0])
        self.cp("pool", wd[:, 0, :], dstg[0])
        for fc in range(NFC + 1):
            s = fc % 2
            for ti, (t0, n) in enumerate(TBS):
                if fc < NFC:
                    w = wup[s]
                    pa = self.bank("pj")
                    for k_ in range(8):
                        self.mm(pa[:, 0:n], w[:, k_, 0:128], self.hT[:, k_, t0:t0 + n], start=(k_ == 0), stop=(k_ == 7))
                    pg = self.bank("s")
                    for k_ in range(8):
                        self.mm(pg[:, 0:n], w[:, k_, 128:256], self.hT[:, k_, t0:t0 + n], start=(k_ == 0), stop=(k_ == 7))
                    off = 1 + t0 if ti < 4 else 2051
                    self.cp("act", abufs[s][:, t0:t0 + n], pa[:, 0:n])
                    self.cp("dve", Gps[s][:, off:off + n], pg[:, 0:n])
                if fc > 0:
                    conv_stage(fc - 1, ti)
                if ti == 1 and fc + 1 < NFC:
                    self.cp("pool", wup[1 - s], stgs[1 - s])
                    self.cp("pool", wd[:, fc + 1, :], dstg[1 - s])
                if ti == 2 and fc + 2 < NFC:
                    ldw(fc + 2)
        self.release(wd_mark)
        ubs = [self.alloc(NFC * 512, BF16).rearrange("p (f t) -> p f t", f=NFC) for _ in range(2)]
        xbs = [self.alloc(4096), self.alloc(4096)]
        last = (l == 1)
        if last:
            tmp = self.norm_tmp()
            ots = [self.alloc(1024), self.alloc(1024)]
        ukeys = [("usc", fc) for fc in range(NFC)]
        nblk = 4 if last else 5

        def ldd(ti_):
            t0_, n_ = TBS[ti_]
            s_ = ti_ % 2
            self.ld(ubs[s_][:, :, 0:n_], self.usc[:, :, t0_:t0_ + n_].rearrange("f p t -> p f t"), "ub%d" % s_, rk=ukeys)
            v = xbs[s_].rearrange("p (k t) -> p k t", k=8)
            self.ld(v[:, :, 0:n_], self.xt_view(b, t0_, n_), "xb%d" % s_, rk=self.tb_keys(b, ti_))
        for ti, (t0, n) in enumerate(TBS):
            if last and ti == 4:
                break
            s = ti % 2
            ub = ubs[s]
            xb = xbs[s].rearrange("p (k t) -> p k t", k=8)
            if ti == 0:
                ldd(0)
            if ti + 1 < nblk:
                ldd(ti + 1)
            r = b if ti < 4 else 2
            for dc in range(8):
                po = self.bank("pj")
                for fc in range(NFC):
                    self.mm(po[:, 0:n], wd[:, fc, dc * 128:(dc + 1) * 128], ub[:, fc, 0:n],
                            start=(fc == 0), stop=(fc == NFC - 1))
                self.stt(xb[:, dc, 0:n], po[:, 0:n], self.modv[:, 5, dc, r:r + 1], xb[:, dc, 0:n], ALU.mult, ALU.add)
            if not last:
                self.st(self.xt_view(b, t0, n), xb[:, :, 0:n], "xst%d" % s, wk=self.tb_keys(b, ti))
                continue
            ss = self.bank("aux")
            for k in range(8):
                sq = self.rot("sqb", tmp["sq"])
                self.act(sq[:, 0:n], xb[:, k, 0:n], AF.Square)
                self.mm(ss[:, 0:n], self.onesB, sq[:, 0:n], start=(k == 0), stop=(k == 7))
            rs = self.rot("rsb", tmp["rs"])
            self.rstd(rs[:, 0:n], ss[:, 0:n], 1.0 / D)
            for k in range(8):
                self.stt(xb[:, k, 0:n], xb[:, k, 0:n], self.fnw[:, k:k + 1], rs[:, 0:n], ALU.mult, ALU.mult)
            for tt_ in range(n // 128):
                ot = self.rot("ot", ots)
                p0 = self.bank("s")
                p1 = self.bank("acc")
                for k in range(8):
                    pp = p0 if k < 4 else p1
                    self.tr(pp[:, (k % 4) * 128:(k % 4 + 1) * 128], xb[:, k, tt_ * 128:(tt_ + 1) * 128], self.identF)
                self.cp("act", ot[:, 0:512], p0[:, :])
                self.cp("dve", ot[:, 512:1024], p1[:, :])
                tok = t0 + tt_ * 128
                self.st(self.out[b, tok:tok + 128, :], ot, "ost%d" % (self.rotc["ot"] % 2), wk=[("out", b, tok)])
                self.outkeys.append(("out", b, tok))
        self.release(m)

    def build(self, stop=None):
        self.outkeys = []
        self.P.cur_tag = "setup"
        self.setup_consts()
        self.P.cur_tag = "x0"
        self.phase_x0()
        done = False
        for l in range(2):
            self.P.cur_tag = ("lsetup", l)
            self.layer_setup(l)
            for b in range(2):
                self.lb_mark = self.mark()
                self.hT = self.alloc(8 * T, BF16).rearrange("p (k t) -> p k t", k=8)
                self.y_mark = self.mark()
                self.Y = self.alloc(8 * T, BF16).rearrange("p (k t) -> p k t", k=8)
                for pi, ph in enumerate(("p1", "pA", "pB", "pC", "pD", "pO", "pF")):
                    self.P.cur_tag = (l, b, ph)
                    if MARK:
                        self.act(self.mk, self.mk, AF.Abs)
                    getattr(self, ph)(l, b)
                    if stop == (l, b, ph):
                        done = True
                        break
                if done:
                    break
                self.release(self.lb_mark)
            if done:
                break
        if done:
            self.dump("hT", self.hT.rearrange("p k t -> p (k t)"), 8 * T)
            self.dump("Y", self.Y.rearrange("p k t -> p (k t)"), 8 * T)
        keys = list(self.outkeys) + [("dbg", n, c) for n in self.dump_aps for c in range(0, 8 * T, 512)]
        self.P.op("sp", lambda e: e.nop(), reads=keys, writes=[])


def build_program(stop=None, dumps=()):
    nc = bass.Bass("TRN2", target_bir_lowering=False)
    with ExitStack() as es:
        kb = KB(nc, es, dumps)
        kb.build(stop)
        kb.P.finalize(kb.sem)
        print("ops", len(kb.P.allops), "waits", kb.P.nwaits, "sems", kb.nsem, flush=True)
        with nc.Block() as block:
            kb.P.emit(block)
    return nc


def make_in_maps(inputs):
    cst = host_consts()
    maps = []
    for core in range(8):
        m = {}
        for name, shp in WSPEC:
            if name in cst:
                m[name] = cst[name]
            elif name in ("x", "c", "ctx"):
                m[name] = np.ascontiguousarray(np.asarray(inputs[name], dtype=np.float32)[2 * core:2 * core + 2])
            else:
                m[name] = np.ascontiguousarray(np.asarray(inputs[name], dtype=np.float32))
        maps.append(m)
    return maps


_NC_CACHE = {}


def kernel(**inputs):
    if "nc" not in _NC_CACHE:
        _NC_CACHE["nc"] = build_program()
    nc = _NC_CACHE["nc"]
    maps = make_in_maps(inputs)
    res = run_bass_kernel_spmd(nc, maps, core_ids=list(range(8)))
    return np.concatenate([np.asarray(r["out"]) for r in res.results], axis=0).astype(np.float32)
```

```python
import numpy as np
import concourse.bass as bass
import concourse.mybir as mybir
from concourse.bass_utils import run_bass_kernel_spmd

F32 = mybir.dt.float32
BF16 = mybir.dt.bfloat16
AF = mybir.ActivationFunctionType
ALU = mybir.AluOpType
AX = mybir.AxisListType

GRAN = 64
EPOCH = 30000
_ESZ = {F32: 4, BF16: 2}


def _esize(dt):
    return _ESZ.get(dt, 4)


class DmaSem:
    def __init__(self, handle):
        self.h = handle
        self.count = 0
        self.last_waited = 0


class Op:
    __slots__ = ("eng", "fn", "deps", "pdeps", "is_dma", "sem", "semval", "signal",
                 "sigdim", "sigval", "waits", "clock", "tag")

    def __init__(self, eng, fn):
        self.eng = eng
        self.fn = fn
        self.deps = {}
        self.pdeps = []
        self.tag = None
        self.is_dma = False
        self.sem = None
        self.semval = 0
        self.signal = False
        self.sigdim = None
        self.sigval = 0
        self.waits = None
        self.clock = None


class Prog:
    ENGS = ("pe", "act", "dve", "pool", "sp")

    def __init__(self, nc):
        self.nc = nc
        self.allops = []
        self.lastw = {}
        self.readers = {}
        self.pseudo = set()
        self.nops = 0
        self.cur_tag = None

    @staticmethod
    def keys_of(ap):
        t = ap.tensor
        pat = ap.ap
        es = _esize(ap.dtype)
        rowlen = pat[0][0]
        off = ap.offset
        start = off % rowlen if rowlen > 0 else off
        ext = 0
        for st, cnt in pat[1:]:
            ext += (cnt - 1) * abs(st)
        b0 = (start * es) // GRAN
        b1 = ((start + ext + 1) * es - 1) // GRAN
        name = t.name
        if name.startswith("ps"):
            return [(name, 0)]
        return [(name, b) for b in range(b0, b1 + 1)]

    def _collect(self, items):
        ks = []
        for it in items:
            if it is None:
                continue
            if isinstance(it, tuple):
                ks.append(it)
            else:
                ks.extend(self.keys_of(it))
        return ks

    def _record(self, o, reads, writes):
        rk = self._collect(reads)
        wk = self._collect(writes)
        deps = o.deps
        prk = [k for k in rk if k[0].startswith("ps") and k not in wk]
        for k in rk:
            w = self.lastw.get(k)
            if w is not None:
                deps[w] = "rar" if (k in self.pseudo and deps.get(w) != "raw") else "raw"
        for k in wk:
            w = self.lastw.get(k)
            if w is not None and w not in deps:
                deps[w] = "waw"
            for r in self.readers.get(k, ()):
                if r not in deps:
                    deps[r] = "war"
        for k in rk:
            self.readers.setdefault(k, []).append(o)
        for k in wk:
            self.lastw[k] = o
            self.readers[k] = []
            self.pseudo.discard(k)
        for k in prk:
            self.lastw[k] = o
            self.readers[k] = []
            self.pseudo.add(k)
        keep = {}
        for d, kind in deps.items():
            if d is o:
                continue
            if d.eng == o.eng and not d.is_dma and not o.is_dma:
                if o.eng == "pe":
                    continue
                if kind == "rar":
                    continue
            keep[d] = kind
        o.deps = keep
        o.tag = self.cur_tag
        for d in keep:
            if d.is_dma:
                sm = d.sem
                o.pdeps.append((sm, sm.count))
                if sm.count > sm.last_waited:
                    sm.last_waited = sm.count
        self.allops.append(o)

    def op(self, eng, fn, reads=(), writes=()):
        o = Op(eng, fn)
        self._record(o, reads, writes)
        return o

    def dma(self, eng, out, in_, sem, reads=None, writes=None, **kw):
        o = Op(eng, None)
        o.is_dma = True
        o.sem = sem
        if sem.last_waited > 0:
            o.pdeps.append((sem, sem.last_waited))
        o.fn = lambda e: e.dma_start(out=out, in_=in_, **kw)
        self._record(o, reads if reads is not None else [in_],
                     writes if writes is not None else [out])
        sem.count += 16
        o.semval = sem.count
        return o

    def finalize(self, sem_alloc):
        for o in self.allops:
            for d in o.deps:
                if not d.is_dma:
                    d.signal = True
        cnt = {e: 0 for e in self.ENGS}
        ep = {e: 0 for e in self.ENGS}
        self.engsem = {}
        for o in self.allops:
            if o.is_dma:
                o.sigdim = o.sem
                o.sigval = o.semval
                continue
            if o.signal:
                e = o.eng
                if cnt[e] >= EPOCH:
                    cnt[e] = 0
                    ep[e] += 1
                cnt[e] += 1
                dim = (e, ep[e])
                if dim not in self.engsem:
                    self.engsem[dim] = sem_alloc()
                o.sigdim = dim
                o.sigval = cnt[e]
        known = {e: {} for e in self.ENGS}
        nwaits = 0
        for o in self.allops:
            K = known[o.eng]
            need = {}
            for d in o.deps:
                if d.is_dma:
                    continue
                dim, val = d.sigdim, d.sigval
                if K.get(dim, 0) < val and need.get(dim, 0) < val:
                    need[dim] = val
            for sem, val in o.pdeps:
                if K.get(sem, 0) < val and need.get(sem, 0) < val:
                    need[sem] = val
            for d in o.deps:
                ck = d.clock
                for dim, val in ck.items():
                    if K.get(dim, 0) < val:
                        K[dim] = val
            for dim, val in need.items():
                if K.get(dim, 0) < val:
                    K[dim] = val
            o.waits = list(need.items())
            nwaits += len(o.waits)
            ck = dict(K)
            if o.is_dma or o.signal:
                ck[o.sigdim] = max(ck.get(o.sigdim, 0), o.sigval)
            o.clock = ck
        self.nwaits = nwaits

    def emit(self, block):
        per = {e: [] for e in self.ENGS}
        for o in self.allops:
            per[o.eng].append(o)
        engsem = self.engsem

        def run(eng_obj, ops):
            for o in ops:
                for dim, val in o.waits:
                    h = dim.h if isinstance(dim, DmaSem) else engsem[dim]
                    eng_obj.wait_ge(h, val)
                ins = o.fn(eng_obj)
                if o.is_dma:
                    ins.then_inc(o.sem.h, 16)
                elif o.signal:
                    ins.then_inc(engsem[o.sigdim], 1)

        @block.tensor
        def _(e):
            run(e, per["pe"])

        @block.scalar
        def _(e):
            run(e, per["act"])

        @block.vector
        def _(e):
            run(e, per["dve"])

        @block.gpsimd
        def _(e):
            run(e, per["pool"])

        @block.sync
        def _(e):
            run(e, per["sp"])
from contextlib import ExitStack

D = 1024
T = 2304
NLAT = 2048
NTILE = 18
TBS = [(0, 512), (512, 512), (1024, 512), (1536, 512), (2048, 256)]
DFF = 2816
NFC = 22
EPS = 1e-6
import os as _os
ARENA = int(_os.environ.get("KARENA", "53000"))
BSTOP = int(_os.environ.get("BSTOP", "0"))
ALIGN = int(_os.environ.get("KALIGN", "16"))
MARK = int(_os.environ.get("KMARK", "0"))
GW = 2310
NEG = -30000.0


def host_consts():
    f = np.float32
    idn = np.eye(128, dtype=f)
    k = np.arange(128)
    tri_f = (k[:, None] <= k[None, :]).astype(f)
    tri_b = (k[:, None] >= k[None, :]).astype(f)
    nm_f = np.where(k[None, :] >= k[:, None], 0.0, NEG).astype(f)
    nm_b = np.where(k[:, None] >= k[None, :], 0.0, NEG).astype(f)
    cF = np.concatenate([idn, tri_f, tri_b, nm_f, nm_b], axis=1)

    def perm(base, half):
        Pm = np.zeros((128, 128), f)
        for h0 in (base, base + half):
            q = half // 2
            for jj in range(q):
                Pm[h0 + jj + q, h0 + jj] = -1.0
                Pm[h0 + jj, h0 + q + jj] = 1.0
        return Pm
    PA = perm(0, 32) + perm(64, 32)
    PD = perm(64, 16)
    BD = np.zeros((128, 128), f)
    BD[0:64, 0:64] = 1.0
    BD[64:128, 64:128] = 1.0
    a = np.arange(128)[:, None]
    bq = np.arange(128)[None, :]
    masks = []
    for r in range(-1, 5):
        m = np.zeros((128, 512), f)
        for c in range(4):
            if r == c:
                m[:, c * 128:(c + 1) * 128] = 1.0
            elif r - c == -1:
                m[:, c * 128:(c + 1) * 128] = (bq <= a)
            elif r - c == 1:
                m[:, c * 128:(c + 1) * 128] = (a <= bq)
        masks.append(m)
    cB = np.concatenate([PA, PD, BD] + masks, axis=1)

    def tables(rot_dim):
        rows = NLAT // 64
        row = np.repeat(np.arange(rows, dtype=f), 64)
        col = np.tile(np.arange(64, dtype=f), rows)
        half = rot_dim // 2
        inv = np.power(f(10000.0), -np.arange(0, half, 2, dtype=f) / f(half)).astype(f)
        ar = (row[:, None] * inv[None, :]).astype(f)
        ac = (col[:, None] * inv[None, :]).astype(f)
        ang = np.concatenate([ar, ar, ac, ac], axis=-1)
        return np.cos(ang).astype(f), np.sin(ang).astype(f)
    cA, sA = tables(64)
    ropeA = np.zeros((2, 128, T), f)
    ropeA[0] = 1.0
    ropeA[0, 0:64, 0:NLAT] = cA.T
    ropeA[0, 64:128, 0:NLAT] = cA.T
    ropeA[1, 0:64, 0:NLAT] = sA.T
    ropeA[1, 64:128, 0:NLAT] = sA.T
    cD, sD = tables(32)
    ropeD = np.zeros((2, 128, T), f)
    ropeD[0] = 1.0
    ropeD[0, 64:96, 0:NLAT] = cD.T
    ropeD[1, 64:96, 0:NLAT] = sD.T
    return dict(cF=cF, cB=cB, ropeA=ropeA, ropeD=ropeD)


WSPEC = [
    ("x", [2, 2048, 1024]), ("c", [2, 1024]), ("ctx", [2, 256, 1024]), ("c_ctx", [1024]),
    ("norm1_w", [2, 1024]), ("w_mod", [2, 1024, 6144]), ("b_mod", [2, 6144]),
    ("w_in", [2, 1024, 2408]), ("attn_q_norm", [2, 64]), ("attn_k_norm", [2, 64]),
    ("ssm_conv_w", [2, 768, 3]), ("ssm_conv_b", [2, 768]), ("ssm_dt_bias", [2, 2, 4]),
    ("ssm_a_log", [2, 2, 4]), ("ssm_d", [2, 4]), ("ssm_norm_w", [2, 256]), ("win_sink", [2, 4]),
    ("mla_q_norm", [2, 192]), ("mla_w_uq", [2, 192, 384]), ("mla_kv_norm", [2, 128]),
    ("mla_w_ukv", [2, 128, 512]), ("w_out", [2, 1024, 1024]), ("norm2_w", [2, 1024]),
    ("ffn_w_up", [2, 1024, 5632]), ("ffn_conv_w", [2, 2816, 3]), ("ffn_conv_b", [2, 2816]),
    ("ffn_w_down", [2, 2816, 1024]), ("final_norm_w", [1024]),
    ("cF", [128, 640]), ("cB", [128, 384 + 6 * 512]), ("ropeA", [2, 128, T]), ("ropeD", [2, 128, T]),
]


class KB:
    def __init__(self, nc, es, dumps=()):
        self.nc = nc
        self.es = es
        self.P = Prog(nc)
        self.dumps = set(dumps)
        self.I = {}
        for name, shp in WSPEC:
            self.I[name] = nc.dram_tensor(name, shp, F32, kind="ExternalInput").ap()
        self.out = nc.dram_tensor("out", [2, 2048, 1024], F32, kind="ExternalOutput").ap()
        self.xt = nc.dram_tensor("xt", [2, 8, 128, T], F32, kind=("ExternalOutput" if "xt" in self.dumps else "Internal")).ap()
        self.usc = nc.dram_tensor("usc", [NFC, 128, T], BF16, kind="Internal").ap()
        self.arena = es.enter_context(nc.sbuf_tensor("arena", [128, ARENA], F32))
        self.pss = [es.enter_context(nc.psum_tensor("ps%d" % i, [128, 512], F32)) for i in range(8)]
        self.top = 0
        self.nsem = 0
        self.rotc = {}
        self.dsems = {}
        self.dump_aps = {}
        self.pe_slices = {}

    def sem(self):
        s = self.es.enter_context(self.nc.semaphore("s%d" % self.nsem))
        self.nsem += 1
        return s

    def dsem(self, name):
        if name not in self.dsems:
            self.dsems[name] = DmaSem(self.sem())
        return self.dsems[name]

    def alloc(self, n, dt=F32):
        es_ = _esize(dt)
        n4 = (n * es_ + 3) // 4
        off = (self.top + ALIGN - 1) // ALIGN * ALIGN
        self.top = off + n4
        assert self.top <= ARENA, ("arena overflow", self.top)
        a = self.arena[:, off:off + n4]
        if dt == F32:
            return a
        return a.bitcast(dt)[:, 0:n]

    def mark(self):
        return self.top

    def release(self, m):
        self.top = m

    def rot(self, name, items):
        i = self.rotc.get(name, 0)
        self.rotc[name] = i + 1
        return items[i % len(items)]

    BANKS_DEFAULT = {"pj": (0, 1), "s": (2, 3), "acc": (4, 5), "aux": (6,), "aux2": (7,)}
    BANKS_PROJ = {"pj": (0, 1, 2, 3, 4, 5), "s": (2, 3), "acc": (4, 5), "aux": (6,), "aux2": (7,)}
    BANKS_ATT = {"pj": (0, 1), "s": (0, 1, 2, 3, 6, 7), "acc": (4, 5), "aux": (6,), "aux2": (7,)}

    def bank(self, grp):
        banks = getattr(self, "bankmap", self.BANKS_DEFAULT)[grp]
        return self.pss[self.rot("bank_" + grp, banks)]

    def mm(self, out, lhsT, rhs, start=True, stop=True, tp=None):
        kw = dict(start=start, stop=stop)
        if tp is not None:
            kw["tile_position"] = tp
        self.P.op("pe", lambda e: e.matmul(out, lhsT, rhs, **kw), [lhsT, rhs], [out])
        t_ = self.P.cur_tag
        self.pe_slices[t_] = self.pe_slices.get(t_, 0) + (2 if lhsT.dtype == F32 else 1)

    def tr(self, out, in_, ident):
        self.P.op("pe", lambda e: e.transpose(out, in_, ident), [in_, ident], [out])
        t_ = self.P.cur_tag
        self.pe_slices[t_] = self.pe_slices.get(t_, 0) + 1

    def act(self, out, in_, func, bias=None, scale=None, accum=None):
        kw = {}
        rd = [in_]
        wr = [out]
        if bias is not None:
            kw["bias"] = bias
            if not isinstance(bias, (int, float)):
                rd.append(bias)
        if scale is not None:
            kw["scale"] = scale
            if not isinstance(scale, (int, float)):
                rd.append(scale)
        if accum is not None:
            kw["accum_out"] = accum
            wr.append(accum)
        self.P.op("act", lambda e: e.activation(out, in_, func, **kw), rd, wr)

    def tt(self, eng, out, in0, in1, op):
        self.P.op(eng, lambda e: e.tensor_tensor(out, in0, in1, op), [in0, in1], [out])

    def ts(self, eng, out, in0, s1, s2, op0, op1=None):
        rd = [in0]
        for s in (s1, s2):
            if s is not None and not isinstance(s, (int, float)):
                rd.append(s)
        if op1 is None:
            self.P.op(eng, lambda e: e.tensor_scalar(out, in0, s1, None, op0), rd, [out])
        else:
            self.P.op(eng, lambda e: e.tensor_scalar(out, in0, s1, s2, op0, op1), rd, [out])

    def stt(self, out, in0, scalar, in1, op0, op1):
        rd = [in0, in1]
        if not isinstance(scalar, (int, float)):
            rd.append(scalar)
        self.P.op("dve", lambda e: e.scalar_tensor_tensor(out, in0, scalar, in1, op0, op1), rd, [out])

    def cp(self, eng, out, in_):
        if eng == "act":
            self.act(out, in_, AF.Copy)
        else:
            self.P.op(eng, lambda e: e.tensor_copy(out, in_), [in_], [out])

    def ms(self, eng, out, val):
        self.P.op(eng, lambda e: e.memset(out, val), [], [out])

    def recip(self, out, in_):
        self.P.op("dve", lambda e: e.reciprocal(out, in_), [in_], [out])

    def ld(self, out, in_, sem, rk=()):
        self.P.dma("sp", out, in_, self.dsem(sem), reads=list(rk), writes=[out])

    def st(self, out, in_, sem, wk=()):
        self.P.dma("sp", out, in_, self.dsem(sem), reads=[in_], writes=list(wk))

    def rstd(self, out, ss, inv_n):
        self.act(out, ss, AF.Ln, bias=EPS, scale=inv_n)
        self.act(out, out, AF.Exp, scale=-0.5)

    def dump(self, name, src, n):
        if name not in self.dumps:
            return
        dst = self.nc.dram_tensor("dbg_" + name, [128, n], F32, kind="ExternalOutput").ap()
        pn = src.shape[0]
        p0 = src.base_partition()
        m = self.mark()
        tmp = self.alloc(512)
        for c0 in range(0, n, 512):
            w = min(512, n - c0)
            self.cp("dve", tmp[p0:p0 + pn, 0:w], src[:, c0:c0 + w])
            self.st(dst[p0:p0 + pn, c0:c0 + w], tmp[p0:p0 + pn, 0:w], "dbg", wk=[("dbg", name, c0)])
        self.release(m)
        self.dump_aps[name] = 1

    def load_w(self, dst, src, K, n, rows_last=128):
        per = max(1, self.wst_cap // n)
        k0 = 0
        while k0 < K:
            kg = min(per, K - k0)
            slot = self.rot("wst", (0, 1))
            stg = self.wst[slot]
            full = kg if not (k0 + kg == K and rows_last != 128) else kg - 1
            v = stg[:, 0:kg * n].rearrange("p (k n) -> p k n", k=kg)
            if full > 0:
                self.ld(v[:, 0:full, :], src[k0 * 128:(k0 + full) * 128, :].rearrange("(k p) n -> p k n", p=128),
                        "wst%d" % slot)
                self.cp("dve", dst[:, k0:k0 + full, :], v[:, 0:full, :])
            if full < kg:
                r0 = (k0 + full) * 128
                self.ld(v[0:rows_last, full, :], src[r0:r0 + rows_last, :], "wst%d" % slot)
                self.cp("dve", dst[0:rows_last, k0 + full, :], v[0:rows_last, full, :])
            k0 += kg

    def setup_consts(self):
        I = self.I
        cf = self.alloc(640)
        self.ld(cf, I["cF"], "cst")
        self.identF = cf[:, 0:128]
        self.triF = cf[:, 128:256]
        self.triB = cf[:, 256:384]
        self.nmF = cf[:, 384:512]
        self.nmB = cf[:, 512:640]
        self.onesF = self.alloc(128)
        self.ms("pool", self.onesF, 1.0)
        self.onesB = self.alloc(128, BF16)
        self.ms("pool", self.onesB, 1.0)
        self.identB = self.alloc(128, BF16)
        self.cp("pool", self.identB, self.identF)
        self.nmFb = self.alloc(128, BF16)
        self.nmBb = self.alloc(128, BF16)
        self.cp("pool", self.nmFb, self.nmF)
        self.cp("pool", self.nmBb, self.nmB)
        self.nmb4 = [self.alloc(512, BF16), self.alloc(512, BF16)]
        for d_, src_ in enumerate((self.nmF, self.nmB)):
            for h_ in range(4):
                self.cp("pool", self.nmb4[d_][:, h_ * 128:(h_ + 1) * 128], src_)
        self.PA = self.alloc(128, BF16)
        self.PD = self.alloc(128, BF16)
        self.BD = self.alloc(128, BF16)
        self.wmask = self.alloc(6 * 512, BF16)
        self.LV = self.alloc(128)
        self.scw = self.alloc(18)
        self.fcw = self.alloc(66)
        self.rows = self.alloc(24)
        self.arow = self.alloc(8)
        self.sinkexp = self.alloc(4)
        self.Drow = self.alloc(256)
        self.SC = self.alloc(24)
        self.modT = self.alloc(144)
        self.A1 = self.alloc(24)
        self.A2 = self.alloc(24)
        self.wst = None
        self.wst_cap = 4096
        self.mk = self.alloc(2)
        self.ms("pool", self.mk, 0.0)
        m = self.mark()
        stg = self.alloc(384 + 3072)
        self.ld(stg, I["cB"], "cst")
        self.cp("pool", self.PA, stg[:, 0:128])
        self.cp("pool", self.PD, stg[:, 128:256])
        self.cp("pool", self.BD, stg[:, 256:384])
        self.cp("pool", self.wmask, stg[:, 384:384 + 3072])
        cst = self.alloc(128)
        self.ms("dve", cst, 0.0)
        self.ld(cst[0:16, :], I["c"].rearrange("b (k p) -> (b k) p", p=128), "cst")
        self.ld(cst[16:24, :], I["c_ctx"].rearrange("(k p) -> k p", p=128), "cst")
        ps = self.bank("aux")
        self.tr(ps[:, 0:128], cst, self.identF)
        self.act(self.SC, ps[:, 0:24], AF.Silu)
        self.release(m)
        self.persist_top = self.top

    def layer_setup(self, l):
        I = self.I
        m = self.mark()
        stg = self.alloc(128)
        self.ms("dve", stg, 0.0)

        def rows(r0, src2d):
            n = src2d.shape[0]
            self.ld(stg[r0:r0 + n, 0:src2d.shape[1]], src2d, "cst")
        rows(0, I["norm1_w"][l].rearrange("(k p) -> k p", p=128))
        rows(8, I["norm2_w"][l].rearrange("(k p) -> k p", p=128))
        rows(16, I["b_mod"][l].rearrange("(k p) -> k p", p=128))
        rows(64, I["ssm_conv_b"][l].rearrange("(k p) -> k p", p=128))
        rows(70, I["ssm_norm_w"][l].rearrange("(k p) -> k p", p=128))
        rows(72, I["ffn_conv_b"][l].rearrange("(k p) -> k p", p=128))
        rows(94, I["mla_kv_norm"][l].rearrange("(k p) -> k p", p=128))
        aq = I["attn_q_norm"][l].rearrange("(k p) -> k p", p=64)
        ak = I["attn_k_norm"][l].rearrange("(k p) -> k p", p=64)
        self.ld(stg[95:96, 0:64], aq, "cst")
        self.ld(stg[95:96, 64:128], aq, "cst")
        self.ld(stg[96:97, 0:64], ak, "cst")
        self.ld(stg[96:97, 64:128], ak, "cst")
        mq = I["mla_q_norm"][l]
        self.ld(stg[97:98, :], mq[0:128].rearrange("(k p) -> k p", p=128), "cst")
        self.ld(stg[98:99, 0:64], mq[128:192].rearrange("(k p) -> k p", p=64), "cst")
        rows(99, I["final_norm_w"].rearrange("(k p) -> k p", p=128))
        ps = self.bank("aux")
        self.tr(ps[:, 0:128], stg, self.identF)
        self.cp("dve", self.LV, ps[:, 0:128])
        LV = self.LV
        self.n1w = LV[:, 0:8]
        self.n2w = LV[:, 8:16]
        self.bmodT = LV[:, 16:64]
        self.scb = LV[:, 64:70]
        self.snw = LV[:, 70:72]
        self.fcb = LV[:, 72:94]
        self.kvn = LV[:, 94:95]
        self.aqn = LV[:, 95:96]
        self.akn = LV[:, 96:97]
        self.mqn0 = LV[:, 97:98]
        self.mqn1 = LV[:, 98:99]
        self.fnw = LV[:, 99:107]
        self.ld(self.scw.rearrange("p (j k) -> p j k", k=3),
                I["ssm_conv_w"][l].rearrange("(j p) k -> p j k", p=128), "cst")
        self.ld(self.fcw.rearrange("p (j k) -> p j k", k=3),
                I["ffn_conv_w"][l].rearrange("(j p) k -> p j k", p=128), "cst")

        def bro(dst, src1d, n):
            self.ld(dst, src1d.rearrange("(o n) -> o n", o=1).to_broadcast([128, n]), "cst")
        bro(self.rows[:, 0:8], I["ssm_dt_bias"][l].rearrange("d h -> (d h)"), 8)
        bro(self.rows[:, 8:16], I["ssm_a_log"][l].rearrange("d h -> (d h)"), 8)
        bro(self.rows[:, 16:20], I["ssm_d"][l], 4)
        bro(self.rows[:, 20:24], I["win_sink"][l], 4)
        self.dtb = self.rows[:, 0:8]
        self.act(self.arow, self.rows[:, 8:16], AF.Exp)
        self.ts("dve", self.arow, self.arow, -1.0, None, ALU.mult)
        self.act(self.sinkexp, self.rows[:, 20:24], AF.Exp)
        for h in range(4):
            self.cp("dve", self.Drow[:, h * 64:(h + 1) * 64], self.rows[:, 16 + h:17 + h].to_broadcast([128, 64]))
        wm = [self.alloc(8192), self.alloc(8192)]
        wmb = [self.alloc(8192, BF16), self.alloc(8192, BF16)]
        SCb = self.alloc(24, BF16)
        self.cp("dve", SCb, self.SC)
        SCv = SCb.rearrange("p (r k) -> p k r", r=3)
        modv = self.modT.rearrange("p (j d r) -> p j d r", j=6, d=8)
        for j6 in range(6):
            w = wm[j6 % 2]
            wv = w.rearrange("p (k n) -> p k n", k=8)
            wb = wmb[j6 % 2].rearrange("p (k n) -> p k n", k=8)
            self.ld(wv, I["w_mod"][l][:, j6 * 1024:(j6 + 1) * 1024].rearrange("(k p) n -> p k n", p=128),
                    "wm%d" % (j6 % 2))
            self.cp("dve", wb[:, 0:3, :], wv[:, 0:3, :])
            self.cp("act", wb[:, 3:6, :], wv[:, 3:6, :])
            self.cp("pool", wb[:, 6:8, :], wv[:, 6:8, :])
            pm = self.bank("pj")
            for dc in range(8):
                for k_ in range(8):
                    self.mm(pm[:, dc * 3:dc * 3 + 3], wb[:, k_, dc * 128:(dc + 1) * 128], SCv[:, k_, :],
                            start=(k_ == 0), stop=(k_ == 7))
            self.tt("dve", modv[:, j6], pm[:, 0:24].rearrange("p (d r) -> p d r", r=3),
                    self.bmodT[:, j6 * 8:(j6 + 1) * 8].unsqueeze(2).to_broadcast([128, 8, 3]), ALU.add)
        A1v = self.A1.rearrange("p (r k) -> p r k", r=3)
        A2v = self.A2.rearrange("p (r k) -> p r k", r=3)
        for r in range(3):
            self.stt(A1v[:, r, :], modv[:, 1, :, r], 1.0, self.n1w, ALU.add, ALU.mult)
            self.stt(A2v[:, r, :], modv[:, 4, :, r], 1.0, self.n2w, ALU.add, ALU.mult)
        self.modv = modv
        self.A1v = A1v
        self.A2v = A2v
        self.release(m)

    def tb_keys(self, b, ti):
        t0, n = TBS[ti]
        return [("xt", b, i) for i in range(t0 // 128, (t0 + n) // 128)]

    def xt_view(self, b, t0, n):
        return self.xt[b][:, :, t0:t0 + n].rearrange("k p t -> p k t")

    def phase_x0(self):
        I = self.I
        m = self.mark()
        xin = [self.alloc(1024), self.alloc(1024)]
        xo = [self.alloc(1024), self.alloc(1024)]
        def src_of(g):
            b, i = divmod(g, NTILE)
            return I["x"][b, i * 128:(i + 1) * 128, :] if i < 16 else I["ctx"][b, (i - 16) * 128:(i - 15) * 128, :]
        self.ld(xin[0], src_of(0), "xin0")
        for b in range(2):
            for i in range(NTILE):
                g = b * NTILE + i
                s = g % 2
                if g + 1 < 2 * NTILE:
                    self.ld(xin[1 - s], src_of(g + 1), "xin%d" % (1 - s))
                p0 = self.bank("pj")
                p1 = self.bank("pj")
                for k in range(8):
                    pp = p0 if k < 4 else p1
                    self.tr(pp[:, (k % 4) * 128:(k % 4 + 1) * 128], xin[s][:, k * 128:(k + 1) * 128], self.identF)
                self.cp("act", xo[s][:, 0:512], p0[:, :])
                self.cp("dve", xo[s][:, 512:1024], p1[:, :])
                self.st(self.xt_view(b, i * 128, 128), xo[s].rearrange("p (k t) -> p k t", k=8),
                        "xo%d" % s, wk=[("xt", b, i)])
        self.release(m)

    def norm_mod_block(self, xb, n, t0, Av, shift_j, r, tmp):
        ss = self.bank("aux")
        for k in range(8):
            sq = self.rot("sqb", tmp["sq"])
            if k % 3 == 2:
                self.tt("pool", sq[:, 0:n], xb[:, k, 0:n], xb[:, k, 0:n], ALU.mult)
            else:
                self.act(sq[:, 0:n], xb[:, k, 0:n], AF.Square)
            self.mm(ss[:, 0:n], self.onesB, sq[:, 0:n], start=(k == 0), stop=(k == 7))
        rs = self.rot("rsb", tmp["rs"])
        self.rstd(rs[:, 0:n], ss[:, 0:n], 1.0 / D)
        for k in range(8):
            t = self.rot("nmt", tmp["t"])
            self.stt(t[:, 0:n], xb[:, k, 0:n], Av[:, r, k:k + 1], rs[:, 0:n], ALU.mult, ALU.mult)
            bcol = self.modv[:, shift_j, k, r:r + 1]
            if k % 4 == 3:
                self.ts("pool", self.hT[:, k, t0:t0 + n], t[:, 0:n], bcol, 1.0, ALU.add, ALU.mult)
            elif k % 4 == 1:
                self.ts("dve", self.hT[:, k, t0:t0 + n], t[:, 0:n], bcol, None, ALU.add)
            else:
                self.act(self.hT[:, k, t0:t0 + n], t[:, 0:n], AF.Identity, bias=bcol)

    def norm_tmp(self):
        return dict(sq=[self.alloc(512, BF16) for _ in range(3)], rs=[self.alloc(512) for _ in range(2)],
                    t=[self.alloc(512) for _ in range(3)])

    def p1(self, l, b):
        m = self.mark()
        xbs = [self.alloc(4096), self.alloc(4096)]
        tmp = self.norm_tmp()
        for ti, (t0, n) in enumerate(TBS):
            s = ti % 2
            xb = xbs[s].rearrange("p (k t) -> p k t", k=8)
            self.ld(xb[:, :, 0:n], self.xt_view(b, t0, n), "xb%d" % s, rk=self.tb_keys(b, ti))
            r = b if ti < 4 else 2
            self.norm_mod_block(xb, n, t0, self.A1v, 0, r, tmp)
        self.release(m)


    def proj_fm(self, out, w, c0, M, t0, n, tp=None):
        for k in range(8):
            self.mm(out, w[:, k, c0:c0 + M], self.hT[:, k, t0:t0 + n], start=(k == 0), stop=(k == 7), tp=tp)

    def proj_tm(self, out, w, c0, N, i):
        for k in range(8):
            self.mm(out, self.hT[:, k, i * 128:(i + 1) * 128], w[:, k, c0:c0 + N], start=(k == 0), stop=(k == 7))

    def rope_tmp(self):
        return dict(raw=[self.alloc(512, BF16) for _ in range(3)], sq=[self.alloc(512, BF16) for _ in range(3)],
                    rs=[self.alloc(512) for _ in range(3)], qn=[self.alloc(512, BF16) for _ in range(3)],
                    t1=[self.alloc(512) for _ in range(3)], t2=[self.alloc(512) for _ in range(3)])

    def norm_rope_gen(self, src, p0, p1, n, t0, dst, tmp, perm, cos, sin, gain=None, ones=None, nfeat=64):
        raw = self.rot("rp_raw", tmp["raw"])[p0:p1, 0:n]
        self.act(raw, src, AF.Copy)
        if gain is not None:
            sq = self.rot("rp_sq", tmp["sq"])[p0:p1, 0:n]
            self.act(sq, src, AF.Square)
        yield
        if gain is not None:
            ssp = self.bank("aux")[p0:p1, 0:n]
            self.mm(ssp, ones[p0:p1, p0:p1], sq)
            rs = self.rot("rp_rs", tmp["rs"])[p0:p1, 0:n]
            self.rstd(rs, ssp, 1.0 / nfeat)
            qn = self.rot("rp_qn", tmp["qn"])[p0:p1, 0:n]
            self.stt(qn, raw, gain[p0:p1, :], rs, ALU.mult, ALU.mult)
        else:
            qn = raw
        yield
        rotp = self.bank("aux2")[p0:p1, 0:n]
        self.mm(rotp, perm[p0:p1, p0:p1], qn)
        t1 = self.rot("rp_t1", tmp["t1"])[p0:p1, 0:n]
        t2 = self.rot("rp_t2", tmp["t2"])[p0:p1, 0:n]
        self.tt(self.rot("rp_eng", ("pool", "dve")), t1, qn, cos[p0:p1, t0:t0 + n], ALU.mult)
        self.tt("dve", t2, rotp, sin[p0:p1, t0:t0 + n], ALU.mult)
        if isinstance(dst, list):
            for (a0, a1, d) in dst:
                self.tt("pool", d, t1[a0 - p0:a1 - p0, :], t2[a0 - p0:a1 - p0, :], ALU.add)
        else:
            self.tt("pool", dst, t1, t2, ALU.add)

    def norm_rope(self, *a, **kw):
        for _ in self.norm_rope_gen(*a, **kw):
            pass

    def pipe_step(self):
        for g in list(self.active):
            try:
                next(g)
            except StopIteration:
                self.active.remove(g)

    def pipe_add(self, g):
        next(g)
        self.active.append(g)

    def pipe_drain(self):
        while self.active:
            self.pipe_step()

    def load_rope(self, name):
        cos = self.alloc(T)
        sin = self.alloc(T)
        self.ld(cos, self.I[name][0], "cst")
        self.ld(sin, self.I[name][1], "cst")
        return cos, sin

    def attend(self, qT, kT, K, pb, vaug, ychunk, ob, scale, window=False, sinkcol=None, pts=None, recs=None):
        so = 64 - ob
        for qi, (q0, n) in enumerate(TBS):
            if qi == 4:
                tiles = [(16, None), (17, None)]
            elif not window:
                tiles = [(j, None) for j in range(NTILE)]
            else:
                i0 = 4 * qi
                tiles = [(16, None), (17, None)] + [(j, j - i0 + 1) for j in range(max(0, i0 - 1), min(15, i0 + 4) + 1)]
            acc = self.bank("acc")
            sts = {}
            LA = 3
            for idx in range(len(tiles) + LA):
                if idx < len(tiles):
                    j = tiles[idx][0]
                    st = self.bank("s")
                    self.mm(st[:, 0:n], kT[pb:pb + K, j * 128:(j + 1) * 128], qT[pb:pb + K, q0:q0 + n])
                    sts[idx] = st
                if idx >= LA:
                    i2 = idx - LA
                    j, mi = tiles[i2]
                    st = sts.pop(i2)
                    pt = self.rot("pt", pts)
                    self.act(pt[:, 0:n], st[:, 0:n], AF.Exp, scale=scale)
                    if mi is not None:
                        self.tt("pool" if i2 % 2 == 0 else "dve", pt[:, 0:n], pt[:, 0:n],
                                self.wmask[:, mi * 512:mi * 512 + n], ALU.mult)
                    self.mm(acc[:, 0:n], vaug(j), pt[:, 0:n], start=(i2 == 0), stop=(i2 == len(tiles) - 1))
            rec = self.rot("rec", recs)
            if sinkcol is not None:
                self.ts("dve", rec[ob:ob + 64, 0:n], acc[so:so + 64, 0:n], sinkcol[so:so + 64, :], None, ALU.add)
                self.recip(rec[ob:ob + 64, 0:n], rec[ob:ob + 64, 0:n])
            else:
                self.recip(rec[ob:ob + 64, 0:n], acc[so:so + 64, 0:n])
            self.tt("dve", ychunk[ob:ob + 64, q0:q0 + n], acc[ob:ob + 64, 0:n], rec[ob:ob + 64, 0:n], ALU.mult)

    def gqa_mixer(self, l, b, c0, ych0, qgain, kgain, window, sink):
        I = self.I
        m = self.mark()
        w = self.alloc(8 * 512, BF16).rearrange("p (k n) -> p k n", k=8)
        cos, sin = self.load_rope("ropeA")
        qT = self.alloc(4 * T, BF16).rearrange("p (c t) -> p c t", c=4)
        kT = self.alloc(2 * T, BF16).rearrange("p (c t) -> p c t", c=2)
        wk2 = self.alloc(8 * 256, BF16).rearrange("p (k n) -> p k n", k=8)
        VW = 320
        V = self.alloc(NTILE * VW, BF16).rearrange("p (i v) -> p i v", i=NTILE)
        m2 = self.mark()
        self.wst = [self.alloc(4096), self.alloc(4096)]
        self.load_w(w, I["w_in"][l][:, c0:c0 + 512], 8, 512)
        self.release(m2)
        pts = [self.alloc(512, BF16) for _ in range(4)]
        recs = [self.alloc(512) for _ in range(2)]
        tmp = self.rope_tmp()
        for c in range(2):
            self.cp("pool", wk2[:, :, c * 128:c * 128 + 64], w[:, :, 256 + c * 64:320 + c * 64])
            self.cp("pool", wk2[:, :, c * 128 + 64:c * 128 + 128], w[:, :, 256 + c * 64:320 + c * 64])
        self.ms("pool", V[:, :, 0:64], 1.0)
        self.ms("pool", V[:, :, 128:192], 1.0)
        self.ms("pool", V[:, :, 256:320], 1.0)
        for h in range(4):
            o = 64 - (h % 2) * 64
            self.ms("pool", qT[o:o + 64, h, :], 0.0)
        self.bankmap = self.BANKS_PROJ
        self.active = []
        for ti, (t0, n) in enumerate(TBS):
            for ci in range(4):
                pr = self.bank("pj")
                self.proj_fm(pr[:, 0:n], w if ci < 2 else wk2, (ci % 2) * 128, 128, t0, n)
                if ci < 2:
                    dst = [(0, 64, qT[0:64, 2 * ci, t0:t0 + n]), (64, 128, qT[64:128, 2 * ci + 1, t0:t0 + n])]
                else:
                    dst = kT[:, ci - 2, t0:t0 + n]
                g = None
                if qgain is not None:
                    g = qgain if ci < 2 else kgain
                self.pipe_step()
                self.pipe_add(self.norm_rope_gen(pr[:, 0:n], 0, 128, n, t0, dst, tmp, self.PA, cos, sin, gain=g,
                                                 ones=self.BD, nfeat=64))
            pv = self.bank("pj")
            nt = n // 128
            for ii in range(nt):
                self.proj_tm(pv[:, ii * 128:(ii + 1) * 128], w, 384, 128, t0 // 128 + ii)
            pvv = pv[:, 0:nt * 128].rearrange("p (i c) -> p i c", c=128)
            i0 = t0 // 128
            self.pipe_step()
            self.act(V[:, i0:i0 + nt, 64:128], pvv[:, :, 0:64], AF.Copy)
            self.act(V[:, i0:i0 + nt, 192:256], pvv[:, :, 64:128], AF.Copy)
        self.pipe_drain()
        scale = 64 ** -0.5
        self.bankmap = self.BANKS_ATT
        for h in range(4):
            kv = h // 2
            ob = (h % 2) * 64
            voff = (64 if ob == 0 else 0) + kv * 128
            self.attend(qT[:, h, :], kT[:, kv, :], 128, 0, lambda j, vo=voff: V[:, j, vo:vo + 128],
                        self.Y[:, ych0 + h // 2, :], ob, scale, window=window,
                        sinkcol=(self.sinkexp[:, h:h + 1] if sink else None), pts=pts, recs=recs)
        self.bankmap = self.BANKS_DEFAULT
        self.release(m)

    def pA(self, l, b):
        self.gqa_mixer(l, b, 0, 0, self.aqn, self.akn, False, False)

    def pC(self, l, b):
        self.gqa_mixer(l, b, 1544, 4, None, None, True, True)


    def pD(self, l, b):
        I = self.I
        m = self.mark()
        wD = self.alloc(8 * 352, BF16).rearrange("p (k n) -> p k n", k=8)
        wuq = self.alloc(2 * 384, BF16).rearrange("p (k n) -> p k n", k=2)
        wukv = self.alloc(512, BF16).rearrange("p (k n) -> p k n", k=1)
        wv = self.alloc(256, BF16)
        cos, sin = self.load_rope("ropeD")
        qT = self.alloc(4 * T, BF16).rearrange("p (c t) -> p c t", c=4)
        kT = self.alloc(4 * T, BF16).rearrange("p (c t) -> p c t", c=4)
        VW = 384
        V = self.alloc(NTILE * VW, BF16).rearrange("p (i v) -> p i v", i=NTILE)
        m2 = self.mark()
        self.wst = [self.alloc(4096), self.alloc(4096)]
        self.load_w(wD, I["w_in"][l][:, 2056:2408], 8, 352)
        self.load_w(wuq, I["mla_w_uq"][l], 2, 384, rows_last=64)
        self.load_w(wukv, I["mla_w_ukv"][l], 1, 512)
        self.release(m2)
        for h in range(4):
            self.cp("pool", wv[:, h * 64:(h + 1) * 64], wukv[:, 0, h * 128 + 64:(h + 1) * 128])
        pts = [self.alloc(512, BF16) for _ in range(4)]
        recs = [self.alloc(512) for _ in range(2)]
        tmp = self.rope_tmp()
        cq0s = [self.alloc(512, BF16) for _ in range(2)]
        cq1s = [self.alloc(512, BF16) for _ in range(2)]
        ckvs = [self.alloc(512, BF16) for _ in range(2)]
        sqks = [self.alloc(512, BF16) for _ in range(2)]
        self.ms("pool", V[:, :, 64:128], 1.0)
        self.ms("pool", V[:, :, 256:320], 1.0)
        self.ms("pool", qT[64:128, :, :], 0.0)
        self.ms("pool", kT[64:128, :, :], 0.0)
        self.bankmap = self.BANKS_PROJ
        self.active = []
        voffs = (0, 128, 192, 320)
        for ti, (t0, n) in enumerate(TBS):
            p0 = self.bank("pj")
            self.proj_fm(p0[:, 0:n], wD, 0, 128, t0, n)
            p1 = self.bank("pj")
            self.proj_fm(p1[0:64, 0:n], wD, 128, 64, t0, n)
            p2 = self.bank("pj")
            self.proj_fm(p2[:, 0:n], wD, 192, 128, t0, n)
            p3 = self.bank("pj")
            self.proj_fm(p3[64:96, 0:n], wD, 320, 32, t0, n, tp=(0, 64))
            self.pipe_step()
            sq0 = self.rot("rp_sq", tmp["sq"])
            self.act(sq0[:, 0:n], p0[:, 0:n], AF.Square)
            sq1 = self.rot("rp_sq", tmp["sq"])
            self.act(sq1[0:64, 0:n], p1[0:64, 0:n], AF.Square)
            sqk = self.rot("dsqk", sqks)
            self.act(sqk[:, 0:n], p2[:, 0:n], AF.Square)
            ss = self.bank("aux")
            self.mm(ss[:, 0:n], self.onesB, sq0[:, 0:n], start=True, stop=False)
            self.mm(ss[:, 0:n], self.onesB[0:64, :], sq1[0:64, 0:n], start=False, stop=True)
            ssk = self.bank("aux2")
            self.mm(ssk[:, 0:n], self.onesB, sqk[:, 0:n])
            rs = self.rot("rp_rs", tmp["rs"])
            self.rstd(rs[:, 0:n], ss[:, 0:n], 1.0 / 192)
            rsk = self.rot("rp_rs", tmp["rs"])
            self.rstd(rsk[:, 0:n], ssk[:, 0:n], 1.0 / 128)
            cq0 = self.rot("cq0", cq0s)
            cq1 = self.rot("cq1", cq1s)
            self.stt(cq0[:, 0:n], p0[:, 0:n], self.mqn0, rs[:, 0:n], ALU.mult, ALU.mult)
            self.stt(cq1[0:64, 0:n], p1[0:64, 0:n], self.mqn1[0:64, :], rs[0:64, 0:n], ALU.mult, ALU.mult)
            ckv = self.rot("ckv", ckvs)
            self.stt(ckv[:, 0:n], p2[:, 0:n], self.kvn, rsk[:, 0:n], ALU.mult, ALU.mult)
            def krot_gen(p3=p3, t0=t0, n=n):
                yield from self.norm_rope_gen(p3[64:96, 0:n], 64, 96, n, t0, kT[64:96, 0, t0:t0 + n], tmp, self.PD,
                                              cos, sin)
                for h in range(1, 4):
                    self.cp("pool", kT[64:96, h, t0:t0 + n], kT[64:96, 0, t0:t0 + n])
            self.pipe_step()
            self.pipe_add(krot_gen())
            for h in range(4):
                pq = self.bank("pj")
                self.mm(pq[0:96, 0:n], wuq[:, 0, h * 96:(h + 1) * 96], cq0[:, 0:n], start=True, stop=False)
                self.mm(pq[0:96, 0:n], wuq[0:64, 1, h * 96:(h + 1) * 96], cq1[0:64, 0:n], start=False, stop=True)
                self.pipe_step()
                self.pipe_add(self.norm_rope_gen(pq[0:96, 0:n], 0, 96, n, t0, qT[0:96, h, t0:t0 + n], tmp, self.PD,
                                                 cos, sin))
            for h in range(4):
                pk = self.bank("pj")
                self.mm(pk[0:64, 0:n], wukv[:, 0, h * 128:h * 128 + 64], ckv[:, 0:n])
                self.cp("act", kT[0:64, h, t0:t0 + n], pk[0:64, 0:n])
            nt = n // 128
            i0 = t0 // 128
            for pr in range(nt // 2):
                pv = self.bank("pj")
                for ii in range(2):
                    tok = (pr * 2 + ii) * 128
                    self.mm(pv[:, ii * 256:(ii + 1) * 256], ckv[:, tok:tok + 128], wv)
                pvv = pv[:, :].rearrange("p (i c) -> p i c", c=256)
                for h in range(4):
                    self.cp("act" if h % 2 == 0 else "dve", V[:, i0 + pr * 2:i0 + pr * 2 + 2, voffs[h]:voffs[h] + 64],
                            pvv[:, :, h * 64:(h + 1) * 64])
        self.pipe_drain()
        scale = 96 ** -0.5
        self.bankmap = self.BANKS_ATT
        for h in range(4):
            ob = (h % 2) * 64
            vo = voffs[h] - ob
            self.attend(qT[:, h, :], kT[:, h, :], 128, 0, lambda j, vo=vo: V[:, j, vo:vo + 128],
                        self.Y[:, 6 + h // 2, :], ob, scale, pts=pts, recs=recs)
        self.bankmap = self.BANKS_DEFAULT
        self.release(m)

    def pB(self, l, b):
        I = self.I
        m = self.mark()
        BT = self.alloc(2 * T, BF16).rearrange("p (g t) -> p g t", g=2)
        CT = self.alloc(2 * T, BF16).rearrange("p (g t) -> p g t", g=2)
        xs_tok = self.alloc(NTILE * 256, BF16).rearrange("p (i c) -> p i c", i=NTILE)
        B_tok = self.alloc(NTILE * 256, BF16).rearrange("p (i c) -> p i c", i=NTILE)
        z_tok = self.alloc(NTILE * 256, BF16).rearrange("p (i c) -> p i c", i=NTILE)
        dt = self.alloc(144)
        da = self.alloc(144)
        cs = self.alloc(144)
        tot = self.alloc(144)
        ecs = self.alloc(144)
        dtw = self.alloc(144)
        cdec = self.alloc(144)
        ncs = self.alloc(144)
        v3 = lambda a: a.rearrange("p (i c) -> p i c", c=8)
        m2 = self.mark()
        wB = self.alloc(8 * 1032, BF16).rearrange("p (k n) -> p k n", k=8)
        m3 = self.mark()
        self.wst_cap = 2064
        self.wst = [self.alloc(2064), self.alloc(2064)]
        self.load_w(wB, I["w_in"][l][:, 512:1544], 8, 1032)
        self.wst_cap = 4096
        self.release(m3)
        Gpl = [self.alloc(GW), self.alloc(GW)]
        accl = [self.alloc(GW), self.alloc(GW)]
        xsT = self.alloc(2 * T, BF16).rearrange("p (c t) -> p c t", c=2)
        pdt = self.bank("aux")
        for i in range(NTILE):
            self.proj_tm(pdt[:, i * 8:(i + 1) * 8], wB, 1024, 8, i)
        xr = ecs
        self.tt("dve", v3(xr), v3(pdt[:, 0:144]), self.dtb.unsqueeze(1).to_broadcast([128, NTILE, 8]), ALU.add)
        self.ts("dve", cs, xr, -1.0, None, ALU.mult)
        self.tt("dve", cs, cs, xr, ALU.max)
        self.act(cs, cs, AF.Exp, scale=-1.0)
        self.act(cs, cs, AF.Ln, bias=1.0)
        self.ts("dve", xr, xr, 0.0, None, ALU.max)
        self.tt("dve", dt, xr, cs, ALU.add)
        self.tt("dve", v3(da), v3(dt), self.arow.unsqueeze(1).to_broadcast([128, NTILE, 8]), ALU.mult)
        if BSTOP == 1:
            self.top = m
            return
        for ip in range(NTILE // 2):
            pz = self.bank("pj")
            for ii in range(2):
                self.proj_tm(pz[:, ii * 256:(ii + 1) * 256], wB, 0, 256, ip * 2 + ii)
            self.act(z_tok[:, ip * 2:ip * 2 + 2, :], pz[:, :].rearrange("p (i c) -> p i c", c=256), AF.Silu)
        if BSTOP == 2:
            self.top = m
            return
        for Gp in Gpl:
            for c0 in (0, 2049, 2050, 2307):
                self.ms("pool", Gp[:, c0:c0 + 1], 0.0)
        self.bankmap = self.BANKS_PROJ
        for cidx in range(6):
            Gp = Gpl[cidx % 2]
            acc = accl[cidx % 2]
            for ti, (t0, n) in enumerate(TBS):
                pr = self.bank("pj")
                self.proj_fm(pr[:, 0:n], wB, 256 + cidx * 128, 128, t0, n)
                off = 1 + t0 if ti < 4 else 2051
                self.cp("dve" if ti % 2 else "act", Gp[:, off:off + n], pr[:, 0:n])
            W = 2306
            self.act(acc[:, 0:W], Gp[:, 1:1 + W], AF.Identity, bias=self.scb[:, cidx:cidx + 1],
                     scale=self.scw[:, cidx * 3 + 1:cidx * 3 + 2])
            self.stt(acc[:, 0:W], Gp[:, 0:W], self.scw[:, cidx * 3:cidx * 3 + 1], acc[:, 0:W], ALU.mult, ALU.add)
            self.stt(acc[:, 0:W], Gp[:, 2:2 + W], self.scw[:, cidx * 3 + 2:cidx * 3 + 3], acc[:, 0:W], ALU.mult, ALU.add)
            dstT = xsT[:, cidx, :] if cidx < 2 else (BT[:, cidx - 2, :] if cidx < 4 else CT[:, cidx - 4, :])
            self.act(dstT[:, 0:NLAT], acc[:, 0:NLAT], AF.Silu)
            self.act(dstT[:, NLAT:T], acc[:, 2050:2306], AF.Silu)
        self.bankmap = self.BANKS_DEFAULT
        if BSTOP == 3:
            self.top = m
            return
        for i in range(NTILE):
            ptr = self.bank("aux2").bitcast(BF16)
            for c in range(2):
                self.tr(ptr[:, c * 128:(c + 1) * 128], xsT[:, c, i * 128:(i + 1) * 128], self.identB)
                self.tr(ptr[:, 256 + c * 128:256 + (c + 1) * 128], BT[:, c, i * 128:(i + 1) * 128], self.identB)
            self.cp("act", xs_tok[:, i, :], ptr[:, 0:256])
            self.cp("dve", B_tok[:, i, :], ptr[:, 256:512])
        self.release(m2)
        if BSTOP == 4:
            self.top = m
            return
        ybuf = self.alloc(NTILE * 256).rearrange("p (i c) -> p i c", i=NTILE)
        H = [self.alloc(256), self.alloc(256)]
        Hb = [self.alloc(256, BF16), self.alloc(256, BF16)]
        dtr4 = [[self.alloc(512) for _ in range(2)] for _ in range(2)]
        decs = [self.alloc(128) for _ in range(8)]
        scs = [self.alloc(128, BF16) for _ in range(8)]
        tmpos = [self.alloc(256) for _ in range(4)]
        xsws = [self.alloc(256, BF16) for _ in range(4)]
        t1s = [self.alloc(256) for _ in range(2)]
        t2s = [self.alloc(256) for _ in range(2)]
        yns = [self.alloc(256, BF16) for _ in range(2)]
        junk = self.alloc(256, BF16)
        ssq = self.alloc(NTILE)
        pcs = self.bank("aux")
        ptot = self.bank("aux2")
        dav = v3(da)
        for j in range(NTILE):
            self.mm(pcs[:, j * 8:j * 8 + 4], self.triF, dav[:, j, 0:4])
            self.mm(pcs[:, j * 8 + 4:j * 8 + 8], self.triB, dav[:, j, 4:8])
            self.mm(ptot[:, j * 8:(j + 1) * 8], self.onesF, dav[:, j, :])
        self.cp("dve", cs, pcs[:, 0:144])
        self.cp("dve", tot, ptot[:, 0:144])
        self.act(ecs, cs, AF.Exp)
        self.tt("dve", dtw, tot, cs, ALU.subtract)
        self.act(dtw, dtw, AF.Exp)
        self.tt("dve", dtw, dtw, dt, ALU.mult)
        self.act(cdec, tot, AF.Exp)
        self.ts("dve", ncs, cs, -1.0, None, ALU.mult)
        if BSTOP == 5:
            self.top = m
            return
        self.ms("pool", ybuf.rearrange("p i c -> p (i c)"), 0.0)
        for d in range(2):
            self.ms("pool", H[d], 0.0)
            self.ms("pool", Hb[d], 0.0)
        forder = [16, 17] + list(range(16))
        border = [17, 16] + list(range(15, -1, -1))
        tri = (self.triF, self.triB)
        h3 = lambda a: a.rearrange("p (h c) -> p h c", h=4)
        hd = [(h, d) for h in range(4) for d in range(2)]
        pEb = (((self.pss[0], self.pss[1])), ((self.pss[5], self.pss[7])))
        pGb = self.pss[2]
        pSb = self.pss[3]
        pYb = self.pss[4]
        pOb = self.pss[6]

        def stage_a(i):
            p = i % 2
            for d in range(2):
                for h in range(4):
                    col = i * 8 + d * 4 + h
                    self.ts("pool", dtr4[p][d][:, h * 128:(h + 1) * 128], tri[d], da[:, col:col + 1], 1.0,
                            ALU.mult, ALU.mult)
                self.mm(pEb[p][d][:, :], self.onesF, dtr4[p][d], start=True, stop=False)
                self.mm(pEb[p][d][:, :], self.identB, self.nmb4[d], start=False, stop=True)

        def stage_g(i):
            for g in range(2):
                self.mm(pGb[:, g * 128:(g + 1) * 128], BT[:, g, i * 128:(i + 1) * 128], CT[:, g, i * 128:(i + 1) * 128])

        def stage_bc(i):
            p = i % 2
            for q_, (h, d) in enumerate(hd):
                col = i * 8 + d * 4 + h
                self.act(decs[q_], pEb[p][d][:, h * 128:(h + 1) * 128], AF.Exp, bias=ncs[:, col:col + 1])
            for q_, (h, d) in enumerate(hd):
                col = i * 8 + d * 4 + h
                g = h // 2
                self.stt(scs[q_], decs[q_], dt[:, col:col + 1], pGb[:, g * 128:(g + 1) * 128], ALU.mult, ALU.mult)

        def stage_d(i):
            for q_, (h, d) in enumerate(hd):
                self.mm(pYb[:, h * 64:(h + 1) * 64], scs[q_], xs_tok[:, i, h * 64:(h + 1) * 64],
                        start=(d == 0), stop=(d == 1))

        def scan(i):
            dj = ((0, forder[i]), (1, border[i]))
            xsw = {}
            for d, jj in dj:
                c4 = jj * 8 + d * 4
                xsw[d] = self.rot("xsw", xsws)
                self.tt("pool", h3(xsw[d]), h3(xs_tok[:, jj, :]), dtw[:, c4:c4 + 4].unsqueeze(2).to_broadcast([128, 4, 64]),
                        ALU.mult)
            for d, jj in dj:
                for h in range(4):
                    self.mm(pOb[:, d * 256 + h * 64:d * 256 + (h + 1) * 64], CT[:, h // 2, jj * 128:(jj + 1) * 128],
                            Hb[d][:, h * 64:(h + 1) * 64])
            for d, jj in dj:
                for h in range(4):
                    g = h // 2
                    self.mm(pSb[:, d * 256 + h * 64:d * 256 + (h + 1) * 64], B_tok[:, jj, g * 128:(g + 1) * 128],
                            xsw[d][:, h * 64:(h + 1) * 64])
            tm = {}
            for d, jj in dj:
                c4 = jj * 8 + d * 4
                tm[d] = self.rot("tmpo", tmpos)
                self.tt("dve", h3(tm[d]), h3(pOb[:, d * 256:(d + 1) * 256]),
                        ecs[:, c4:c4 + 4].unsqueeze(2).to_broadcast([128, 4, 64]), ALU.mult)
            for d, jj in dj:
                c4 = jj * 8 + d * 4
                self.tt("dve", h3(H[d]), h3(H[d]), cdec[:, c4:c4 + 4].unsqueeze(2).to_broadcast([128, 4, 64]), ALU.mult)
                self.tt("dve", H[d], H[d], pSb[:, d * 256:(d + 1) * 256], ALU.add)
                self.cp("act", Hb[d], H[d])
            self.tt("dve", ybuf[:, i, :], ybuf[:, i, :], pYb[:, 0:256], ALU.add)
            for d, jj in dj:
                self.tt("pool", ybuf[:, jj, :], ybuf[:, jj, :], tm[d], ALU.add)
        stage_a(0)
        stage_g(0)
        for i in range(NTILE):
            stage_bc(i)
            if i + 1 < NTILE:
                stage_a(i + 1)
            stage_d(i)
            if i + 1 < NTILE:
                stage_g(i + 1)
            scan(i)
        if BSTOP == 6:
            self.top = m
            return
        for j in range(NTILE):
            t1 = self.rot("bt1", t1s)
            self.tt("dve", t1, xs_tok[:, j, :], self.Drow, ALU.mult)
            self.tt("dve", ybuf[:, j, :], t1, ybuf[:, j, :], ALU.add)
            self.tt("pool", ybuf[:, j, :], ybuf[:, j, :], z_tok[:, j, :], ALU.mult)
            self.act(junk, ybuf[:, j, :], AF.Square, accum=ssq[:, j:j + 1])
        self.rstd(ssq, ssq, 1.0 / 256)
        for j in range(NTILE):
            yn = self.rot("byn", yns)
            self.ts("dve", yn, ybuf[:, j, :], ssq[:, j:j + 1], None, ALU.mult)
            ptr = self.bank("acc").bitcast(BF16)
            for c in range(2):
                self.tr(ptr[:, c * 128:(c + 1) * 128], yn[:, c * 128:(c + 1) * 128], self.identB)
            self.act(self.Y[:, 2, j * 128:(j + 1) * 128], ptr[:, 0:128], AF.Identity, scale=self.snw[:, 0:1])
            self.ts("dve", self.Y[:, 3, j * 128:(j + 1) * 128], ptr[:, 128:256], self.snw[:, 1:2], None, ALU.mult)
        self.release(m)

    def pO(self, l, b):
        I = self.I
        m = self.mark()
        wo = self.alloc(8 * 1024, BF16).rearrange("p (k n) -> p k n", k=8)
        m2 = self.mark()
        self.wst = [self.alloc(4096), self.alloc(4096)]
        self.load_w(wo, I["w_out"][l], 8, 1024)
        self.release(m2)
        xbs = [self.alloc(4096), self.alloc(4096)]
        tmp = self.norm_tmp()
        def ldx(ti):
            t0_, n_ = TBS[ti]
            v = xbs[ti % 2].rearrange("p (k t) -> p k t", k=8)
            self.ld(v[:, :, 0:n_], self.xt_view(b, t0_, n_), "xb%d" % (ti % 2), rk=self.tb_keys(b, ti))
        ldx(0)
        for ti, (t0, n) in enumerate(TBS):
            s = ti % 2
            xb = xbs[s].rearrange("p (k t) -> p k t", k=8)
            if ti + 1 < len(TBS):
                ldx(ti + 1)
            r = b if ti < 4 else 2
            for dc in range(8):
                po = self.bank("pj")
                for k in range(8):
                    self.mm(po[:, 0:n], wo[:, k, dc * 128:(dc + 1) * 128], self.Y[:, k, t0:t0 + n],
                            start=(k == 0), stop=(k == 7))
                self.stt(xb[:, dc, 0:n], po[:, 0:n], self.modv[:, 2, dc, r:r + 1], xb[:, dc, 0:n], ALU.mult, ALU.add)
            self.st(self.xt_view(b, t0, n), xb[:, :, 0:n], "xst%d" % s, wk=self.tb_keys(b, ti))
            self.norm_mod_block(xb, n, t0, self.A2v, 3, r, tmp)
        self.release(m)

    def pF(self, l, b):
        I = self.I
        self.release(self.y_mark)
        m = self.mark()
        wd = self.alloc(NFC * 1024, BF16).rearrange("p (k n) -> p k n", k=NFC)
        wd_mark = self.mark()
        wup = [self.alloc(8 * 256, BF16).rearrange("p (k n) -> p k n", k=8) for _ in range(2)]
        stgs = [self.alloc(2048).rearrange("p (k n) -> p k n", k=8) for _ in range(2)]
        dstg = [self.alloc(1024) for _ in range(2)]
        Gps = [self.alloc(GW) for _ in range(2)]
        accs = [self.alloc(GW) for _ in range(2)]
        abufs = [self.alloc(T, BF16) for _ in range(2)]
        sgs = [self.alloc(T, BF16) for _ in range(2)]
        us = [self.alloc(T, BF16) for _ in range(2)]
        for Gp in Gps:
            for c0 in (0, 2049, 2050, 2307):
                self.ms("pool", Gp[:, c0:c0 + 1], 0.0)
        wupd = I["ffn_w_up"][l]
        wdd = I["ffn_w_down"][l]
        W = 2306

        def ldw(fc_):
            s_ = fc_ % 2
            self.ld(stgs[s_][:, :, 0:128], wupd[:, fc_ * 128:(fc_ + 1) * 128].rearrange("(k p) n -> p k n", p=128),
                    "wup%d" % s_)
            self.ld(stgs[s_][:, :, 128:256],
                    wupd[:, DFF + fc_ * 128:DFF + (fc_ + 1) * 128].rearrange("(k p) n -> p k n", p=128), "wup%d" % s_)
            self.ld(dstg[s_], wdd[fc_ * 128:(fc_ + 1) * 128, :], "wdn%d" % s_)

        def conv_stage(fc_, stage):
            s_ = fc_ % 2
            Gp, acc, abuf, sg, u = Gps[s_], accs[s_], abufs[s_], sgs[s_], us[s_]
            if stage == 0:
                self.act(acc[:, 0:W], Gp[:, 1:1 + W], AF.Identity, bias=self.fcb[:, fc_:fc_ + 1],
                         scale=self.fcw[:, fc_ * 3 + 1:fc_ * 3 + 2])
            elif stage == 1:
                self.stt(acc[:, 0:W], Gp[:, 0:W], self.fcw[:, fc_ * 3:fc_ * 3 + 1], acc[:, 0:W], ALU.mult, ALU.add)
            elif stage == 2:
                self.stt(acc[:, 0:W], Gp[:, 2:2 + W], self.fcw[:, fc_ * 3 + 2:fc_ * 3 + 3], acc[:, 0:W],
                         ALU.mult, ALU.add)
            elif stage == 3:
                self.act(sg[:, 0:NLAT], acc[:, 0:NLAT], AF.Silu)
                self.act(sg[:, NLAT:T], acc[:, 2050:2306], AF.Silu)
            else:
                self.tt("dve", u, abuf, sg, ALU.mult)
                self.st(self.usc[fc_], u, "ust%d" % s_, wk=[("usc", fc_)])
        ldw(0)
        ldw(1)
        self.cp("pool", wup[0], stgs[0])
        self.cp("pool", wd[:, 0, :], dstg[0])
        for fc in range(NFC + 1):
            s = fc % 2
            for ti, (t0, n) in enumerate(TBS):
                if fc < NFC:
                    w = wup[s]
                    pa = self.bank("pj")
                    for k_ in range(8):
                        self.mm(pa[:, 0:n], w[:, k_, 0:128], self.hT[:, k_, t0:t0 + n], start=(k_ == 0), stop=(k_ == 7))
                    pg = self.bank("s")
                    for k_ in range(8):
                        self.mm(pg[:, 0:n], w[:, k_, 128:256], self.hT[:, k_, t0:t0 + n], start=(k_ == 0), stop=(k_ == 7))
                    off = 1 + t0 if ti < 4 else 2051
                    self.cp("act", abufs[s][:, t0:t0 + n], pa[:, 0:n])
                    self.cp("dve", Gps[s][:, off:off + n], pg[:, 0:n])
                if fc > 0:
                    conv_stage(fc - 1, ti)
                if ti == 1 and fc + 1 < NFC:
                    self.cp("pool", wup[1 - s], stgs[1 - s])
                    self.cp("pool", wd[:, fc + 1, :], dstg[1 - s])
                if ti == 2 and fc + 2 < NFC:
                    ldw(fc + 2)
        self.release(wd_mark)
        ubs = [self.alloc(NFC * 512, BF16).rearrange("p (f t) -> p f t", f=NFC) for _ in range(2)]
        xbs = [self.alloc(4096), self.alloc(4096)]
        last = (l == 1)
        if last:
            tmp = self.norm_tmp()
            ots = [self.alloc(1024), self.alloc(1024)]
        ukeys = [("usc", fc) for fc in range(NFC)]
        nblk = 4 if last else 5

        def ldd(ti_):
            t0_, n_ = TBS[ti_]
            s_ = ti_ % 2
            self.ld(ubs[s_][:, :, 0:n_], self.usc[:, :, t0_:t0_ + n_].rearrange("f p t -> p f t"), "ub%d" % s_, rk=ukeys)
            v = xbs[s_].rearrange("p (k t) -> p k t", k=8)
            self.ld(v[:, :, 0:n_], self.xt_view(b, t0_, n_), "xb%d" % s_, rk=self.tb_keys(b, ti_))
        for ti, (t0, n) in enumerate(TBS):
            if last and ti == 4:
                break
            s = ti % 2
            ub = ubs[s]
            xb = xbs[s].rearrange("p (k t) -> p k t", k=8)
            if ti == 0:
                ldd(0)
            if ti + 1 < nblk:
                ldd(ti + 1)
            r = b if ti < 4 else 2
            for dc in range(8):
                po = self.bank("pj")
                for fc in range(NFC):
                    self.mm(po[:, 0:n], wd[:, fc, dc * 128:(dc + 1) * 128], ub[:, fc, 0:n],
                            start=(fc == 0), stop=(fc == NFC - 1))
                self.stt(xb[:, dc, 0:n], po[:, 0:n], self.modv[:, 5, dc, r:r + 1], xb[:, dc, 0:n], ALU.mult, ALU.add)
            if not last:
                self.st(self.xt_view(b, t0, n), xb[:, :, 0:n], "xst%d" % s, wk=self.tb_keys(b, ti))
                continue
            ss = self.bank("aux")
            for k in range(8):
                sq = self.rot("sqb", tmp["sq"])
                self.act(sq[:, 0:n], xb[:, k, 0:n], AF.Square)
                self.mm(ss[:, 0:n], self.onesB, sq[:, 0:n], start=(k == 0), stop=(k == 7))
            rs = self.rot("rsb", tmp["rs"])
            self.rstd(rs[:, 0:n], ss[:, 0:n], 1.0 / D)
            for k in range(8):
                self.stt(xb[:, k, 0:n], xb[:, k, 0:n], self.fnw[:, k:k + 1], rs[:, 0:n], ALU.mult, ALU.mult)
            for tt_ in range(n // 128):
                ot = self.rot("ot", ots)
                p0 = self.bank("s")
                p1 = self.bank("acc")
                for k in range(8):
                    pp = p0 if k < 4 else p1
                    self.tr(pp[:, (k % 4) * 128:(k % 4 + 1) * 128], xb[:, k, tt_ * 128:(tt_ + 1) * 128], self.identF)
                self.cp("act", ot[:, 0:512], p0[:, :])
                self.cp("dve", ot[:, 512:1024], p1[:, :])
                tok = t0 + tt_ * 128
                self.st(self.out[b, tok:tok + 128, :], ot, "ost%d" % (self.rotc["ot"] % 2), wk=[("out", b, tok)])
                self.outkeys.append(("out", b, tok))
        self.release(m)

    def build(self, stop=None):
        self.outkeys = []
        self.P.cur_tag = "setup"
        self.setup_consts()
        self.P.cur_tag = "x0"
        self.phase_x0()
        done = False
        for l in range(2):
            self.P.cur_tag = ("lsetup", l)
            self.layer_setup(l)
            for b in range(2):
                self.lb_mark = self.mark()
                self.hT = self.alloc(8 * T, BF16).rearrange("p (k t) -> p k t", k=8)
                self.y_mark = self.mark()
                self.Y = self.alloc(8 * T, BF16).rearrange("p (k t) -> p k t", k=8)
                for pi, ph in enumerate(("p1", "pA", "pB", "pC", "pD", "pO", "pF")):
                    self.P.cur_tag = (l, b, ph)
                    if MARK:
                        self.act(self.mk, self.mk, AF.Abs)
                    getattr(self, ph)(l, b)
                    if stop == (l, b, ph):
                        done = True
                        break
                if done:
                    break
                self.release(self.lb_mark)
            if done:
                break
        if done:
            self.dump("hT", self.hT.rearrange("p k t -> p (k t)"), 8 * T)
            self.dump("Y", self.Y.rearrange("p k t -> p (k t)"), 8 * T)
        keys = list(self.outkeys) + [("dbg", n, c) for n in self.dump_aps for c in range(0, 8 * T, 512)]
        self.P.op("sp", lambda e: e.nop(), reads=keys, writes=[])


def build_program(stop=None, dumps=()):
    nc = bass.Bass("TRN2", target_bir_lowering=False)
    with ExitStack() as es:
        kb = KB(nc, es, dumps)
        kb.build(stop)
        kb.P.finalize(kb.sem)
        print("ops", len(kb.P.allops), "waits", kb.P.nwaits, "sems", kb.nsem, flush=True)
        with nc.Block() as block:
            kb.P.emit(block)
    return nc


def make_in_maps(inputs):
    cst = host_consts()
    maps = []
    for core in range(8):
        m = {}
        for name, shp in WSPEC:
            if name in cst:
                m[name] = cst[name]
            elif name in ("x", "c", "ctx"):
                m[name] = np.ascontiguousarray(np.asarray(inputs[name], dtype=np.float32)[2 * core:2 * core + 2])
            else:
                m[name] = np.ascontiguousarray(np.asarray(inputs[name], dtype=np.float32))
        maps.append(m)
    return maps


_NC_CACHE = {}


def kernel(**inputs):
    if "nc" not in _NC_CACHE:
        _NC_CACHE["nc"] = build_program()
    nc = _NC_CACHE["nc"]
    maps = make_in_maps(inputs)
    res = run_bass_kernel_spmd(nc, maps, core_ids=list(range(8)))
    return np.concatenate([np.asarray(r["out"]) for r in res.results], axis=0).astype(np.float32)
```

```python
import numpy as np
import concourse.bass as bass
import concourse.mybir as mybir
from concourse.bass_utils import run_bass_kernel_spmd

F32 = mybir.dt.float32
BF16 = mybir.dt.bfloat16
AF = mybir.ActivationFunctionType
ALU = mybir.AluOpType
AX = mybir.AxisListType

GRAN = 64
EPOCH = 30000
_ESZ = {F32: 4, BF16: 2}


def _esize(dt):
    return _ESZ.get(dt, 4)


class DmaSem:
    def __init__(self, handle):
        self.h = handle
        self.count = 0
        self.last_waited = 0


class Op:
    __slots__ = ("eng", "fn", "deps", "pdeps", "is_dma", "sem", "semval", "signal",
                 "sigdim", "sigval", "waits", "clock", "tag")

    def __init__(self, eng, fn):
        self.eng = eng
        self.fn = fn
        self.deps = {}
        self.pdeps = []
        self.tag = None
        self.is_dma = False
        self.sem = None
        self.semval = 0
        self.signal = False
        self.sigdim = None
        self.sigval = 0
        self.waits = None
        self.clock = None


class Prog:
    ENGS = ("pe", "act", "dve", "pool", "sp")

    def __init__(self, nc):
        self.nc = nc
        self.allops = []
        self.lastw = {}
        self.readers = {}
        self.pseudo = set()
        self.nops = 0
        self.cur_tag = None

    @staticmethod
    def keys_of(ap):
        t = ap.tensor
        pat = ap.ap
        es = _esize(ap.dtype)
        rowlen = pat[0][0]
        off = ap.offset
        start = off % rowlen if rowlen > 0 else off
        ext = 0
        for st, cnt in pat[1:]:
            ext += (cnt - 1) * abs(st)
        b0 = (start * es) // GRAN
        b1 = ((start + ext + 1) * es - 1) // GRAN
        name = t.name
        if name.startswith("ps"):
            return [(name, 0)]
        return [(name, b) for b in range(b0, b1 + 1)]

    def _collect(self, items):
        ks = []
        for it in items:
            if it is None:
                continue
            if isinstance(it, tuple):
                ks.append(it)
            else:
                ks.extend(self.keys_of(it))
        return ks

    def _record(self, o, reads, writes):
        rk = self._collect(reads)
        wk = self._collect(writes)
        deps = o.deps
        prk = [k for k in rk if k[0].startswith("ps") and k not in wk]
        for k in rk:
            w = self.lastw.get(k)
            if w is not None:
                deps[w] = "rar" if (k in self.pseudo and deps.get(w) != "raw") else "raw"
        for k in wk:
            w = self.lastw.get(k)
            if w is not None and w not in deps:
                deps[w] = "waw"
            for r in self.readers.get(k, ()):
                if r not in deps:
                    deps[r] = "war"
        for k in rk:
            self.readers.setdefault(k, []).append(o)
        for k in wk:
            self.lastw[k] = o
            self.readers[k] = []
            self.pseudo.discard(k)
        for k in prk:
            self.lastw[k] = o
            self.readers[k] = []
            self.pseudo.add(k)
        keep = {}
        for d, kind in deps.items():
            if d is o:
                continue
            if d.eng == o.eng and not d.is_dma and not o.is_dma:
                if o.eng == "pe":
                    continue
                if kind == "rar":
                    continue
            keep[d] = kind
        o.deps = keep
        o.tag = self.cur_tag
        for d in keep:
            if d.is_dma:
                sm = d.sem
                o.pdeps.append((sm, sm.count))
                if sm.count > sm.last_waited:
                    sm.last_waited = sm.count
        self.allops.append(o)

    def op(self, eng, fn, reads=(), writes=()):
        o = Op(eng, fn)
        self._record(o, reads, writes)
        return o

    def dma(self, eng, out, in_, sem, reads=None, writes=None, **kw):
        o = Op(eng, None)
        o.is_dma = True
        o.sem = sem
        if sem.last_waited > 0:
            o.pdeps.append((sem, sem.last_waited))
        o.fn = lambda e: e.dma_start(out=out, in_=in_, **kw)
        self._record(o, reads if reads is not None else [in_],
                     writes if writes is not None else [out])
        sem.count += 16
        o.semval = sem.count
        return o

    def finalize(self, sem_alloc):
        for o in self.allops:
            for d in o.deps:
                if not d.is_dma:
                    d.signal = True
        cnt = {e: 0 for e in self.ENGS}
        ep = {e: 0 for e in self.ENGS}
        self.engsem = {}
        for o in self.allops:
            if o.is_dma:
                o.sigdim = o.sem
                o.sigval = o.semval
                continue
            if o.signal:
                e = o.eng
                if cnt[e] >= EPOCH:
                    cnt[e] = 0
                    ep[e] += 1
                cnt[e] += 1
                dim = (e, ep[e])
                if dim not in self.engsem:
                    self.engsem[dim] = sem_alloc()
                o.sigdim = dim
                o.sigval = cnt[e]
        known = {e: {} for e in self.ENGS}
        nwaits = 0
        for o in self.allops:
            K = known[o.eng]
            need = {}
            for d in o.deps:
                if d.is_dma:
                    continue
                dim, val = d.sigdim, d.sigval
                if K.get(dim, 0) < val and need.get(dim, 0) < val:
                    need[dim] = val
            for sem, val in o.pdeps:
                if K.get(sem, 0) < val and need.get(sem, 0) < val:
                    need[sem] = val
            for d in o.deps:
                ck = d.clock
                for dim, val in ck.items():
                    if K.get(dim, 0) < val:
                        K[dim] = val
            for dim, val in need.items():
                if K.get(dim, 0) < val:
                    K[dim] = val
            o.waits = list(need.items())
            nwaits += len(o.waits)
            ck = dict(K)
            if o.is_dma or o.signal:
                ck[o.sigdim] = max(ck.get(o.sigdim, 0), o.sigval)
            o.clock = ck
        self.nwaits = nwaits

    def emit(self, block):
        per = {e: [] for e in self.ENGS}
        for o in self.allops:
            per[o.eng].append(o)
        engsem = self.engsem

        def run(eng_obj, ops):
            for o in ops:
                for dim, val in o.waits:
                    h = dim.h if isinstance(dim, DmaSem) else engsem[dim]
                    eng_obj.wait_ge(h, val)
                ins = o.fn(eng_obj)
                if o.is_dma:
                    ins.then_inc(o.sem.h, 16)
                elif o.signal:
                    ins.then_inc(engsem[o.sigdim], 1)

        @block.tensor
        def _(e):
            run(e, per["pe"])

        @block.scalar
        def _(e):
            run(e, per["act"])

        @block.vector
        def _(e):
            run(e, per["dve"])

        @block.gpsimd
        def _(e):
            run(e, per["pool"])

        @block.sync
        def _(e):
            run(e, per["sp"])
from contextlib import ExitStack

D = 1024
T = 2304
NLAT = 2048
NTILE = 18
TBS = [(0, 512), (512, 512), (1024, 512), (1536, 512), (2048, 256)]
DFF = 2816
NFC = 22
EPS = 1e-6
import os as _os
ARENA = int(_os.environ.get("KARENA", "53000"))
BSTOP = int(_os.environ.get("BSTOP", "0"))
ALIGN = int(_os.environ.get("KALIGN", "16"))
MARK = int(_os.environ.get("KMARK", "0"))
GW = 2310
NEG = -30000.0


def host_consts():
    f = np.float32
    idn = np.eye(128, dtype=f)
    k = np.arange(128)
    tri_f = (k[:, None] <= k[None, :]).astype(f)
    tri_b = (k[:, None] >= k[None, :]).astype(f)
    nm_f = np.where(k[None, :] >= k[:, None], 0.0, NEG).astype(f)
    nm_b = np.where(k[:, None] >= k[None, :], 0.0, NEG).astype(f)
    cF = np.concatenate([idn, tri_f, tri_b, nm_f, nm_b], axis=1)

    def perm(base, half):
        Pm = np.zeros((128, 128), f)
        for h0 in (base, base + half):
            q = half // 2
            for jj in range(q):
                Pm[h0 + jj + q, h0 + jj] = -1.0
                Pm[h0 + jj, h0 + q + jj] = 1.0
        return Pm
    PA = perm(0, 32) + perm(64, 32)
    PD = perm(64, 16)
    BD = np.zeros((128, 128), f)
    BD[0:64, 0:64] = 1.0
    BD[64:128, 64:128] = 1.0
    a = np.arange(128)[:, None]
    bq = np.arange(128)[None, :]
    masks = []
    for r in range(-1, 5):
        m = np.zeros((128, 512), f)
        for c in range(4):
            if r == c:
                m[:, c * 128:(c + 1) * 128] = 1.0
            elif r - c == -1:
                m[:, c * 128:(c + 1) * 128] = (bq <= a)
            elif r - c == 1:
                m[:, c * 128:(c + 1) * 128] = (a <= bq)
        masks.append(m)
    cB = np.concatenate([PA, PD, BD] + masks, axis=1)

    def tables(rot_dim):
        rows = NLAT // 64
        row = np.repeat(np.arange(rows, dtype=f), 64)
        col = np.tile(np.arange(64, dtype=f), rows)
        half = rot_dim // 2
        inv = np.power(f(10000.0), -np.arange(0, half, 2, dtype=f) / f(half)).astype(f)
        ar = (row[:, None] * inv[None, :]).astype(f)
        ac = (col[:, None] * inv[None, :]).astype(f)
        ang = np.concatenate([ar, ar, ac, ac], axis=-1)
        return np.cos(ang).astype(f), np.sin(ang).astype(f)
    cA, sA = tables(64)
    ropeA = np.zeros((2, 128, T), f)
    ropeA[0] = 1.0
    ropeA[0, 0:64, 0:NLAT] = cA.T
    ropeA[0, 64:128, 0:NLAT] = cA.T
    ropeA[1, 0:64, 0:NLAT] = sA.T
    ropeA[1, 64:128, 0:NLAT] = sA.T
    cD, sD = tables(32)
    ropeD = np.zeros((2, 128, T), f)
    ropeD[0] = 1.0
    ropeD[0, 64:96, 0:NLAT] = cD.T
    ropeD[1, 64:96, 0:NLAT] = sD.T
    return dict(cF=cF, cB=cB, ropeA=ropeA, ropeD=ropeD)


WSPEC = [
    ("x", [2, 2048, 1024]), ("c", [2, 1024]), ("ctx", [2, 256, 1024]), ("c_ctx", [1024]),
    ("norm1_w", [2, 1024]), ("w_mod", [2, 1024, 6144]), ("b_mod", [2, 6144]),
    ("w_in", [2, 1024, 2408]), ("attn_q_norm", [2, 64]), ("attn_k_norm", [2, 64]),
    ("ssm_conv_w", [2, 768, 3]), ("ssm_conv_b", [2, 768]), ("ssm_dt_bias", [2, 2, 4]),
    ("ssm_a_log", [2, 2, 4]), ("ssm_d", [2, 4]), ("ssm_norm_w", [2, 256]), ("win_sink", [2, 4]),
    ("mla_q_norm", [2, 192]), ("mla_w_uq", [2, 192, 384]), ("mla_kv_norm", [2, 128]),
    ("mla_w_ukv", [2, 128, 512]), ("w_out", [2, 1024, 1024]), ("norm2_w", [2, 1024]),
    ("ffn_w_up", [2, 1024, 5632]), ("ffn_conv_w", [2, 2816, 3]), ("ffn_conv_b", [2, 2816]),
    ("ffn_w_down", [2, 2816, 1024]), ("final_norm_w", [1024]),
    ("cF", [128, 640]), ("cB", [128, 384 + 6 * 512]), ("ropeA", [2, 128, T]), ("ropeD", [2, 128, T]),
]


class KB:
    def __init__(self, nc, es, dumps=()):
        self.nc = nc
        self.es = es
        self.P = Prog(nc)
        self.dumps = set(dumps)
        self.I = {}
        for name, shp in WSPEC:
            self.I[name] = nc.dram_tensor(name, shp, F32, kind="ExternalInput").ap()
        self.out = nc.dram_tensor("out", [2, 2048, 1024], F32, kind="ExternalOutput").ap()
        self.xt = nc.dram_tensor("xt", [2, 8, 128, T], F32, kind=("ExternalOutput" if "xt" in self.dumps else "Internal")).ap()
        self.usc = nc.dram_tensor("usc", [NFC, 128, T], BF16, kind="Internal").ap()
        self.arena = es.enter_context(nc.sbuf_tensor("arena", [128, ARENA], F32))
        self.pss = [es.enter_context(nc.psum_tensor("ps%d" % i, [128, 512], F32)) for i in range(8)]
        self.top = 0
        self.nsem = 0
        self.rotc = {}
        self.dsems = {}
        self.dump_aps = {}
        self.pe_slices = {}

    def sem(self):
        s = self.es.enter_context(self.nc.semaphore("s%d" % self.nsem))
        self.nsem += 1
        return s

    def dsem(self, name):
        if name not in self.dsems:
            self.dsems[name] = DmaSem(self.sem())
        return self.dsems[name]

    def alloc(self, n, dt=F32):
        es_ = _esize(dt)
        n4 = (n * es_ + 3) // 4
        off = (self.top + ALIGN - 1) // ALIGN * ALIGN
        self.top = off + n4
        assert self.top <= ARENA, ("arena overflow", self.top)
        a = self.arena[:, off:off + n4]
        if dt == F32:
            return a
        return a.bitcast(dt)[:, 0:n]

    def mark(self):
        return self.top

    def release(self, m):
        self.top = m

    def rot(self, name, items):
        i = self.rotc.get(name, 0)
        self.rotc[name] = i + 1
        return items[i % len(items)]

    BANKS_DEFAULT = {"pj": (0, 1), "s": (2, 3), "acc": (4, 5), "aux": (6,), "aux2": (7,)}
    BANKS_PROJ = {"pj": (0, 1, 2, 3, 4, 5), "s": (2, 3), "acc": (4, 5), "aux": (6,), "aux2": (7,)}
    BANKS_ATT = {"pj": (0, 1), "s": (0, 1, 2, 3, 6, 7), "acc": (4, 5), "aux": (6,), "aux2": (7,)}

    def bank(self, grp):
        banks = getattr(self, "bankmap", self.BANKS_DEFAULT)[grp]
        return self.pss[self.rot("bank_" + grp, banks)]

    def mm(self, out, lhsT, rhs, start=True, stop=True, tp=None):
        kw = dict(start=start, stop=stop)
        if tp is not None:
            kw["tile_position"] = tp
        self.P.op("pe", lambda e: e.matmul(out, lhsT, rhs, **kw), [lhsT, rhs], [out])
        t_ = self.P.cur_tag
        self.pe_slices[t_] = self.pe_slices.get(t_, 0) + (2 if lhsT.dtype == F32 else 1)

    def tr(self, out, in_, ident):
        self.P.op("pe", lambda e: e.transpose(out, in_, ident), [in_, ident], [out])
        t_ = self.P.cur_tag
        self.pe_slices[t_] = self.pe_slices.get(t_, 0) + 1

    def act(self, out, in_, func, bias=None, scale=None, accum=None):
        kw = {}
        rd = [in_]
        wr = [out]
        if bias is not None:
            kw["bias"] = bias
            if not isinstance(bias, (int, float)):
                rd.append(bias)
        if scale is not None:
            kw["scale"] = scale
            if not isinstance(scale, (int, float)):
                rd.append(scale)
        if accum is not None:
            kw["accum_out"] = accum
            wr.append(accum)
        self.P.op("act", lambda e: e.activation(out, in_, func, **kw), rd, wr)

    def tt(self, eng, out, in0, in1, op):
        self.P.op(eng, lambda e: e.tensor_tensor(out, in0, in1, op), [in0, in1], [out])

    def ts(self, eng, out, in0, s1, s2, op0, op1=None):
        rd = [in0]
        for s in (s1, s2):
            if s is not None and not isinstance(s, (int, float)):
                rd.append(s)
        if op1 is None:
            self.P.op(eng, lambda e: e.tensor_scalar(out, in0, s1, None, op0), rd, [out])
        else:
            self.P.op(eng, lambda e: e.tensor_scalar(out, in0, s1, s2, op0, op1), rd, [out])

    def stt(self, out, in0, scalar, in1, op0, op1):
        rd = [in0, in1]
        if not isinstance(scalar, (int, float)):
            rd.append(scalar)
        self.P.op("dve", lambda e: e.scalar_tensor_tensor(out, in0, scalar, in1, op0, op1), rd, [out])

    def cp(self, eng, out, in_):
        if eng == "act":
            self.act(out, in_, AF.Copy)
        else:
            self.P.op(eng, lambda e: e.tensor_copy(out, in_), [in_], [out])

    def ms(self, eng, out, val):
        self.P.op(eng, lambda e: e.memset(out, val), [], [out])

    def recip(self, out, in_):
        self.P.op("dve", lambda e: e.reciprocal(out, in_), [in_], [out])

    def ld(self, out, in_, sem, rk=()):
        self.P.dma("sp", out, in_, self.dsem(sem), reads=list(rk), writes=[out])

    def st(self, out, in_, sem, wk=()):
        self.P.dma("sp", out, in_, self.dsem(sem), reads=[in_], writes=list(wk))

    def rstd(self, out, ss, inv_n):
        self.act(out, ss, AF.Ln, bias=EPS, scale=inv_n)
        self.act(out, out, AF.Exp, scale=-0.5)

    def dump(self, name, src, n):
        if name not in self.dumps:
            return
        dst = self.nc.dram_tensor("dbg_" + name, [128, n], F32, kind="ExternalOutput").ap()
        pn = src.shape[0]
        p0 = src.base_partition()
        m = self.mark()
        tmp = self.alloc(512)
        for c0 in range(0, n, 512):
            w = min(512, n - c0)
            self.cp("dve", tmp[p0:p0 + pn, 0:w], src[:, c0:c0 + w])
            self.st(dst[p0:p0 + pn, c0:c0 + w], tmp[p0:p0 + pn, 0:w], "dbg", wk=[("dbg", name, c0)])
        self.release(m)
        self.dump_aps[name] = 1

    def load_w(self, dst, src, K, n, rows_last=128):
        per = max(1, self.wst_cap // n)
        k0 = 0
        while k0 < K:
            kg = min(per, K - k0)
            slot = self.rot("wst", (0, 1))
            stg = self.wst[slot]
            full = kg if not (k0 + kg == K and rows_last != 128) else kg - 1
            v = stg[:, 0:kg * n].rearrange("p (k n) -> p k n", k=kg)
            if full > 0:
                self.ld(v[:, 0:full, :], src[k0 * 128:(k0 + full) * 128, :].rearrange("(k p) n -> p k n", p=128),
                        "wst%d" % slot)
                self.cp("dve", dst[:, k0:k0 + full, :], v[:, 0:full, :])
            if full < kg:
                r0 = (k0 + full) * 128
                self.ld(v[0:rows_last, full, :], src[r0:r0 + rows_last, :], "wst%d" % slot)
                self.cp("dve", dst[0:rows_last, k0 + full, :], v[0:rows_last, full, :])
            k0 += kg

    def setup_consts(self):
        I = self.I
        cf = self.alloc(640)
        self.ld(cf, I["cF"], "cst")
        self.identF = cf[:, 0:128]
        self.triF = cf[:, 128:256]
        self.triB = cf[:, 256:384]
        self.nmF = cf[:, 384:512]
        self.nmB = cf[:, 512:640]
        self.onesF = self.alloc(128)
        self.ms("pool", self.onesF, 1.0)
        self.onesB = self.alloc(128, BF16)
        self.ms("pool", self.onesB, 1.0)
        self.identB = self.alloc(128, BF16)
        self.cp("pool", self.identB, self.identF)
        self.nmFb = self.alloc(128, BF16)
        self.nmBb = self.alloc(128, BF16)
        self.cp("pool", self.nmFb, self.nmF)
        self.cp("pool", self.nmBb, self.nmB)
        self.nmb4 = [self.alloc(512, BF16), self.alloc(512, BF16)]
        for d_, src_ in enumerate((self.nmF, self.nmB)):
            for h_ in range(4):
                self.cp("pool", self.nmb4[d_][:, h_ * 128:(h_ + 1) * 128], src_)
        self.PA = self.alloc(128, BF16)
        self.PD = self.alloc(128, BF16)
        self.BD = self.alloc(128, BF16)
        self.wmask = self.alloc(6 * 512, BF16)
        self.LV = self.alloc(128)
        self.scw = self.alloc(18)
        self.fcw = self.alloc(66)
        self.rows = self.alloc(24)
        self.arow = self.alloc(8)
        self.sinkexp = self.alloc(4)
        self.Drow = self.alloc(256)
        self.SC = self.alloc(24)
        self.modT = self.alloc(144)
        self.A1 = self.alloc(24)
        self.A2 = self.alloc(24)
        self.wst = None
        self.wst_cap = 4096
        self.mk = self.alloc(2)
        self.ms("pool", self.mk, 0.0)
        m = self.mark()
        stg = self.alloc(384 + 3072)
        self.ld(stg, I["cB"], "cst")
        self.cp("pool", self.PA, stg[:, 0:128])
        self.cp("pool", self.PD, stg[:, 128:256])
        self.cp("pool", self.BD, stg[:, 256:384])
        self.cp("pool", self.wmask, stg[:, 384:384 + 3072])
        cst = self.alloc(128)
        self.ms("dve", cst, 0.0)
        self.ld(cst[0:16, :], I["c"].rearrange("b (k p) -> (b k) p", p=128), "cst")
        self.ld(cst[16:24, :], I["c_ctx"].rearrange("(k p) -> k p", p=128), "cst")
        ps = self.bank("aux")
        self.tr(ps[:, 0:128], cst, self.identF)
        self.act(self.SC, ps[:, 0:24], AF.Silu)
        self.release(m)
        self.persist_top = self.top

    def layer_setup(self, l):
        I = self.I
        m = self.mark()
        stg = self.alloc(128)
        self.ms("dve", stg, 0.0)

        def rows(r0, src2d):
            n = src2d.shape[0]
            self.ld(stg[r0:r0 + n, 0:src2d.shape[1]], src2d, "cst")
        rows(0, I["norm1_w"][l].rearrange("(k p) -> k p", p=128))
        rows(8, I["norm2_w"][l].rearrange("(k p) -> k p", p=128))
        rows(16, I["b_mod"][l].rearrange("(k p) -> k p", p=128))
        rows(64, I["ssm_conv_b"][l].rearrange("(k p) -> k p", p=128))
        rows(70, I["ssm_norm_w"][l].rearrange("(k p) -> k p", p=128))
        rows(72, I["ffn_conv_b"][l].rearrange("(k p) -> k p", p=128))
        rows(94, I["mla_kv_norm"][l].rearrange("(k p) -> k p", p=128))
        aq = I["attn_q_norm"][l].rearrange("(k p) -> k p", p=64)
        ak = I["attn_k_norm"][l].rearrange("(k p) -> k p", p=64)
        self.ld(stg[95:96, 0:64], aq, "cst")
        self.ld(stg[95:96, 64:128], aq, "cst")
        self.ld(stg[96:97, 0:64], ak, "cst")
        self.ld(stg[96:97, 64:128], ak, "cst")
        mq = I["mla_q_norm"][l]
        self.ld(stg[97:98, :], mq[0:128].rearrange("(k p) -> k p", p=128), "cst")
        self.ld(stg[98:99, 0:64], mq[128:192].rearrange("(k p) -> k p", p=64), "cst")
        rows(99, I["final_norm_w"].rearrange("(k p) -> k p", p=128))
        ps = self.bank("aux")
        self.tr(ps[:, 0:128], stg, self.identF)
        self.cp("dve", self.LV, ps[:, 0:128])
        LV = self.LV
        self.n1w = LV[:, 0:8]
        self.n2w = LV[:, 8:16]
        self.bmodT = LV[:, 16:64]
        self.scb = LV[:, 64:70]
        self.snw = LV[:, 70:72]
        self.fcb = LV[:, 72:94]
        self.kvn = LV[:, 94:95]
        self.aqn = LV[:, 95:96]
        self.akn = LV[:, 96:97]
        self.mqn0 = LV[:, 97:98]
        self.mqn1 = LV[:, 98:99]
        self.fnw = LV[:, 99:107]
        self.ld(self.scw.rearrange("p (j k) -> p j k", k=3),
                I["ssm_conv_w"][l].rearrange("(j p) k -> p j k", p=128), "cst")
        self.ld(self.fcw.rearrange("p (j k) -> p j k", k=3),
                I["ffn_conv_w"][l].rearrange("(j p) k -> p j k", p=128), "cst")

        def bro(dst, src1d, n):
            self.ld(dst, src1d.rearrange("(o n) -> o n", o=1).to_broadcast([128, n]), "cst")
        bro(self.rows[:, 0:8], I["ssm_dt_bias"][l].rearrange("d h -> (d h)"), 8)
        bro(self.rows[:, 8:16], I["ssm_a_log"][l].rearrange("d h -> (d h)"), 8)
        bro(self.rows[:, 16:20], I["ssm_d"][l], 4)
        bro(self.rows[:, 20:24], I["win_sink"][l], 4)
        self.dtb = self.rows[:, 0:8]
        self.act(self.arow, self.rows[:, 8:16], AF.Exp)
        self.ts("dve", self.arow, self.arow, -1.0, None, ALU.mult)
        self.act(self.sinkexp, self.rows[:, 20:24], AF.Exp)
        for h in range(4):
            self.cp("dve", self.Drow[:, h * 64:(h + 1) * 64], self.rows[:, 16 + h:17 + h].to_broadcast([128, 64]))
        wm = [self.alloc(8192), self.alloc(8192)]
        wmb = [self.alloc(8192, BF16), self.alloc(8192, BF16)]
        SCb = self.alloc(24, BF16)
        self.cp("dve", SCb, self.SC)
        SCv = SCb.rearrange("p (r k) -> p k r", r=3)
        modv = self.modT.rearrange("p (j d r) -> p j d r", j=6, d=8)
        for j6 in range(6):
            w = wm[j6 % 2]
            wv = w.rearrange("p (k n) -> p k n", k=8)
            wb = wmb[j6 % 2].rearrange("p (k n) -> p k n", k=8)
            self.ld(wv, I["w_mod"][l][:, j6 * 1024:(j6 + 1) * 1024].rearrange("(k p) n -> p k n", p=128),
                    "wm%d" % (j6 % 2))
            self.cp("dve", wb[:, 0:3, :], wv[:, 0:3, :])
            self.cp("act", wb[:, 3:6, :], wv[:, 3:6, :])
            self.cp("pool", wb[:, 6:8, :], wv[:, 6:8, :])
            pm = self.bank("pj")
            for dc in range(8):
                for k_ in range(8):
                    self.mm(pm[:, dc * 3:dc * 3 + 3], wb[:, k_, dc * 128:(dc + 1) * 128], SCv[:, k_, :],
                            start=(k_ == 0), stop=(k_ == 7))
            self.tt("dve", modv[:, j6], pm[:, 0:24].rearrange("p (d r) -> p d r", r=3),
                    self.bmodT[:, j6 * 8:(j6 + 1) * 8].unsqueeze(2).to_broadcast([128, 8, 3]), ALU.add)
        A1v = self.A1.rearrange("p (r k) -> p r k", r=3)
        A2v = self.A2.rearrange("p (r k) -> p r k", r=3)
        for r in range(3):
            self.stt(A1v[:, r, :], modv[:, 1, :, r], 1.0, self.n1w, ALU.add, ALU.mult)
            self.stt(A2v[:, r, :], modv[:, 4, :, r], 1.0, self.n2w, ALU.add, ALU.mult)
        self.modv = modv
        self.A1v = A1v
        self.A2v = A2v
        self.release(m)

    def tb_keys(self, b, ti):
        t0, n = TBS[ti]
        return [("xt", b, i) for i in range(t0 // 128, (t0 + n) // 128)]

    def xt_view(self, b, t0, n):
        return self.xt[b][:, :, t0:t0 + n].rearrange("k p t -> p k t")

    def phase_x0(self):
        I = self.I
        m = self.mark()
        xin = [self.alloc(1024), self.alloc(1024)]
        xo = [self.alloc(1024), self.alloc(1024)]
        def src_of(g):
            b, i = divmod(g, NTILE)
            return I["x"][b, i * 128:(i + 1) * 128, :] if i < 16 else I["ctx"][b, (i - 16) * 128:(i - 15) * 128, :]
        self.ld(xin[0], src_of(0), "xin0")
        for b in range(2):
            for i in range(NTILE):
                g = b * NTILE + i
                s = g % 2
                if g + 1 < 2 * NTILE:
                    self.ld(xin[1 - s], src_of(g + 1), "xin%d" % (1 - s))
                p0 = self.bank("pj")
                p1 = self.bank("pj")
                for k in range(8):
                    pp = p0 if k < 4 else p1
                    self.tr(pp[:, (k % 4) * 128:(k % 4 + 1) * 128], xin[s][:, k * 128:(k + 1) * 128], self.identF)
                self.cp("act", xo[s][:, 0:512], p0[:, :])
                self.cp("dve", xo[s][:, 512:1024], p1[:, :])
                self.st(self.xt_view(b, i * 128, 128), xo[s].rearrange("p (k t) -> p k t", k=8),
                        "xo%d" % s, wk=[("xt", b, i)])
        self.release(m)

    def norm_mod_block(self, xb, n, t0, Av, shift_j, r, tmp):
        ss = self.bank("aux")
        for k in range(8):
            sq = self.rot("sqb", tmp["sq"])
            if k % 3 == 2:
                self.tt("pool", sq[:, 0:n], xb[:, k, 0:n], xb[:, k, 0:n], ALU.mult)
            else:
                self.act(sq[:, 0:n], xb[:, k, 0:n], AF.Square)
            self.mm(ss[:, 0:n], self.onesB, sq[:, 0:n], start=(k == 0), stop=(k == 7))
        rs = self.rot("rsb", tmp["rs"])
        self.rstd(rs[:, 0:n], ss[:, 0:n], 1.0 / D)
        for k in range(8):
            t = self.rot("nmt", tmp["t"])
            self.stt(t[:, 0:n], xb[:, k, 0:n], Av[:, r, k:k + 1], rs[:, 0:n], ALU.mult, ALU.mult)
            bcol = self.modv[:, shift_j, k, r:r + 1]
            if k % 4 == 3:
                self.ts("pool", self.hT[:, k, t0:t0 + n], t[:, 0:n], bcol, 1.0, ALU.add, ALU.mult)
            elif k % 4 == 1:
                self.ts("dve", self.hT[:, k, t0:t0 + n], t[:, 0:n], bcol, None, ALU.add)
            else:
                self.act(self.hT[:, k, t0:t0 + n], t[:, 0:n], AF.Identity, bias=bcol)

    def norm_tmp(self):
        return dict(sq=[self.alloc(512, BF16) for _ in range(3)], rs=[self.alloc(512) for _ in range(2)],
                    t=[self.alloc(512) for _ in range(3)])

    def p1(self, l, b):
        m = self.mark()
        xbs = [self.alloc(4096), self.alloc(4096)]
        tmp = self.norm_tmp()
        for ti, (t0, n) in enumerate(TBS):
            s = ti % 2
            xb = xbs[s].rearrange("p (k t) -> p k t", k=8)
            self.ld(xb[:, :, 0:n], self.xt_view(b, t0, n), "xb%d" % s, rk=self.tb_keys(b, ti))
            r = b if ti < 4 else 2
            self.norm_mod_block(xb, n, t0, self.A1v, 0, r, tmp)
        self.release(m)


    def proj_fm(self, out, w, c0, M, t0, n, tp=None):
        for k in range(8):
            self.mm(out, w[:, k, c0:c0 + M], self.hT[:, k, t0:t0 + n], start=(k == 0), stop=(k == 7), tp=tp)

    def proj_tm(self, out, w, c0, N, i):
        for k in range(8):
            self.mm(out, self.hT[:, k, i * 128:(i + 1) * 128], w[:, k, c0:c0 + N], start=(k == 0), stop=(k == 7))

    def rope_tmp(self):
        return dict(raw=[self.alloc(512, BF16) for _ in range(3)], sq=[self.alloc(512, BF16) for _ in range(3)],
                    rs=[self.alloc(512) for _ in range(3)], qn=[self.alloc(512, BF16) for _ in range(3)],
                    t1=[self.alloc(512) for _ in range(3)], t2=[self.alloc(512) for _ in range(3)])

    def norm_rope_gen(self, src, p0, p1, n, t0, dst, tmp, perm, cos, sin, gain=None, ones=None, nfeat=64):
        raw = self.rot("rp_raw", tmp["raw"])[p0:p1, 0:n]
        self.act(raw, src, AF.Copy)
        if gain is not None:
            sq = self.rot("rp_sq", tmp["sq"])[p0:p1, 0:n]
            self.act(sq, src, AF.Square)
        yield
        if gain is not None:
            ssp = self.bank("aux")[p0:p1, 0:n]
            self.mm(ssp, ones[p0:p1, p0:p1], sq)
            rs = self.rot("rp_rs", tmp["rs"])[p0:p1, 0:n]
            self.rstd(rs, ssp, 1.0 / nfeat)
            qn = self.rot("rp_qn", tmp["qn"])[p0:p1, 0:n]
            self.stt(qn, raw, gain[p0:p1, :], rs, ALU.mult, ALU.mult)
        else:
            qn = raw
        yield
        rotp = self.bank("aux2")[p0:p1, 0:n]
        self.mm(rotp, perm[p0:p1, p0:p1], qn)
        t1 = self.rot("rp_t1", tmp["t1"])[p0:p1, 0:n]
        t2 = self.rot("rp_t2", tmp["t2"])[p0:p1, 0:n]
        self.tt(self.rot("rp_eng", ("pool", "dve")), t1, qn, cos[p0:p1, t0:t0 + n], ALU.mult)
        self.tt("dve", t2, rotp, sin[p0:p1, t0:t0 + n], ALU.mult)
        if isinstance(dst, list):
            for (a0, a1, d) in dst:
                self.tt("pool", d, t1[a0 - p0:a1 - p0, :], t2[a0 - p0:a1 - p0, :], ALU.add)
        else:
            self.tt("pool", dst, t1, t2, ALU.add)

    def norm_rope(self, *a, **kw):
        for _ in self.norm_rope_gen(*a, **kw):
            pass

    def pipe_step(self):
        for g in list(self.active):
            try:
                next(g)
            except StopIteration:
                self.active.remove(g)

    def pipe_add(self, g):
        next(g)
        self.active.append(g)

    def pipe_drain(self):
        while self.active:
            self.pipe_step()

    def load_rope(self, name):
        cos = self.alloc(T)
        sin = self.alloc(T)
        self.ld(cos, self.I[name][0], "cst")
        self.ld(sin, self.I[name][1], "cst")
        return cos, sin

    def attend(self, qT, kT, K, pb, vaug, ychunk, ob, scale, window=False, sinkcol=None, pts=None, recs=None):
        so = 64 - ob
        for qi, (q0, n) in enumerate(TBS):
            if qi == 4:
                tiles = [(16, None), (17, None)]
            elif not window:
                tiles = [(j, None) for j in range(NTILE)]
            else:
                i0 = 4 * qi
                tiles = [(16, None), (17, None)] + [(j, j - i0 + 1) for j in range(max(0, i0 - 1), min(15, i0 + 4) + 1)]
            acc = self.bank("acc")
            sts = {}
            LA = 3
            for idx in range(len(tiles) + LA):
                if idx < len(tiles):
                    j = tiles[idx][0]
                    st = self.bank("s")
                    self.mm(st[:, 0:n], kT[pb:pb + K, j * 128:(j + 1) * 128], qT[pb:pb + K, q0:q0 + n])
                    sts[idx] = st
                if idx >= LA:
                    i2 = idx - LA
                    j, mi = tiles[i2]
                    st = sts.pop(i2)
                    pt = self.rot("pt", pts)
                    self.act(pt[:, 0:n], st[:, 0:n], AF.Exp, scale=scale)
                    if mi is not None:
                        self.tt("pool" if i2 % 2 == 0 else "dve", pt[:, 0:n], pt[:, 0:n],
                                self.wmask[:, mi * 512:mi * 512 + n], ALU.mult)
                    self.mm(acc[:, 0:n], vaug(j), pt[:, 0:n], start=(i2 == 0), stop=(i2 == len(tiles) - 1))
            rec = self.rot("rec", recs)
            if sinkcol is not None:
                self.ts("dve", rec[ob:ob + 64, 0:n], acc[so:so + 64, 0:n], sinkcol[so:so + 64, :], None, ALU.add)
                self.recip(rec[ob:ob + 64, 0:n], rec[ob:ob + 64, 0:n])
            else:
                self.recip(rec[ob:ob + 64, 0:n], acc[so:so + 64, 0:n])
            self.tt("dve", ychunk[ob:ob + 64, q0:q0 + n], acc[ob:ob + 64, 0:n], rec[ob:ob + 64, 0:n], ALU.mult)

    def gqa_mixer(self, l, b, c0, ych0, qgain, kgain, window, sink):
        I = self.I
        m = self.mark()
        w = self.alloc(8 * 512, BF16).rearrange("p (k n) -> p k n", k=8)
        cos, sin = self.load_rope("ropeA")
        qT = self.alloc(4 * T, BF16).rearrange("p (c t) -> p c t", c=4)
        kT = self.alloc(2 * T, BF16).rearrange("p (c t) -> p c t", c=2)
        wk2 = self.alloc(8 * 256, BF16).rearrange("p (k n) -> p k n", k=8)
        VW = 320
        V = self.alloc(NTILE * VW, BF16).rearrange("p (i v) -> p i v", i=NTILE)
        m2 = self.mark()
        self.wst = [self.alloc(4096), self.alloc(4096)]
        self.load_w(w, I["w_in"][l][:, c0:c0 + 512], 8, 512)
        self.release(m2)
        pts = [self.alloc(512, BF16) for _ in range(4)]
        recs = [self.alloc(512) for _ in range(2)]
        tmp = self.rope_tmp()
        for c in range(2):
            self.cp("pool", wk2[:, :, c * 128:c * 128 + 64], w[:, :, 256 + c * 64:320 + c * 64])
            self.cp("pool", wk2[:, :, c * 128 + 64:c * 128 + 128], w[:, :, 256 + c * 64:320 + c * 64])
        self.ms("pool", V[:, :, 0:64], 1.0)
        self.ms("pool", V[:, :, 128:192], 1.0)
        self.ms("pool", V[:, :, 256:320], 1.0)
        for h in range(4):
            o = 64 - (h % 2) * 64
            self.ms("pool", qT[o:o + 64, h, :], 0.0)
        self.bankmap = self.BANKS_PROJ
        self.active = []
        for ti, (t0, n) in enumerate(TBS):
            for ci in range(4):
                pr = self.bank("pj")
                self.proj_fm(pr[:, 0:n], w if ci < 2 else wk2, (ci % 2) * 128, 128, t0, n)
                if ci < 2:
                    dst = [(0, 64, qT[0:64, 2 * ci, t0:t0 + n]), (64, 128, qT[64:128, 2 * ci + 1, t0:t0 + n])]
                else:
                    dst = kT[:, ci - 2, t0:t0 + n]
                g = None
                if qgain is not None:
                    g = qgain if ci < 2 else kgain
                self.pipe_step()
                self.pipe_add(self.norm_rope_gen(pr[:, 0:n], 0, 128, n, t0, dst, tmp, self.PA, cos, sin, gain=g,
                                                 ones=self.BD, nfeat=64))
            pv = self.bank("pj")
            nt = n // 128
            for ii in range(nt):
                self.proj_tm(pv[:, ii * 128:(ii + 1) * 128], w, 384, 128, t0 // 128 + ii)
            pvv = pv[:, 0:nt * 128].rearrange("p (i c) -> p i c", c=128)
            i0 = t0 // 128
            self.pipe_step()
            self.act(V[:, i0:i0 + nt, 64:128], pvv[:, :, 0:64], AF.Copy)
            self.act(V[:, i0:i0 + nt, 192:256], pvv[:, :, 64:128], AF.Copy)
        self.pipe_drain()
        scale = 64 ** -0.5
        self.bankmap = self.BANKS_ATT
        for h in range(4):
            kv = h // 2
            ob = (h % 2) * 64
            voff = (64 if ob == 0 else 0) + kv * 128
            self.attend(qT[:, h, :], kT[:, kv, :], 128, 0, lambda j, vo=voff: V[:, j, vo:vo + 128],
                        self.Y[:, ych0 + h // 2, :], ob, scale, window=window,
                        sinkcol=(self.sinkexp[:, h:h + 1] if sink else None), pts=pts, recs=recs)
        self.bankmap = self.BANKS_DEFAULT
        self.release(m)

    def pA(self, l, b):
        self.gqa_mixer(l, b, 0, 0, self.aqn, self.akn, False, False)

    def pC(self, l, b):
        self.gqa_mixer(l, b, 1544, 4, None, None, True, True)


    def pD(self, l, b):
        I = self.I
        m = self.mark()
        wD = self.alloc(8 * 352, BF16).rearrange("p (k n) -> p k n", k=8)
        wuq = self.alloc(2 * 384, BF16).rearrange("p (k n) -> p k n", k=2)
        wukv = self.alloc(512, BF16).rearrange("p (k n) -> p k n", k=1)
        wv = self.alloc(256, BF16)
        cos, sin = self.load_rope("ropeD")
        qT = self.alloc(4 * T, BF16).rearrange("p (c t) -> p c t", c=4)
        kT = self.alloc(4 * T, BF16).rearrange("p (c t) -> p c t", c=4)
        VW = 384
        V = self.alloc(NTILE * VW, BF16).rearrange("p (i v) -> p i v", i=NTILE)
        m2 = self.mark()
        self.wst = [self.alloc(4096), self.alloc(4096)]
        self.load_w(wD, I["w_in"][l][:, 2056:2408], 8, 352)
        self.load_w(wuq, I["mla_w_uq"][l], 2, 384, rows_last=64)
        self.load_w(wukv, I["mla_w_ukv"][l], 1, 512)
        self.release(m2)
        for h in range(4):
            self.cp("pool", wv[:, h * 64:(h + 1) * 64], wukv[:, 0, h * 128 + 64:(h + 1) * 128])
        pts = [self.alloc(512, BF16) for _ in range(4)]
        recs = [self.alloc(512) for _ in range(2)]
        tmp = self.rope_tmp()
        cq0s = [self.alloc(512, BF16) for _ in range(2)]
        cq1s = [self.alloc(512, BF16) for _ in range(2)]
        ckvs = [self.alloc(512, BF16) for _ in range(2)]
        sqks = [self.alloc(512, BF16) for _ in range(2)]
        self.ms("pool", V[:, :, 64:128], 1.0)
        self.ms("pool", V[:, :, 256:320], 1.0)
        self.ms("pool", qT[64:128, :, :], 0.0)
        self.ms("pool", kT[64:128, :, :], 0.0)
        self.bankmap = self.BANKS_PROJ
        self.active = []
        voffs = (0, 128, 192, 320)
        for ti, (t0, n) in enumerate(TBS):
            p0 = self.bank("pj")
            self.proj_fm(p0[:, 0:n], wD, 0, 128, t0, n)
            p1 = self.bank("pj")
            self.proj_fm(p1[0:64, 0:n], wD, 128, 64, t0, n)
            p2 = self.bank("pj")
            self.proj_fm(p2[:, 0:n], wD, 192, 128, t0, n)
            p3 = self.bank("pj")
            self.proj_fm(p3[64:96, 0:n], wD, 320, 32, t0, n, tp=(0, 64))
            self.pipe_step()
            sq0 = self.rot("rp_sq", tmp["sq"])
            self.act(sq0[:, 0:n], p0[:, 0:n], AF.Square)
            sq1 = self.rot("rp_sq", tmp["sq"])
            self.act(sq1[0:64, 0:n], p1[0:64, 0:n], AF.Square)
            sqk = self.rot("dsqk", sqks)
            self.act(sqk[:, 0:n], p2[:, 0:n], AF.Square)
            ss = self.bank("aux")
            self.mm(ss[:, 0:n], self.onesB, sq0[:, 0:n], start=True, stop=False)
            self.mm(ss[:, 0:n], self.onesB[0:64, :], sq1[0:64, 0:n], start=False, stop=True)
            ssk = self.bank("aux2")
            self.mm(ssk[:, 0:n], self.onesB, sqk[:, 0:n])
            rs = self.rot("rp_rs", tmp["rs"])
            self.rstd(rs[:, 0:n], ss[:, 0:n], 1.0 / 192)
            rsk = self.rot("rp_rs", tmp["rs"])
            self.rstd(rsk[:, 0:n], ssk[:, 0:n], 1.0 / 128)
            cq0 = self.rot("cq0", cq0s)
            cq1 = self.rot("cq1", cq1s)
            self.stt(cq0[:, 0:n], p0[:, 0:n], self.mqn0, rs[:, 0:n], ALU.mult, ALU.mult)
            self.stt(cq1[0:64, 0:n], p1[0:64, 0:n], self.mqn1[0:64, :], rs[0:64, 0:n], ALU.mult, ALU.mult)
            ckv = self.rot("ckv", ckvs)
            self.stt(ckv[:, 0:n], p2[:, 0:n], self.kvn, rsk[:, 0:n], ALU.mult, ALU.mult)
            def krot_gen(p3=p3, t0=t0, n=n):
                yield from self.norm_rope_gen(p3[64:96, 0:n], 64, 96, n, t0, kT[64:96, 0, t0:t0 + n], tmp, self.PD,
                                              cos, sin)
                for h in range(1, 4):
                    self.cp("pool", kT[64:96, h, t0:t0 + n], kT[64:96, 0, t0:t0 + n])
            self.pipe_step()
            self.pipe_add(krot_gen())
            for h in range(4):
                pq = self.bank("pj")
                self.mm(pq[0:96, 0:n], wuq[:, 0, h * 96:(h + 1) * 96], cq0[:, 0:n], start=True, stop=False)
                self.mm(pq[0:96, 0:n], wuq[0:64, 1, h * 96:(h + 1) * 96], cq1[0:64, 0:n], start=False, stop=True)
                self.pipe_step()
                self.pipe_add(self.norm_rope_gen(pq[0:96, 0:n], 0, 96, n, t0, qT[0:96, h, t0:t0 + n], tmp, self.PD,
                                                 cos, sin))
            for h in range(4):
                pk = self.bank("pj")
                self.mm(pk[0:64, 0:n], wukv[:, 0, h * 128:h * 128 + 64], ckv[:, 0:n])
                self.cp("act", kT[0:64, h, t0:t0 + n], pk[0:64, 0:n])
            nt = n // 128
            i0 = t0 // 128
            for pr in range(nt // 2):
                pv = self.bank("pj")
                for ii in range(2):
                    tok = (pr * 2 + ii) * 128
                    self.mm(pv[:, ii * 256:(ii + 1) * 256], ckv[:, tok:tok + 128], wv)
                pvv = pv[:, :].rearrange("p (i c) -> p i c", c=256)
                for h in range(4):
                    self.cp("act" if h % 2 == 0 else "dve", V[:, i0 + pr * 2:i0 + pr * 2 + 2, voffs[h]:voffs[h] + 64],
                            pvv[:, :, h * 64:(h + 1) * 64])
        self.pipe_drain()
        scale = 96 ** -0.5
        self.bankmap = self.BANKS_ATT
        for h in range(4):
            ob = (h % 2) * 64
            vo = voffs[h] - ob
            self.attend(qT[:, h, :], kT[:, h, :], 128, 0, lambda j, vo=vo: V[:, j, vo:vo + 128],
                        self.Y[:, 6 + h // 2, :], ob, scale, pts=pts, recs=recs)
        self.bankmap = self.BANKS_DEFAULT
        self.release(m)

    def pB(self, l, b):
        I = self.I
        m = self.mark()
        BT = self.alloc(2 * T, BF16).rearrange("p (g t) -> p g t", g=2)
        CT = self.alloc(2 * T, BF16).rearrange("p (g t) -> p g t", g=2)
        xs_tok = self.alloc(NTILE * 256, BF16).rearrange("p (i c) -> p i c", i=NTILE)
        B_tok = self.alloc(NTILE * 256, BF16).rearrange("p (i c) -> p i c", i=NTILE)
        z_tok = self.alloc(NTILE * 256, BF16).rearrange("p (i c) -> p i c", i=NTILE)
        dt = self.alloc(144)
        da = self.alloc(144)
        cs = self.alloc(144)
        tot = self.alloc(144)
        ecs = self.alloc(144)
        dtw = self.alloc(144)
        cdec = self.alloc(144)
        ncs = self.alloc(144)
        v3 = lambda a: a.rearrange("p (i c) -> p i c", c=8)
        m2 = self.mark()
        wB = self.alloc(8 * 1032, BF16).rearrange("p (k n) -> p k n", k=8)
        m3 = self.mark()
        self.wst_cap = 2064
        self.wst = [self.alloc(2064), self.alloc(2064)]
        self.load_w(wB, I["w_in"][l][:, 512:1544], 8, 1032)
        self.wst_cap = 4096
        self.release(m3)
        Gpl = [self.alloc(GW), self.alloc(GW)]
        accl = [self.alloc(GW), self.alloc(GW)]
        xsT = self.alloc(2 * T, BF16).rearrange("p (c t) -> p c t", c=2)
        pdt = self.bank("aux")
        for i in range(NTILE):
            self.proj_tm(pdt[:, i * 8:(i + 1) * 8], wB, 1024, 8, i)
        xr = ecs
        self.tt("dve", v3(xr), v3(pdt[:, 0:144]), self.dtb.unsqueeze(1).to_broadcast([128, NTILE, 8]), ALU.add)
        self.ts("dve", cs, xr, -1.0, None, ALU.mult)
        self.tt("dve", cs, cs, xr, ALU.max)
        self.act(cs, cs, AF.Exp, scale=-1.0)
        self.act(cs, cs, AF.Ln, bias=1.0)
        self.ts("dve", xr, xr, 0.0, None, ALU.max)
        self.tt("dve", dt, xr, cs, ALU.add)
        self.tt("dve", v3(da), v3(dt), self.arow.unsqueeze(1).to_broadcast([128, NTILE, 8]), ALU.mult)
        if BSTOP == 1:
            self.top = m
            return
        for ip in range(NTILE // 2):
            pz = self.bank("pj")
            for ii in range(2):
                self.proj_tm(pz[:, ii * 256:(ii + 1) * 256], wB, 0, 256, ip * 2 + ii)
            self.act(z_tok[:, ip * 2:ip * 2 + 2, :], pz[:, :].rearrange("p (i c) -> p i c", c=256), AF.Silu)
        if BSTOP == 2:
            self.top = m
            return
        for Gp in Gpl:
            for c0 in (0, 2049, 2050, 2307):
                self.ms("pool", Gp[:, c0:c0 + 1], 0.0)
        self.bankmap = self.BANKS_PROJ
        W = 2306

        def silu_out(cidx):
            acc = accl[cidx % 2]
            dstT = xsT[:, cidx, :] if cidx < 2 else (BT[:, cidx - 2, :] if cidx < 4 else CT[:, cidx - 4, :])
            self.act(dstT[:, 0:NLAT], acc[:, 0:NLAT], AF.Silu)
            self.act(dstT[:, NLAT:T], acc[:, 2050:2306], AF.Silu)
        for cidx in range(7):
            if cidx < 6:
                Gp = Gpl[cidx % 2]
                acc = accl[cidx % 2]
                for ti, (t0, n) in enumerate(TBS):
                    pr = self.bank("pj")
                    self.proj_fm(pr[:, 0:n], wB, 256 + cidx * 128, 128, t0, n)
                    off = 1 + t0 if ti < 4 else 2051
                    self.cp("dve" if ti % 2 else "act", Gp[:, off:off + n], pr[:, 0:n])
                self.act(acc[:, 0:W], Gp[:, 1:1 + W], AF.Identity, bias=self.scb[:, cidx:cidx + 1],
                         scale=self.scw[:, cidx * 3 + 1:cidx * 3 + 2])
                self.stt(acc[:, 0:W], Gp[:, 0:W], self.scw[:, cidx * 3:cidx * 3 + 1], acc[:, 0:W], ALU.mult, ALU.add)
                self.stt(acc[:, 0:W], Gp[:, 2:2 + W], self.scw[:, cidx * 3 + 2:cidx * 3 + 3], acc[:, 0:W],
                         ALU.mult, ALU.add)
            if cidx > 0:
                silu_out(cidx - 1)
        self.bankmap = self.BANKS_DEFAULT
        if BSTOP == 3:
            self.top = m
            return
        for i in range(NTILE):
            ptr = self.bank("aux2").bitcast(BF16)
            for c in range(2):
                self.tr(ptr[:, c * 128:(c + 1) * 128], xsT[:, c, i * 128:(i + 1) * 128], self.identB)
                self.tr(ptr[:, 256 + c * 128:256 + (c + 1) * 128], BT[:, c, i * 128:(i + 1) * 128], self.identB)
            self.cp("act", xs_tok[:, i, :], ptr[:, 0:256])
            self.cp("dve", B_tok[:, i, :], ptr[:, 256:512])
        self.release(m2)
        if BSTOP == 4:
            self.top = m
            return
        ybuf = self.alloc(NTILE * 256).rearrange("p (i c) -> p i c", i=NTILE)
        H = [self.alloc(256), self.alloc(256)]
        Hb = [self.alloc(256, BF16), self.alloc(256, BF16)]
        dtr4 = [[self.alloc(512) for _ in range(2)] for _ in range(2)]
        decs = [self.alloc(128) for _ in range(8)]
        scs = [self.alloc(128, BF16) for _ in range(8)]
        tmpos = [self.alloc(256) for _ in range(4)]
        xsws = [self.alloc(256, BF16) for _ in range(4)]
        t1s = [self.alloc(256) for _ in range(2)]
        t2s = [self.alloc(256) for _ in range(2)]
        yns = [self.alloc(256, BF16) for _ in range(2)]
        junk = self.alloc(256, BF16)
        ssq = self.alloc(NTILE)
        pcs = self.bank("aux")
        ptot = self.bank("aux2")
        dav = v3(da)
        for j in range(NTILE):
            self.mm(pcs[:, j * 8:j * 8 + 4], self.triF, dav[:, j, 0:4])
            self.mm(pcs[:, j * 8 + 4:j * 8 + 8], self.triB, dav[:, j, 4:8])
            self.mm(ptot[:, j * 8:(j + 1) * 8], self.onesF, dav[:, j, :])
        self.cp("dve", cs, pcs[:, 0:144])
        self.cp("dve", tot, ptot[:, 0:144])
        self.act(ecs, cs, AF.Exp)
        self.tt("dve", dtw, tot, cs, ALU.subtract)
        self.act(dtw, dtw, AF.Exp)
        self.tt("dve", dtw, dtw, dt, ALU.mult)
        self.act(cdec, tot, AF.Exp)
        self.ts("dve", ncs, cs, -1.0, None, ALU.mult)
        if BSTOP == 5:
            self.top = m
            return
        self.ms("pool", ybuf.rearrange("p i c -> p (i c)"), 0.0)
        for d in range(2):
            self.ms("pool", H[d], 0.0)
            self.ms("pool", Hb[d], 0.0)
        forder = [16, 17] + list(range(16))
        border = [17, 16] + list(range(15, -1, -1))
        tri = (self.triF, self.triB)
        h3 = lambda a: a.rearrange("p (h c) -> p h c", h=4)
        hd = [(h, d) for h in range(4) for d in range(2)]
        pEb = (((self.pss[0], self.pss[1])), ((self.pss[5], self.pss[7])))
        pGb = self.pss[2]
        pSb = self.pss[3]
        pYb = self.pss[4]
        pOb = self.pss[6]

        def stage_a(i):
            p = i % 2
            for d in range(2):
                for h in range(4):
                    col = i * 8 + d * 4 + h
                    self.ts("pool", dtr4[p][d][:, h * 128:(h + 1) * 128], tri[d], da[:, col:col + 1], 1.0,
                            ALU.mult, ALU.mult)
                self.mm(pEb[p][d][:, :], self.onesF, dtr4[p][d], start=True, stop=False)
                self.mm(pEb[p][d][:, :], self.identB, self.nmb4[d], start=False, stop=True)

        def stage_g(i):
            for g in range(2):
                self.mm(pGb[:, g * 128:(g + 1) * 128], BT[:, g, i * 128:(i + 1) * 128], CT[:, g, i * 128:(i + 1) * 128])

        def stage_bc(i):
            p = i % 2
            for q_, (h, d) in enumerate(hd):
                col = i * 8 + d * 4 + h
                self.act(decs[q_], pEb[p][d][:, h * 128:(h + 1) * 128], AF.Exp, bias=ncs[:, col:col + 1])
            for q_, (h, d) in enumerate(hd):
                col = i * 8 + d * 4 + h
                g = h // 2
                self.stt(scs[q_], decs[q_], dt[:, col:col + 1], pGb[:, g * 128:(g + 1) * 128], ALU.mult, ALU.mult)

        def stage_d(i):
            for q_, (h, d) in enumerate(hd):
                self.mm(pYb[:, h * 64:(h + 1) * 64], scs[q_], xs_tok[:, i, h * 64:(h + 1) * 64],
                        start=(d == 0), stop=(d == 1))

        def scan(i):
            dj = ((0, forder[i]), (1, border[i]))
            xsw = {}
            for d, jj in dj:
                c4 = jj * 8 + d * 4
                xsw[d] = self.rot("xsw", xsws)
                self.tt("pool", h3(xsw[d]), h3(xs_tok[:, jj, :]), dtw[:, c4:c4 + 4].unsqueeze(2).to_broadcast([128, 4, 64]),
                        ALU.mult)
            for d, jj in dj:
                for h in range(4):
                    self.mm(pOb[:, d * 256 + h * 64:d * 256 + (h + 1) * 64], CT[:, h // 2, jj * 128:(jj + 1) * 128],
                            Hb[d][:, h * 64:(h + 1) * 64])
            for d, jj in dj:
                for h in range(4):
                    g = h // 2
                    self.mm(pSb[:, d * 256 + h * 64:d * 256 + (h + 1) * 64], B_tok[:, jj, g * 128:(g + 1) * 128],
                            xsw[d][:, h * 64:(h + 1) * 64])
            tm = {}
            for d, jj in dj:
                c4 = jj * 8 + d * 4
                tm[d] = self.rot("tmpo", tmpos)
                self.tt("dve", h3(tm[d]), h3(pOb[:, d * 256:(d + 1) * 256]),
                        ecs[:, c4:c4 + 4].unsqueeze(2).to_broadcast([128, 4, 64]), ALU.mult)
            for d, jj in dj:
                c4 = jj * 8 + d * 4
                self.tt("dve", h3(H[d]), h3(H[d]), cdec[:, c4:c4 + 4].unsqueeze(2).to_broadcast([128, 4, 64]), ALU.mult)
                self.tt("dve", H[d], H[d], pSb[:, d * 256:(d + 1) * 256], ALU.add)
                self.cp("act", Hb[d], H[d])
            self.tt("dve", ybuf[:, i, :], ybuf[:, i, :], pYb[:, 0:256], ALU.add)
            for d, jj in dj:
                self.tt("pool", ybuf[:, jj, :], ybuf[:, jj, :], tm[d], ALU.add)
        stage_a(0)
        stage_g(0)
        for i in range(NTILE):
            stage_bc(i)
            if i + 1 < NTILE:
                stage_a(i + 1)
            stage_d(i)
            if i + 1 < NTILE:
                stage_g(i + 1)
            scan(i)
        if BSTOP == 6:
            self.top = m
            return
        for j in range(NTILE):
            t1 = self.rot("bt1", t1s)
            self.tt("dve", t1, xs_tok[:, j, :], self.Drow, ALU.mult)
            self.tt("dve", ybuf[:, j, :], t1, ybuf[:, j, :], ALU.add)
            self.tt("pool", ybuf[:, j, :], ybuf[:, j, :], z_tok[:, j, :], ALU.mult)
            self.act(junk, ybuf[:, j, :], AF.Square, accum=ssq[:, j:j + 1])
        self.rstd(ssq, ssq, 1.0 / 256)
        for j in range(NTILE):
            yn = self.rot("byn", yns)
            self.ts("dve", yn, ybuf[:, j, :], ssq[:, j:j + 1], None, ALU.mult)
            ptr = self.bank("acc").bitcast(BF16)
            for c in range(2):
                self.tr(ptr[:, c * 128:(c + 1) * 128], yn[:, c * 128:(c + 1) * 128], self.identB)
            self.act(self.Y[:, 2, j * 128:(j + 1) * 128], ptr[:, 0:128], AF.Identity, scale=self.snw[:, 0:1])
            self.ts("dve", self.Y[:, 3, j * 128:(j + 1) * 128], ptr[:, 128:256], self.snw[:, 1:2], None, ALU.mult)
        self.release(m)

    def pO(self, l, b):
        I = self.I
        m = self.mark()
        wo = self.alloc(8 * 1024, BF16).rearrange("p (k n) -> p k n", k=8)
        m2 = self.mark()
        self.wst = [self.alloc(4096), self.alloc(4096)]
        self.load_w(wo, I["w_out"][l], 8, 1024)
        self.release(m2)
        xbs = [self.alloc(4096), self.alloc(4096)]
        tmp = self.norm_tmp()
        def ldx(ti):
            t0_, n_ = TBS[ti]
            v = xbs[ti % 2].rearrange("p (k t) -> p k t", k=8)
            self.ld(v[:, :, 0:n_], self.xt_view(b, t0_, n_), "xb%d" % (ti % 2), rk=self.tb_keys(b, ti))
        ldx(0)
        for ti, (t0, n) in enumerate(TBS):
            s = ti % 2
            xb = xbs[s].rearrange("p (k t) -> p k t", k=8)
            if ti + 1 < len(TBS):
                ldx(ti + 1)
            r = b if ti < 4 else 2
            for dc in range(8):
                po = self.bank("pj")
                for k in range(8):
                    self.mm(po[:, 0:n], wo[:, k, dc * 128:(dc + 1) * 128], self.Y[:, k, t0:t0 + n],
                            start=(k == 0), stop=(k == 7))
                self.stt(xb[:, dc, 0:n], po[:, 0:n], self.modv[:, 2, dc, r:r + 1], xb[:, dc, 0:n], ALU.mult, ALU.add)
            self.st(self.xt_view(b, t0, n), xb[:, :, 0:n], "xst%d" % s, wk=self.tb_keys(b, ti))
            self.norm_mod_block(xb, n, t0, self.A2v, 3, r, tmp)
        self.release(m)

    def pF(self, l, b):
        I = self.I
        self.release(self.y_mark)
        m = self.mark()
        wd = self.alloc(NFC * 1024, BF16).rearrange("p (k n) -> p k n", k=NFC)
        wd_mark = self.mark()
        wup = [self.alloc(8 * 256, BF16).rearrange("p (k n) -> p k n", k=8) for _ in range(2)]
        stgs = [self.alloc(2048).rearrange("p (k n) -> p k n", k=8) for _ in range(2)]
        dstg = [self.alloc(1024) for _ in range(2)]
        Gps = [self.alloc(GW) for _ in range(2)]
        accs = [self.alloc(GW) for _ in range(2)]
        abufs = [self.alloc(T, BF16) for _ in range(2)]
        sgs = [self.alloc(T, BF16) for _ in range(2)]
        us = [self.alloc(T, BF16) for _ in range(2)]
        for Gp in Gps:
            for c0 in (0, 2049, 2050, 2307):
                self.ms("pool", Gp[:, c0:c0 + 1], 0.0)
        wupd = I["ffn_w_up"][l]
        wdd = I["ffn_w_down"][l]
        W = 2306

        def ldw(fc_):
            s_ = fc_ % 2
            self.ld(stgs[s_][:, :, 0:128], wupd[:, fc_ * 128:(fc_ + 1) * 128].rearrange("(k p) n -> p k n", p=128),
                    "wup%d" % s_)
            self.ld(stgs[s_][:, :, 128:256],
                    wupd[:, DFF + fc_ * 128:DFF + (fc_ + 1) * 128].rearrange("(k p) n -> p k n", p=128), "wup%d" % s_)
            self.ld(dstg[s_], wdd[fc_ * 128:(fc_ + 1) * 128, :], "wdn%d" % s_)

        def conv_stage(fc_, stage):
            s_ = fc_ % 2
            Gp, acc, abuf, sg, u = Gps[s_], accs[s_], abufs[s_], sgs[s_], us[s_]
            if stage == 0:
                self.act(acc[:, 0:W], Gp[:, 1:1 + W], AF.Identity, bias=self.fcb[:, fc_:fc_ + 1],
                         scale=self.fcw[:, fc_ * 3 + 1:fc_ * 3 + 2])
            elif stage == 1:
                self.stt(acc[:, 0:W], Gp[:, 0:W], self.fcw[:, fc_ * 3:fc_ * 3 + 1], acc[:, 0:W], ALU.mult, ALU.add)
            elif stage == 2:
                self.stt(acc[:, 0:W], Gp[:, 2:2 + W], self.fcw[:, fc_ * 3 + 2:fc_ * 3 + 3], acc[:, 0:W],
                         ALU.mult, ALU.add)
            elif stage == 3:
                self.act(sg[:, 0:NLAT], acc[:, 0:NLAT], AF.Silu)
                self.act(sg[:, NLAT:T], acc[:, 2050:2306], AF.Silu)
            else:
                self.tt("dve", u, abuf, sg, ALU.mult)
                self.st(self.usc[fc_], u, "ust%d" % s_, wk=[("usc", fc_)])
        ldw(0)
        ldw(1)
        self.cp("pool", wup[0], stgs[0])
        self.cp("pool", wd[:, 0, :], dstg[0])
        for fc in range(NFC + 1):
            s = fc % 2
            for ti, (t0, n) in enumerate(TBS):
                if fc < NFC:
                    w = wup[s]
                    pa = self.bank("pj")
                    for k_ in range(8):
                        self.mm(pa[:, 0:n], w[:, k_, 0:128], self.hT[:, k_, t0:t0 + n], start=(k_ == 0), stop=(k_ == 7))
                    pg = self.bank("s")
                    for k_ in range(8):
                        self.mm(pg[:, 0:n], w[:, k_, 128:256], self.hT[:, k_, t0:t0 + n], start=(k_ == 0), stop=(k_ == 7))
                    off = 1 + t0 if ti < 4 else 2051
                    self.cp("act", abufs[s][:, t0:t0 + n], pa[:, 0:n])
                    self.cp("dve", Gps[s][:, off:off + n], pg[:, 0:n])
                if fc > 0:
                    conv_stage(fc - 1, ti)
                if ti == 1 and fc + 1 < NFC:
                    self.cp("pool", wup[1 - s], stgs[1 - s])
                    self.cp("pool", wd[:, fc + 1, :], dstg[1 - s])
                if ti == 2 and fc + 2 < NFC:
                    ldw(fc + 2)
        self.release(wd_mark)
        ubs = [self.alloc(NFC * 512, BF16).rearrange("p (f t) -> p f t", f=NFC) for _ in range(2)]
        xbs = [self.alloc(4096), self.alloc(4096)]
        last = (l == 1)
        if last:
            tmp = self.norm_tmp()
            ots = [self.alloc(1024), self.alloc(1024)]
        ukeys = [("usc", fc) for fc in range(NFC)]
        nblk = 4 if last else 5

        def ldd(ti_):
            t0_, n_ = TBS[ti_]
            s_ = ti_ % 2
            self.ld(ubs[s_][:, :, 0:n_], self.usc[:, :, t0_:t0_ + n_].rearrange("f p t -> p f t"), "ub%d" % s_, rk=ukeys)
            v = xbs[s_].rearrange("p (k t) -> p k t", k=8)
            self.ld(v[:, :, 0:n_], self.xt_view(b, t0_, n_), "xb%d" % s_, rk=self.tb_keys(b, ti_))
        for ti, (t0, n) in enumerate(TBS):
            if last and ti == 4:
                break
            s = ti % 2
            ub = ubs[s]
            xb = xbs[s].rearrange("p (k t) -> p k t", k=8)
            if ti == 0:
                ldd(0)
            if ti + 1 < nblk:
                ldd(ti + 1)
            r = b if ti < 4 else 2
            for dc in range(8):
                po = self.bank("pj")
                for fc in range(NFC):
                    self.mm(po[:, 0:n], wd[:, fc, dc * 128:(dc + 1) * 128], ub[:, fc, 0:n],
                            start=(fc == 0), stop=(fc == NFC - 1))
                self.stt(xb[:, dc, 0:n], po[:, 0:n], self.modv[:, 5, dc, r:r + 1], xb[:, dc, 0:n], ALU.mult, ALU.add)
            if not last:
                self.st(self.xt_view(b, t0, n), xb[:, :, 0:n], "xst%d" % s, wk=self.tb_keys(b, ti))
                continue
            ss = self.bank("aux")
            for k in range(8):
                sq = self.rot("sqb", tmp["sq"])
                self.act(sq[:, 0:n], xb[:, k, 0:n], AF.Square)
                self.mm(ss[:, 0:n], self.onesB, sq[:, 0:n], start=(k == 0), stop=(k == 7))
            rs = self.rot("rsb", tmp["rs"])
            self.rstd(rs[:, 0:n], ss[:, 0:n], 1.0 / D)
            for k in range(8):
                self.stt(xb[:, k, 0:n], xb[:, k, 0:n], self.fnw[:, k:k + 1], rs[:, 0:n], ALU.mult, ALU.mult)
            for tt_ in range(n // 128):
                ot = self.rot("ot", ots)
                p0 = self.bank("s")
                p1 = self.bank("acc")
                for k in range(8):
                    pp = p0 if k < 4 else p1
                    self.tr(pp[:, (k % 4) * 128:(k % 4 + 1) * 128], xb[:, k, tt_ * 128:(tt_ + 1) * 128], self.identF)
                self.cp("act", ot[:, 0:512], p0[:, :])
                self.cp("dve", ot[:, 512:1024], p1[:, :])
                tok = t0 + tt_ * 128
                self.st(self.out[b, tok:tok + 128, :], ot, "ost%d" % (self.rotc["ot"] % 2), wk=[("out", b, tok)])
                self.outkeys.append(("out", b, tok))
        self.release(m)

    def build(self, stop=None):
        self.outkeys = []
        self.P.cur_tag = "setup"
        self.setup_consts()
        self.P.cur_tag = "x0"
        self.phase_x0()
        done = False
        for l in range(2):
            self.P.cur_tag = ("lsetup", l)
            self.layer_setup(l)
            for b in range(2):
                self.lb_mark = self.mark()
                self.hT = self.alloc(8 * T, BF16).rearrange("p (k t) -> p k t", k=8)
                self.y_mark = self.mark()
                self.Y = self.alloc(8 * T, BF16).rearrange("p (k t) -> p k t", k=8)
                for pi, ph in enumerate(("p1", "pA", "pB", "pC", "pD", "pO", "pF")):
                    self.P.cur_tag = (l, b, ph)
                    if MARK:
                        self.act(self.mk, self.mk, AF.Abs)
                    getattr(self, ph)(l, b)
                    if stop == (l, b, ph):
                        done = True
                        break
                if done:
                    break
                self.release(self.lb_mark)
            if done:
                break
        if done:
            self.dump("hT", self.hT.rearrange("p k t -> p (k t)"), 8 * T)
            self.dump("Y", self.Y.rearrange("p k t -> p (k t)"), 8 * T)
        keys = list(self.outkeys) + [("dbg", n, c) for n in self.dump_aps for c in range(0, 8 * T, 512)]
        self.P.op("sp", lambda e: e.nop(), reads=keys, writes=[])


def build_program(stop=None, dumps=()):
    nc = bass.Bass("TRN2", target_bir_lowering=False)
    with ExitStack() as es:
        kb = KB(nc, es, dumps)
        kb.build(stop)
        kb.P.finalize(kb.sem)
        print("ops", len(kb.P.allops), "waits", kb.P.nwaits, "sems", kb.nsem, flush=True)
        with nc.Block() as block:
            kb.P.emit(block)
    return nc


def make_in_maps(inputs):
    cst = host_consts()
    maps = []
    for core in range(8):
        m = {}
        for name, shp in WSPEC:
            if name in cst:
                m[name] = cst[name]
            elif name in ("x", "c", "ctx"):
                m[name] = np.ascontiguousarray(np.asarray(inputs[name], dtype=np.float32)[2 * core:2 * core + 2])
            else:
                m[name] = np.ascontiguousarray(np.asarray(inputs[name], dtype=np.float32))
        maps.append(m)
    return maps


_NC_CACHE = {}


def kernel(**inputs):
    if "nc" not in _NC_CACHE:
        _NC_CACHE["nc"] = build_program()
    nc = _NC_CACHE["nc"]
    maps = make_in_maps(inputs)
    res = run_bass_kernel_spmd(nc, maps, core_ids=list(range(8)))
    return np.concatenate([np.asarray(r["out"]) for r in res.results], axis=0).astype(np.float32)
```

```python
import numpy as np
import concourse.bass as bass
import concourse.mybir as mybir
from concourse.bass_utils import run_bass_kernel_spmd

F32 = mybir.dt.float32
BF16 = mybir.dt.bfloat16
AF = mybir.ActivationFunctionType
ALU = mybir.AluOpType
AX = mybir.AxisListType

GRAN = 64
EPOCH = 30000
_ESZ = {F32: 4, BF16: 2}


def _esize(dt):
    return _ESZ.get(dt, 4)


class DmaSem:
    def __init__(self, handle):
        self.h = handle
        self.count = 0
        self.last_waited = 0


class Op:
    __slots__ = ("eng", "fn", "deps", "pdeps", "is_dma", "sem", "semval", "signal",
                 "sigdim", "sigval", "waits", "clock", "tag")

    def __init__(self, eng, fn):
        self.eng = eng
        self.fn = fn
        self.deps = {}
        self.pdeps = []
        self.tag = None
        self.is_dma = False
        self.sem = None
        self.semval = 0
        self.signal = False
        self.sigdim = None
        self.sigval = 0
        self.waits = None
        self.clock = None


class Prog:
    ENGS = ("pe", "act", "dve", "pool", "sp")

    def __init__(self, nc):
        self.nc = nc
        self.allops = []
        self.lastw = {}
        self.readers = {}
        self.pseudo = set()
        self.nops = 0
        self.cur_tag = None

    @staticmethod
    def keys_of(ap):
        t = ap.tensor
        pat = ap.ap
        es = _esize(ap.dtype)
        rowlen = pat[0][0]
        off = ap.offset
        start = off % rowlen if rowlen > 0 else off
        ext = 0
        for st, cnt in pat[1:]:
            ext += (cnt - 1) * abs(st)
        b0 = (start * es) // GRAN
        b1 = ((start + ext + 1) * es - 1) // GRAN
        name = t.name
        if name.startswith("ps"):
            return [(name, 0)]
        return [(name, b) for b in range(b0, b1 + 1)]

    def _collect(self, items):
        ks = []
        for it in items:
            if it is None:
                continue
            if isinstance(it, tuple):
                ks.append(it)
            else:
                ks.extend(self.keys_of(it))
        return ks

    def _record(self, o, reads, writes):
        rk = self._collect(reads)
        wk = self._collect(writes)
        deps = o.deps
        prk = [k for k in rk if k[0].startswith("ps") and k not in wk]
        for k in rk:
            w = self.lastw.get(k)
            if w is not None:
                deps[w] = "rar" if (k in self.pseudo and deps.get(w) != "raw") else "raw"
        for k in wk:
            w = self.lastw.get(k)
            if w is not None and w not in deps:
                deps[w] = "waw"
            for r in self.readers.get(k, ()):
                if r not in deps:
                    deps[r] = "war"
        for k in rk:
            self.readers.setdefault(k, []).append(o)
        for k in wk:
            self.lastw[k] = o
            self.readers[k] = []
            self.pseudo.discard(k)
        for k in prk:
            self.lastw[k] = o
            self.readers[k] = []
            self.pseudo.add(k)
        keep = {}
        for d, kind in deps.items():
            if d is o:
                continue
            if d.eng == o.eng and not d.is_dma and not o.is_dma:
                if o.eng == "pe":
                    continue
                if kind == "rar":
                    continue
            keep[d] = kind
        o.deps = keep
        o.tag = self.cur_tag
        for d in keep:
            if d.is_dma:
                sm = d.sem
                o.pdeps.append((sm, sm.count))
                if sm.count > sm.last_waited:
                    sm.last_waited = sm.count
        self.allops.append(o)

    def op(self, eng, fn, reads=(), writes=()):
        o = Op(eng, fn)
        self._record(o, reads, writes)
        return o

    def dma(self, eng, out, in_, sem, reads=None, writes=None, **kw):
        o = Op(eng, None)
        o.is_dma = True
        o.sem = sem
        if sem.last_waited > 0:
            o.pdeps.append((sem, sem.last_waited))
        o.fn = lambda e: e.dma_start(out=out, in_=in_, **kw)
        self._record(o, reads if reads is not None else [in_],
                     writes if writes is not None else [out])
        sem.count += 16
        o.semval = sem.count
        return o

    def finalize(self, sem_alloc):
        for o in self.allops:
            for d in o.deps:
                if not d.is_dma:
                    d.signal = True
        cnt = {e: 0 for e in self.ENGS}
        ep = {e: 0 for e in self.ENGS}
        self.engsem = {}
        for o in self.allops:
            if o.is_dma:
                o.sigdim = o.sem
                o.sigval = o.semval
                continue
            if o.signal:
                e = o.eng
                if cnt[e] >= EPOCH:
                    cnt[e] = 0
                    ep[e] += 1
                cnt[e] += 1
                dim = (e, ep[e])
                if dim not in self.engsem:
                    self.engsem[dim] = sem_alloc()
                o.sigdim = dim
                o.sigval = cnt[e]
        known = {e: {} for e in self.ENGS}
        nwaits = 0
        for o in self.allops:
            K = known[o.eng]
            need = {}
            for d in o.deps:
                if d.is_dma:
                    continue
                dim, val = d.sigdim, d.sigval
                if K.get(dim, 0) < val and need.get(dim, 0) < val:
                    need[dim] = val
            for sem, val in o.pdeps:
                if K.get(sem, 0) < val and need.get(sem, 0) < val:
                    need[sem] = val
            for d in o.deps:
                ck = d.clock
                for dim, val in ck.items():
                    if K.get(dim, 0) < val:
                        K[dim] = val
            for dim, val in need.items():
                if K.get(dim, 0) < val:
                    K[dim] = val
            o.waits = list(need.items())
            nwaits += len(o.waits)
            ck = dict(K)
            if o.is_dma or o.signal:
                ck[o.sigdim] = max(ck.get(o.sigdim, 0), o.sigval)
            o.clock = ck
        self.nwaits = nwaits

    def emit(self, block):
        per = {e: [] for e in self.ENGS}
        for o in self.allops:
            per[o.eng].append(o)
        engsem = self.engsem

        def run(eng_obj, ops):
            for o in ops:
                for dim, val in o.waits:
                    h = dim.h if isinstance(dim, DmaSem) else engsem[dim]
                    eng_obj.wait_ge(h, val)
                ins = o.fn(eng_obj)
                if o.is_dma:
                    ins.then_inc(o.sem.h, 16)
                elif o.signal:
                    ins.then_inc(engsem[o.sigdim], 1)

        @block.tensor
        def _(e):
            run(e, per["pe"])

        @block.scalar
        def _(e):
            run(e, per["act"])

        @block.vector
        def _(e):
            run(e, per["dve"])

        @block.gpsimd
        def _(e):
            run(e, per["pool"])

        @block.sync
        def _(e):
            run(e, per["sp"])
from contextlib import ExitStack

D = 1024
T = 2304
NLAT = 2048
NTILE = 18
TBS = [(0, 512), (512, 512), (1024, 512), (1536, 512), (2048, 256)]
DFF = 2816
NFC = 22
EPS = 1e-6
import os as _os
ARENA = int(_os.environ.get("KARENA", "53000"))
BSTOP = int(_os.environ.get("BSTOP", "0"))
ALIGN = int(_os.environ.get("KALIGN", "16"))
MARK = int(_os.environ.get("KMARK", "0"))
GW = 2310
NEG = -30000.0


def host_consts():
    f = np.float32
    idn = np.eye(128, dtype=f)
    k = np.arange(128)
    tri_f = (k[:, None] <= k[None, :]).astype(f)
    tri_b = (k[:, None] >= k[None, :]).astype(f)
    nm_f = np.where(k[None, :] >= k[:, None], 0.0, NEG).astype(f)
    nm_b = np.where(k[:, None] >= k[None, :], 0.0, NEG).astype(f)
    cF = np.concatenate([idn, tri_f, tri_b, nm_f, nm_b], axis=1)

    def perm(base, half):
        Pm = np.zeros((128, 128), f)
        for h0 in (base, base + half):
            q = half // 2
            for jj in range(q):
                Pm[h0 + jj + q, h0 + jj] = -1.0
                Pm[h0 + jj, h0 + q + jj] = 1.0
        return Pm
    PA = perm(0, 32) + perm(64, 32)
    PD = perm(64, 16)
    BD = np.zeros((128, 128), f)
    BD[0:64, 0:64] = 1.0
    BD[64:128, 64:128] = 1.0
    a = np.arange(128)[:, None]
    bq = np.arange(128)[None, :]
    masks = []
    for r in range(-1, 5):
        m = np.zeros((128, 512), f)
        for c in range(4):
            if r == c:
                m[:, c * 128:(c + 1) * 128] = 1.0
            elif r - c == -1:
                m[:, c * 128:(c + 1) * 128] = (bq <= a)
            elif r - c == 1:
                m[:, c * 128:(c + 1) * 128] = (a <= bq)
        masks.append(m)
    cB = np.concatenate([PA, PD, BD] + masks, axis=1)

    def tables(rot_dim):
        rows = NLAT // 64
        row = np.repeat(np.arange(rows, dtype=f), 64)
        col = np.tile(np.arange(64, dtype=f), rows)
        half = rot_dim // 2
        inv = np.power(f(10000.0), -np.arange(0, half, 2, dtype=f) / f(half)).astype(f)
        ar = (row[:, None] * inv[None, :]).astype(f)
        ac = (col[:, None] * inv[None, :]).astype(f)
        ang = np.concatenate([ar, ar, ac, ac], axis=-1)
        return np.cos(ang).astype(f), np.sin(ang).astype(f)
    cA, sA = tables(64)
    ropeA = np.zeros((2, 128, T), f)
    ropeA[0] = 1.0
    ropeA[0, 0:64, 0:NLAT] = cA.T
    ropeA[0, 64:128, 0:NLAT] = cA.T
    ropeA[1, 0:64, 0:NLAT] = sA.T
    ropeA[1, 64:128, 0:NLAT] = sA.T
    cD, sD = tables(32)
    ropeD = np.zeros((2, 128, T), f)
    ropeD[0] = 1.0
    ropeD[0, 64:96, 0:NLAT] = cD.T
    ropeD[1, 64:96, 0:NLAT] = sD.T
    return dict(cF=cF, cB=cB, ropeA=ropeA, ropeD=ropeD)


WSPEC = [
    ("x", [2, 2048, 1024]), ("c", [2, 1024]), ("ctx", [2, 256, 1024]), ("c_ctx", [1024]),
    ("norm1_w", [2, 1024]), ("w_mod", [2, 1024, 6144]), ("b_mod", [2, 6144]),
    ("w_in", [2, 1024, 2408]), ("attn_q_norm", [2, 64]), ("attn_k_norm", [2, 64]),
    ("ssm_conv_w", [2, 768, 3]), ("ssm_conv_b", [2, 768]), ("ssm_dt_bias", [2, 2, 4]),
    ("ssm_a_log", [2, 2, 4]), ("ssm_d", [2, 4]), ("ssm_norm_w", [2, 256]), ("win_sink", [2, 4]),
    ("mla_q_norm", [2, 192]), ("mla_w_uq", [2, 192, 384]), ("mla_kv_norm", [2, 128]),
    ("mla_w_ukv", [2, 128, 512]), ("w_out", [2, 1024, 1024]), ("norm2_w", [2, 1024]),
    ("ffn_w_up", [2, 1024, 5632]), ("ffn_conv_w", [2, 2816, 3]), ("ffn_conv_b", [2, 2816]),
    ("ffn_w_down", [2, 2816, 1024]), ("final_norm_w", [1024]),
    ("cF", [128, 640]), ("cB", [128, 384 + 6 * 512]), ("ropeA", [2, 128, T]), ("ropeD", [2, 128, T]),
]


class KB:
    def __init__(self, nc, es, dumps=()):
        self.nc = nc
        self.es = es
        self.P = Prog(nc)
        self.dumps = set(dumps)
        self.I = {}
        for name, shp in WSPEC:
            self.I[name] = nc.dram_tensor(name, shp, F32, kind="ExternalInput").ap()
        self.out = nc.dram_tensor("out", [2, 2048, 1024], F32, kind="ExternalOutput").ap()
        self.xt = nc.dram_tensor("xt", [2, 8, 128, T], F32, kind=("ExternalOutput" if "xt" in self.dumps else "Internal")).ap()
        self.usc = nc.dram_tensor("usc", [NFC, 128, T], BF16, kind="Internal").ap()
        self.arena = es.enter_context(nc.sbuf_tensor("arena", [128, ARENA], F32))
        self.pss = [es.enter_context(nc.psum_tensor("ps%d" % i, [128, 512], F32)) for i in range(8)]
        self.top = 0
        self.nsem = 0
        self.rotc = {}
        self.dsems = {}
        self.dump_aps = {}
        self.pe_slices = {}

    def sem(self):
        s = self.es.enter_context(self.nc.semaphore("s%d" % self.nsem))
        self.nsem += 1
        return s

    def dsem(self, name):
        if name not in self.dsems:
            self.dsems[name] = DmaSem(self.sem())
        return self.dsems[name]

    def alloc(self, n, dt=F32):
        es_ = _esize(dt)
        n4 = (n * es_ + 3) // 4
        off = (self.top + ALIGN - 1) // ALIGN * ALIGN
        self.top = off + n4
        assert self.top <= ARENA, ("arena overflow", self.top)
        a = self.arena[:, off:off + n4]
        if dt == F32:
            return a
        return a.bitcast(dt)[:, 0:n]

    def mark(self):
        return self.top

    def release(self, m):
        self.top = m

    def rot(self, name, items):
        i = self.rotc.get(name, 0)
        self.rotc[name] = i + 1
        return items[i % len(items)]

    BANKS_DEFAULT = {"pj": (0, 1), "s": (2, 3), "acc": (4, 5), "aux": (6,), "aux2": (7,)}
    BANKS_PROJ = {"pj": (0, 1, 2, 3, 4, 5), "s": (2, 3), "acc": (4, 5), "aux": (6,), "aux2": (7,)}
    BANKS_ATT = {"pj": (0, 1), "s": (0, 1, 2, 3, 6, 7), "acc": (4, 5), "aux": (6,), "aux2": (7,)}

    def bank(self, grp):
        banks = getattr(self, "bankmap", self.BANKS_DEFAULT)[grp]
        return self.pss[self.rot("bank_" + grp, banks)]

    def mm(self, out, lhsT, rhs, start=True, stop=True, tp=None):
        kw = dict(start=start, stop=stop)
        if tp is not None:
            kw["tile_position"] = tp
        self.P.op("pe", lambda e: e.matmul(out, lhsT, rhs, **kw), [lhsT, rhs], [out])
        t_ = self.P.cur_tag
        self.pe_slices[t_] = self.pe_slices.get(t_, 0) + (2 if lhsT.dtype == F32 else 1)

    def tr(self, out, in_, ident):
        self.P.op("pe", lambda e: e.transpose(out, in_, ident), [in_, ident], [out])
        t_ = self.P.cur_tag
        self.pe_slices[t_] = self.pe_slices.get(t_, 0) + 1

    def act(self, out, in_, func, bias=None, scale=None, accum=None):
        kw = {}
        rd = [in_]
        wr = [out]
        if bias is not None:
            kw["bias"] = bias
            if not isinstance(bias, (int, float)):
                rd.append(bias)
        if scale is not None:
            kw["scale"] = scale
            if not isinstance(scale, (int, float)):
                rd.append(scale)
        if accum is not None:
            kw["accum_out"] = accum
            wr.append(accum)
        self.P.op("act", lambda e: e.activation(out, in_, func, **kw), rd, wr)

    def tt(self, eng, out, in0, in1, op):
        self.P.op(eng, lambda e: e.tensor_tensor(out, in0, in1, op), [in0, in1], [out])

    def ts(self, eng, out, in0, s1, s2, op0, op1=None):
        rd = [in0]
        for s in (s1, s2):
            if s is not None and not isinstance(s, (int, float)):
                rd.append(s)
        if op1 is None:
            self.P.op(eng, lambda e: e.tensor_scalar(out, in0, s1, None, op0), rd, [out])
        else:
            self.P.op(eng, lambda e: e.tensor_scalar(out, in0, s1, s2, op0, op1), rd, [out])

    def stt(self, out, in0, scalar, in1, op0, op1):
        rd = [in0, in1]
        if not isinstance(scalar, (int, float)):
            rd.append(scalar)
        self.P.op("dve", lambda e: e.scalar_tensor_tensor(out, in0, scalar, in1, op0, op1), rd, [out])

    def cp(self, eng, out, in_):
        if eng == "act":
            self.act(out, in_, AF.Copy)
        else:
            self.P.op(eng, lambda e: e.tensor_copy(out, in_), [in_], [out])

    def ms(self, eng, out, val):
        self.P.op(eng, lambda e: e.memset(out, val), [], [out])

    def recip(self, out, in_):
        self.P.op("dve", lambda e: e.reciprocal(out, in_), [in_], [out])

    def ld(self, out, in_, sem, rk=()):
        self.P.dma("sp", out, in_, self.dsem(sem), reads=list(rk), writes=[out])

    def st(self, out, in_, sem, wk=()):
        self.P.dma("sp", out, in_, self.dsem(sem), reads=[in_], writes=list(wk))

    def rstd(self, out, ss, inv_n):
        self.act(out, ss, AF.Ln, bias=EPS, scale=inv_n)
        self.act(out, out, AF.Exp, scale=-0.5)

    def dump(self, name, src, n):
        if name not in self.dumps:
            return
        dst = self.nc.dram_tensor("dbg_" + name, [128, n], F32, kind="ExternalOutput").ap()
        pn = src.shape[0]
        p0 = src.base_partition()
        m = self.mark()
        tmp = self.alloc(512)
        for c0 in range(0, n, 512):
            w = min(512, n - c0)
            self.cp("dve", tmp[p0:p0 + pn, 0:w], src[:, c0:c0 + w])
            self.st(dst[p0:p0 + pn, c0:c0 + w], tmp[p0:p0 + pn, 0:w], "dbg", wk=[("dbg", name, c0)])
        self.release(m)
        self.dump_aps[name] = 1

    def load_w(self, dst, src, K, n, rows_last=128):
        per = max(1, self.wst_cap // n)
        k0 = 0
        while k0 < K:
            kg = min(per, K - k0)
            slot = self.rot("wst", (0, 1))
            stg = self.wst[slot]
            full = kg if not (k0 + kg == K and rows_last != 128) else kg - 1
            v = stg[:, 0:kg * n].rearrange("p (k n) -> p k n", k=kg)
            if full > 0:
                self.ld(v[:, 0:full, :], src[k0 * 128:(k0 + full) * 128, :].rearrange("(k p) n -> p k n", p=128),
                        "wst%d" % slot)
                self.cp("dve", dst[:, k0:k0 + full, :], v[:, 0:full, :])
            if full < kg:
                r0 = (k0 + full) * 128
                self.ld(v[0:rows_last, full, :], src[r0:r0 + rows_last, :], "wst%d" % slot)
                self.cp("dve", dst[0:rows_last, k0 + full, :], v[0:rows_last, full, :])
            k0 += kg

    def setup_consts(self):
        I = self.I
        cf = self.alloc(640)
        self.ld(cf, I["cF"], "cst")
        self.identF = cf[:, 0:128]
        self.triF = cf[:, 128:256]
        self.triB = cf[:, 256:384]
        self.nmF = cf[:, 384:512]
        self.nmB = cf[:, 512:640]
        self.onesF = self.alloc(128)
        self.ms("pool", self.onesF, 1.0)
        self.onesB = self.alloc(128, BF16)
        self.ms("pool", self.onesB, 1.0)
        self.identB = self.alloc(128, BF16)
        self.cp("pool", self.identB, self.identF)
        self.nmFb = self.alloc(128, BF16)
        self.nmBb = self.alloc(128, BF16)
        self.cp("pool", self.nmFb, self.nmF)
        self.cp("pool", self.nmBb, self.nmB)
        self.nmb4 = [self.alloc(512, BF16), self.alloc(512, BF16)]
        for d_, src_ in enumerate((self.nmF, self.nmB)):
            for h_ in range(4):
                self.cp("pool", self.nmb4[d_][:, h_ * 128:(h_ + 1) * 128], src_)
        self.PA = self.alloc(128, BF16)
        self.PD = self.alloc(128, BF16)
        self.BD = self.alloc(128, BF16)
        self.wmask = self.alloc(6 * 512, BF16)
        self.LV = self.alloc(128)
        self.scw = self.alloc(18)
        self.fcw = self.alloc(66)
        self.rows = self.alloc(24)
        self.arow = self.alloc(8)
        self.sinkexp = self.alloc(4)
        self.Drow = self.alloc(256)
        self.SC = self.alloc(24)
        self.modT = self.alloc(144)
        self.A1 = self.alloc(24)
        self.A2 = self.alloc(24)
        self.wst = None
        self.wst_cap = 4096
        self.mk = self.alloc(2)
        self.ms("pool", self.mk, 0.0)
        m = self.mark()
        stg = self.alloc(384 + 3072)
        self.ld(stg, I["cB"], "cst")
        self.cp("pool", self.PA, stg[:, 0:128])
        self.cp("pool", self.PD, stg[:, 128:256])
        self.cp("pool", self.BD, stg[:, 256:384])
        self.cp("pool", self.wmask, stg[:, 384:384 + 3072])
        cst = self.alloc(128)
        self.ms("dve", cst, 0.0)
        self.ld(cst[0:16, :], I["c"].rearrange("b (k p) -> (b k) p", p=128), "cst")
        self.ld(cst[16:24, :], I["c_ctx"].rearrange("(k p) -> k p", p=128), "cst")
        ps = self.bank("aux")
        self.tr(ps[:, 0:128], cst, self.identF)
        self.act(self.SC, ps[:, 0:24], AF.Silu)
        self.release(m)
        self.persist_top = self.top

    def layer_setup(self, l):
        I = self.I
        m = self.mark()
        stg = self.alloc(128)
        self.ms("dve", stg, 0.0)

        def rows(r0, src2d):
            n = src2d.shape[0]
            self.ld(stg[r0:r0 + n, 0:src2d.shape[1]], src2d, "cst")
        rows(0, I["norm1_w"][l].rearrange("(k p) -> k p", p=128))
        rows(8, I["norm2_w"][l].rearrange("(k p) -> k p", p=128))
        rows(16, I["b_mod"][l].rearrange("(k p) -> k p", p=128))
        rows(64, I["ssm_conv_b"][l].rearrange("(k p) -> k p", p=128))
        rows(70, I["ssm_norm_w"][l].rearrange("(k p) -> k p", p=128))
        rows(72, I["ffn_conv_b"][l].rearrange("(k p) -> k p", p=128))
        rows(94, I["mla_kv_norm"][l].rearrange("(k p) -> k p", p=128))
        aq = I["attn_q_norm"][l].rearrange("(k p) -> k p", p=64)
        ak = I["attn_k_norm"][l].rearrange("(k p) -> k p", p=64)
        self.ld(stg[95:96, 0:64], aq, "cst")
        self.ld(stg[95:96, 64:128], aq, "cst")
        self.ld(stg[96:97, 0:64], ak, "cst")
        self.ld(stg[96:97, 64:128], ak, "cst")
        mq = I["mla_q_norm"][l]
        self.ld(stg[97:98, :], mq[0:128].rearrange("(k p) -> k p", p=128), "cst")
        self.ld(stg[98:99, 0:64], mq[128:192].rearrange("(k p) -> k p", p=64), "cst")
        rows(99, I["final_norm_w"].rearrange("(k p) -> k p", p=128))
        ps = self.bank("aux")
        self.tr(ps[:, 0:128], stg, self.identF)
        self.cp("dve", self.LV, ps[:, 0:128])
        LV = self.LV
        self.n1w = LV[:, 0:8]
        self.n2w = LV[:, 8:16]
        self.bmodT = LV[:, 16:64]
        self.scb = LV[:, 64:70]
        self.snw = LV[:, 70:72]
        self.fcb = LV[:, 72:94]
        self.kvn = LV[:, 94:95]
        self.aqn = LV[:, 95:96]
        self.akn = LV[:, 96:97]
        self.mqn0 = LV[:, 97:98]
        self.mqn1 = LV[:, 98:99]
        self.fnw = LV[:, 99:107]
        self.ld(self.scw.rearrange("p (j k) -> p j k", k=3),
                I["ssm_conv_w"][l].rearrange("(j p) k -> p j k", p=128), "cst")
        self.ld(self.fcw.rearrange("p (j k) -> p j k", k=3),
                I["ffn_conv_w"][l].rearrange("(j p) k -> p j k", p=128), "cst")

        def bro(dst, src1d, n):
            self.ld(dst, src1d.rearrange("(o n) -> o n", o=1).to_broadcast([128, n]), "cst")
        bro(self.rows[:, 0:8], I["ssm_dt_bias"][l].rearrange("d h -> (d h)"), 8)
        bro(self.rows[:, 8:16], I["ssm_a_log"][l].rearrange("d h -> (d h)"), 8)
        bro(self.rows[:, 16:20], I["ssm_d"][l], 4)
        bro(self.rows[:, 20:24], I["win_sink"][l], 4)
        self.dtb = self.rows[:, 0:8]
        self.act(self.arow, self.rows[:, 8:16], AF.Exp)
        self.ts("dve", self.arow, self.arow, -1.0, None, ALU.mult)
        self.act(self.sinkexp, self.rows[:, 20:24], AF.Exp)
        for h in range(4):
            self.cp("dve", self.Drow[:, h * 64:(h + 1) * 64], self.rows[:, 16 + h:17 + h].to_broadcast([128, 64]))
        wm = [self.alloc(8192), self.alloc(8192)]
        wmb = [self.alloc(8192, BF16), self.alloc(8192, BF16)]
        SCb = self.alloc(24, BF16)
        self.cp("dve", SCb, self.SC)
        SCv = SCb.rearrange("p (r k) -> p k r", r=3)
        modv = self.modT.rearrange("p (j d r) -> p j d r", j=6, d=8)
        for j6 in range(6):
            w = wm[j6 % 2]
            wv = w.rearrange("p (k n) -> p k n", k=8)
            wb = wmb[j6 % 2].rearrange("p (k n) -> p k n", k=8)
            self.ld(wv, I["w_mod"][l][:, j6 * 1024:(j6 + 1) * 1024].rearrange("(k p) n -> p k n", p=128),
                    "wm%d" % (j6 % 2))
            self.cp("dve", wb[:, 0:3, :], wv[:, 0:3, :])
            self.cp("act", wb[:, 3:6, :], wv[:, 3:6, :])
            self.cp("pool", wb[:, 6:8, :], wv[:, 6:8, :])
            pm = self.bank("pj")
            for dc in range(8):
                for k_ in range(8):
                    self.mm(pm[:, dc * 3:dc * 3 + 3], wb[:, k_, dc * 128:(dc + 1) * 128], SCv[:, k_, :],
                            start=(k_ == 0), stop=(k_ == 7))
            self.tt("dve", modv[:, j6], pm[:, 0:24].rearrange("p (d r) -> p d r", r=3),
                    self.bmodT[:, j6 * 8:(j6 + 1) * 8].unsqueeze(2).to_broadcast([128, 8, 3]), ALU.add)
        A1v = self.A1.rearrange("p (r k) -> p r k", r=3)
        A2v = self.A2.rearrange("p (r k) -> p r k", r=3)
        for r in range(3):
            self.stt(A1v[:, r, :], modv[:, 1, :, r], 1.0, self.n1w, ALU.add, ALU.mult)
            self.stt(A2v[:, r, :], modv[:, 4, :, r], 1.0, self.n2w, ALU.add, ALU.mult)
        self.modv = modv
        self.A1v = A1v
        self.A2v = A2v
        self.release(m)

    def tb_keys(self, b, ti):
        t0, n = TBS[ti]
        return [("xt", b, i) for i in range(t0 // 128, (t0 + n) // 128)]

    def xt_view(self, b, t0, n):
        return self.xt[b][:, :, t0:t0 + n].rearrange("k p t -> p k t")

    def phase_x0(self):
        I = self.I
        m = self.mark()
        xin = [self.alloc(1024), self.alloc(1024)]
        xo = [self.alloc(1024), self.alloc(1024)]
        def src_of(g):
            b, i = divmod(g, NTILE)
            return I["x"][b, i * 128:(i + 1) * 128, :] if i < 16 else I["ctx"][b, (i - 16) * 128:(i - 15) * 128, :]
        self.ld(xin[0], src_of(0), "xin0")
        for b in range(2):
            for i in range(NTILE):
                g = b * NTILE + i
                s = g % 2
                if g + 1 < 2 * NTILE:
                    self.ld(xin[1 - s], src_of(g + 1), "xin%d" % (1 - s))
                p0 = self.bank("pj")
                p1 = self.bank("pj")
                for k in range(8):
                    pp = p0 if k < 4 else p1
                    self.tr(pp[:, (k % 4) * 128:(k % 4 + 1) * 128], xin[s][:, k * 128:(k + 1) * 128], self.identF)
                self.cp("act", xo[s][:, 0:512], p0[:, :])
                self.cp("dve", xo[s][:, 512:1024], p1[:, :])
                self.st(self.xt_view(b, i * 128, 128), xo[s].rearrange("p (k t) -> p k t", k=8),
                        "xo%d" % s, wk=[("xt", b, i)])
        self.release(m)

    def norm_mod_block(self, xb, n, t0, Av, shift_j, r, tmp):
        ss = self.bank("aux")
        for k in range(8):
            sq = self.rot("sqb", tmp["sq"])
            if k % 3 == 2:
                self.tt("pool", sq[:, 0:n], xb[:, k, 0:n], xb[:, k, 0:n], ALU.mult)
            else:
                self.act(sq[:, 0:n], xb[:, k, 0:n], AF.Square)
            self.mm(ss[:, 0:n], self.onesB, sq[:, 0:n], start=(k == 0), stop=(k == 7))
        rs = self.rot("rsb", tmp["rs"])
        self.rstd(rs[:, 0:n], ss[:, 0:n], 1.0 / D)
        for k in range(8):
            t = self.rot("nmt", tmp["t"])
            self.stt(t[:, 0:n], xb[:, k, 0:n], Av[:, r, k:k + 1], rs[:, 0:n], ALU.mult, ALU.mult)
            bcol = self.modv[:, shift_j, k, r:r + 1]
            if k % 4 == 3:
                self.ts("pool", self.hT[:, k, t0:t0 + n], t[:, 0:n], bcol, 1.0, ALU.add, ALU.mult)
            elif k % 4 == 1:
                self.ts("dve", self.hT[:, k, t0:t0 + n], t[:, 0:n], bcol, None, ALU.add)
            else:
                self.act(self.hT[:, k, t0:t0 + n], t[:, 0:n], AF.Identity, bias=bcol)

    def norm_tmp(self):
        return dict(sq=[self.alloc(512, BF16) for _ in range(3)], rs=[self.alloc(512) for _ in range(2)],
                    t=[self.alloc(512) for _ in range(3)])

    def p1(self, l, b):
        m = self.mark()
        xbs = [self.alloc(4096), self.alloc(4096)]
        tmp = self.norm_tmp()
        for ti, (t0, n) in enumerate(TBS):
            s = ti % 2
            xb = xbs[s].rearrange("p (k t) -> p k t", k=8)
            self.ld(xb[:, :, 0:n], self.xt_view(b, t0, n), "xb%d" % s, rk=self.tb_keys(b, ti))
            r = b if ti < 4 else 2
            self.norm_mod_block(xb, n, t0, self.A1v, 0, r, tmp)
        self.release(m)


    def proj_fm(self, out, w, c0, M, t0, n, tp=None):
        for k in range(8):
            self.mm(out, w[:, k, c0:c0 + M], self.hT[:, k, t0:t0 + n], start=(k == 0), stop=(k == 7), tp=tp)

    def proj_tm(self, out, w, c0, N, i):
        for k in range(8):
            self.mm(out, self.hT[:, k, i * 128:(i + 1) * 128], w[:, k, c0:c0 + N], start=(k == 0), stop=(k == 7))

    def rope_tmp(self):
        return dict(raw=[self.alloc(512, BF16) for _ in range(3)], sq=[self.alloc(512, BF16) for _ in range(3)],
                    rs=[self.alloc(512) for _ in range(3)], qn=[self.alloc(512, BF16) for _ in range(3)],
                    t1=[self.alloc(512) for _ in range(3)], t2=[self.alloc(512) for _ in range(3)])

    def norm_rope_gen(self, src, p0, p1, n, t0, dst, tmp, perm, cos, sin, gain=None, ones=None, nfeat=64):
        raw = self.rot("rp_raw", tmp["raw"])[p0:p1, 0:n]
        self.act(raw, src, AF.Copy)
        if gain is not None:
            sq = self.rot("rp_sq", tmp["sq"])[p0:p1, 0:n]
            self.act(sq, src, AF.Square)
        yield
        if gain is not None:
            ssp = self.bank("aux")[p0:p1, 0:n]
            self.mm(ssp, ones[p0:p1, p0:p1], sq)
            rs = self.rot("rp_rs", tmp["rs"])[p0:p1, 0:n]
            self.rstd(rs, ssp, 1.0 / nfeat)
            qn = self.rot("rp_qn", tmp["qn"])[p0:p1, 0:n]
            self.stt(qn, raw, gain[p0:p1, :], rs, ALU.mult, ALU.mult)
        else:
            qn = raw
        yield
        rotp = self.bank("aux2")[p0:p1, 0:n]
        self.mm(rotp, perm[p0:p1, p0:p1], qn)
        t1 = self.rot("rp_t1", tmp["t1"])[p0:p1, 0:n]
        t2 = self.rot("rp_t2", tmp["t2"])[p0:p1, 0:n]
        self.tt(self.rot("rp_eng", ("pool", "dve")), t1, qn, cos[p0:p1, t0:t0 + n], ALU.mult)
        self.tt("dve", t2, rotp, sin[p0:p1, t0:t0 + n], ALU.mult)
        if isinstance(dst, list):
            for (a0, a1, d) in dst:
                self.tt("pool", d, t1[a0 - p0:a1 - p0, :], t2[a0 - p0:a1 - p0, :], ALU.add)
        else:
            self.tt("pool", dst, t1, t2, ALU.add)

    def norm_rope(self, *a, **kw):
        for _ in self.norm_rope_gen(*a, **kw):
            pass

    def pipe_step(self):
        for g in list(self.active):
            try:
                next(g)
            except StopIteration:
                self.active.remove(g)

    def pipe_add(self, g):
        next(g)
        self.active.append(g)

    def pipe_drain(self):
        while self.active:
            self.pipe_step()

    def load_rope(self, name):
        cos = self.alloc(T)
        sin = self.alloc(T)
        self.ld(cos, self.I[name][0], "cst")
        self.ld(sin, self.I[name][1], "cst")
        return cos, sin

    def attend(self, qT, kT, K, pb, vaug, ychunk, ob, scale, window=False, sinkcol=None, pts=None, recs=None):
        so = 64 - ob
        for qi, (q0, n) in enumerate(TBS):
            if qi == 4:
                tiles = [(16, None), (17, None)]
            elif not window:
                tiles = [(j, None) for j in range(NTILE)]
            else:
                i0 = 4 * qi
                tiles = [(16, None), (17, None)] + [(j, j - i0 + 1) for j in range(max(0, i0 - 1), min(15, i0 + 4) + 1)]
            acc = self.bank("acc")
            sts = {}
            LA = 3
            for idx in range(len(tiles) + LA):
                if idx < len(tiles):
                    j = tiles[idx][0]
                    st = self.bank("s")
                    self.mm(st[:, 0:n], kT[pb:pb + K, j * 128:(j + 1) * 128], qT[pb:pb + K, q0:q0 + n])
                    sts[idx] = st
                if idx >= LA:
                    i2 = idx - LA
                    j, mi = tiles[i2]
                    st = sts.pop(i2)
                    pt = self.rot("pt", pts)
                    self.act(pt[:, 0:n], st[:, 0:n], AF.Exp, scale=scale)
                    if mi is not None:
                        self.tt("pool" if i2 % 2 == 0 else "dve", pt[:, 0:n], pt[:, 0:n],
                                self.wmask[:, mi * 512:mi * 512 + n], ALU.mult)
                    self.mm(acc[:, 0:n], vaug(j), pt[:, 0:n], start=(i2 == 0), stop=(i2 == len(tiles) - 1))
            rec = self.rot("rec", recs)
            if sinkcol is not None:
                self.ts("dve", rec[ob:ob + 64, 0:n], acc[so:so + 64, 0:n], sinkcol[so:so + 64, :], None, ALU.add)
                self.recip(rec[ob:ob + 64, 0:n], rec[ob:ob + 64, 0:n])
            else:
                self.recip(rec[ob:ob + 64, 0:n], acc[so:so + 64, 0:n])
            self.tt("dve", ychunk[ob:ob + 64, q0:q0 + n], acc[ob:ob + 64, 0:n], rec[ob:ob + 64, 0:n], ALU.mult)

    def gqa_mixer(self, l, b, c0, ych0, qgain, kgain, window, sink):
        I = self.I
        m = self.mark()
        w = self.alloc(8 * 512, BF16).rearrange("p (k n) -> p k n", k=8)
        cos, sin = self.load_rope("ropeA")
        qT = self.alloc(4 * T, BF16).rearrange("p (c t) -> p c t", c=4)
        kT = self.alloc(2 * T, BF16).rearrange("p (c t) -> p c t", c=2)
        wk2 = self.alloc(8 * 256, BF16).rearrange("p (k n) -> p k n", k=8)
        VW = 320
        V = self.alloc(NTILE * VW, BF16).rearrange("p (i v) -> p i v", i=NTILE)
        m2 = self.mark()
        self.wst = [self.alloc(4096), self.alloc(4096)]
        self.load_w(w, I["w_in"][l][:, c0:c0 + 512], 8, 512)
        self.release(m2)
        pts = [self.alloc(512, BF16) for _ in range(4)]
        recs = [self.alloc(512) for _ in range(2)]
        tmp = self.rope_tmp()
        for c in range(2):
            self.cp("pool", wk2[:, :, c * 128:c * 128 + 64], w[:, :, 256 + c * 64:320 + c * 64])
            self.cp("pool", wk2[:, :, c * 128 + 64:c * 128 + 128], w[:, :, 256 + c * 64:320 + c * 64])
        self.ms("pool", V[:, :, 0:64], 1.0)
        self.ms("pool", V[:, :, 128:192], 1.0)
        self.ms("pool", V[:, :, 256:320], 1.0)
        for h in range(4):
            o = 64 - (h % 2) * 64
            self.ms("pool", qT[o:o + 64, h, :], 0.0)
        self.bankmap = self.BANKS_PROJ
        self.active = []
        for ti, (t0, n) in enumerate(TBS):
            for ci in range(4):
                pr = self.bank("pj")
                self.proj_fm(pr[:, 0:n], w if ci < 2 else wk2, (ci % 2) * 128, 128, t0, n)
                if ci < 2:
                    dst = [(0, 64, qT[0:64, 2 * ci, t0:t0 + n]), (64, 128, qT[64:128, 2 * ci + 1, t0:t0 + n])]
                else:
                    dst = kT[:, ci - 2, t0:t0 + n]
                g = None
                if qgain is not None:
                    g = qgain if ci < 2 else kgain
                self.pipe_step()
                self.pipe_add(self.norm_rope_gen(pr[:, 0:n], 0, 128, n, t0, dst, tmp, self.PA, cos, sin, gain=g,
                                                 ones=self.BD, nfeat=64))
            pv = self.bank("pj")
            nt = n // 128
            for ii in range(nt):
                self.proj_tm(pv[:, ii * 128:(ii + 1) * 128], w, 384, 128, t0 // 128 + ii)
            pvv = pv[:, 0:nt * 128].rearrange("p (i c) -> p i c", c=128)
            i0 = t0 // 128
            self.pipe_step()
            self.act(V[:, i0:i0 + nt, 64:128], pvv[:, :, 0:64], AF.Copy)
            self.act(V[:, i0:i0 + nt, 192:256], pvv[:, :, 64:128], AF.Copy)
        self.pipe_drain()
        scale = 64 ** -0.5
        self.bankmap = self.BANKS_ATT
        for h in range(4):
            kv = h // 2
            ob = (h % 2) * 64
            voff = (64 if ob == 0 else 0) + kv * 128
            self.attend(qT[:, h, :], kT[:, kv, :], 128, 0, lambda j, vo=voff: V[:, j, vo:vo + 128],
                        self.Y[:, ych0 + h // 2, :], ob, scale, window=window,
                        sinkcol=(self.sinkexp[:, h:h + 1] if sink else None), pts=pts, recs=recs)
        self.bankmap = self.BANKS_DEFAULT
        self.release(m)

    def pA(self, l, b):
        self.gqa_mixer(l, b, 0, 0, self.aqn, self.akn, False, False)

    def pC(self, l, b):
        self.gqa_mixer(l, b, 1544, 4, None, None, True, True)


    def pD(self, l, b):
        I = self.I
        m = self.mark()
        wD = self.alloc(8 * 352, BF16).rearrange("p (k n) -> p k n", k=8)
        wuq = self.alloc(2 * 384, BF16).rearrange("p (k n) -> p k n", k=2)
        wukv = self.alloc(512, BF16).rearrange("p (k n) -> p k n", k=1)
        wv = self.alloc(256, BF16)
        cos, sin = self.load_rope("ropeD")
        qT = self.alloc(4 * T, BF16).rearrange("p (c t) -> p c t", c=4)
        kT = self.alloc(4 * T, BF16).rearrange("p (c t) -> p c t", c=4)
        VW = 384
        V = self.alloc(NTILE * VW, BF16).rearrange("p (i v) -> p i v", i=NTILE)
        m2 = self.mark()
        self.wst = [self.alloc(4096), self.alloc(4096)]
        self.load_w(wD, I["w_in"][l][:, 2056:2408], 8, 352)
        self.load_w(wuq, I["mla_w_uq"][l], 2, 384, rows_last=64)
        self.load_w(wukv, I["mla_w_ukv"][l], 1, 512)
        self.release(m2)
        for h in range(4):
            self.cp("pool", wv[:, h * 64:(h + 1) * 64], wukv[:, 0, h * 128 + 64:(h + 1) * 128])
        pts = [self.alloc(512, BF16) for _ in range(4)]
        recs = [self.alloc(512) for _ in range(2)]
        tmp = self.rope_tmp()
        cq0s = [self.alloc(512, BF16) for _ in range(2)]
        cq1s = [self.alloc(512, BF16) for _ in range(2)]
        ckvs = [self.alloc(512, BF16) for _ in range(2)]
        sqks = [self.alloc(512, BF16) for _ in range(2)]
        self.ms("pool", V[:, :, 64:128], 1.0)
        self.ms("pool", V[:, :, 256:320], 1.0)
        self.ms("pool", qT[64:128, :, :], 0.0)
        self.ms("pool", kT[64:128, :, :], 0.0)
        self.bankmap = self.BANKS_PROJ
        self.active = []
        voffs = (0, 128, 192, 320)
        for ti, (t0, n) in enumerate(TBS):
            p0 = self.bank("pj")
            self.proj_fm(p0[:, 0:n], wD, 0, 128, t0, n)
            p1 = self.bank("pj")
            self.proj_fm(p1[0:64, 0:n], wD, 128, 64, t0, n)
            p2 = self.bank("pj")
            self.proj_fm(p2[:, 0:n], wD, 192, 128, t0, n)
            p3 = self.bank("pj")
            self.proj_fm(p3[64:96, 0:n], wD, 320, 32, t0, n, tp=(0, 64))
            self.pipe_step()
            sq0 = self.rot("rp_sq", tmp["sq"])
            self.act(sq0[:, 0:n], p0[:, 0:n], AF.Square)
            sq1 = self.rot("rp_sq", tmp["sq"])
            self.act(sq1[0:64, 0:n], p1[0:64, 0:n], AF.Square)
            sqk = self.rot("dsqk", sqks)
            self.act(sqk[:, 0:n], p2[:, 0:n], AF.Square)
            ss = self.bank("aux")
            self.mm(ss[:, 0:n], self.onesB, sq0[:, 0:n], start=True, stop=False)
            self.mm(ss[:, 0:n], self.onesB[0:64, :], sq1[0:64, 0:n], start=False, stop=True)
            ssk = self.bank("aux2")
            self.mm(ssk[:, 0:n], self.onesB, sqk[:, 0:n])
            rs = self.rot("rp_rs", tmp["rs"])
            self.rstd(rs[:, 0:n], ss[:, 0:n], 1.0 / 192)
            rsk = self.rot("rp_rs", tmp["rs"])
            self.rstd(rsk[:, 0:n], ssk[:, 0:n], 1.0 / 128)
            cq0 = self.rot("cq0", cq0s)
            cq1 = self.rot("cq1", cq1s)
            self.stt(cq0[:, 0:n], p0[:, 0:n], self.mqn0, rs[:, 0:n], ALU.mult, ALU.mult)
            self.stt(cq1[0:64, 0:n], p1[0:64, 0:n], self.mqn1[0:64, :], rs[0:64, 0:n], ALU.mult, ALU.mult)
            ckv = self.rot("ckv", ckvs)
            self.stt(ckv[:, 0:n], p2[:, 0:n], self.kvn, rsk[:, 0:n], ALU.mult, ALU.mult)
            def krot_gen(p3=p3, t0=t0, n=n):
                yield from self.norm_rope_gen(p3[64:96, 0:n], 64, 96, n, t0, kT[64:96, 0, t0:t0 + n], tmp, self.PD,
                                              cos, sin)
                for h in range(1, 4):
                    self.cp("pool", kT[64:96, h, t0:t0 + n], kT[64:96, 0, t0:t0 + n])
            self.pipe_step()
            self.pipe_add(krot_gen())
            for h in range(4):
                pq = self.bank("pj")
                self.mm(pq[0:96, 0:n], wuq[:, 0, h * 96:(h + 1) * 96], cq0[:, 0:n], start=True, stop=False)
                self.mm(pq[0:96, 0:n], wuq[0:64, 1, h * 96:(h + 1) * 96], cq1[0:64, 0:n], start=False, stop=True)
                self.pipe_step()
                self.pipe_add(self.norm_rope_gen(pq[0:96, 0:n], 0, 96, n, t0, qT[0:96, h, t0:t0 + n], tmp, self.PD,
                                                 cos, sin))
            for h in range(4):
                pk = self.bank("pj")
                self.mm(pk[0:64, 0:n], wukv[:, 0, h * 128:h * 128 + 64], ckv[:, 0:n])
                self.cp("act", kT[0:64, h, t0:t0 + n], pk[0:64, 0:n])
            nt = n // 128
            i0 = t0 // 128
            for pr in range(nt // 2):
                pv = self.bank("pj")
                for ii in range(2):
                    tok = (pr * 2 + ii) * 128
                    self.mm(pv[:, ii * 256:(ii + 1) * 256], ckv[:, tok:tok + 128], wv)
                pvv = pv[:, :].rearrange("p (i c) -> p i c", c=256)
                for h in range(4):
                    self.cp("act" if h % 2 == 0 else "dve", V[:, i0 + pr * 2:i0 + pr * 2 + 2, voffs[h]:voffs[h] + 64],
                            pvv[:, :, h * 64:(h + 1) * 64])
        self.pipe_drain()
        scale = 96 ** -0.5
        self.bankmap = self.BANKS_ATT
        for h in range(4):
            ob = (h % 2) * 64
            vo = voffs[h] - ob
            self.attend(qT[:, h, :], kT[:, h, :], 128, 0, lambda j, vo=vo: V[:, j, vo:vo + 128],
                        self.Y[:, 6 + h // 2, :], ob, scale, pts=pts, recs=recs)
        self.bankmap = self.BANKS_DEFAULT
        self.release(m)

    def pB(self, l, b):
        I = self.I
        m = self.mark()
        BT = self.alloc(2 * T, BF16).rearrange("p (g t) -> p g t", g=2)
        CT = self.alloc(2 * T, BF16).rearrange("p (g t) -> p g t", g=2)
        xs_tok = self.alloc(NTILE * 256, BF16).rearrange("p (i c) -> p i c", i=NTILE)
        B_tok = self.alloc(NTILE * 256, BF16).rearrange("p (i c) -> p i c", i=NTILE)
        z_tok = self.alloc(NTILE * 256, BF16).rearrange("p (i c) -> p i c", i=NTILE)
        dt = self.alloc(144)
        da = self.alloc(144)
        cs = self.alloc(144)
        tot = self.alloc(144)
        ecs = self.alloc(144)
        dtw = self.alloc(144)
        cdec = self.alloc(144)
        ncs = self.alloc(144)
        v3 = lambda a: a.rearrange("p (i c) -> p i c", c=8)
        m2 = self.mark()
        wB = self.alloc(8 * 1032, BF16).rearrange("p (k n) -> p k n", k=8)
        m3 = self.mark()
        self.wst_cap = 2064
        self.wst = [self.alloc(2064), self.alloc(2064)]
        self.load_w(wB, I["w_in"][l][:, 512:1544], 8, 1032)
        self.wst_cap = 4096
        self.release(m3)
        Gpl = [self.alloc(GW), self.alloc(GW)]
        accl = [self.alloc(GW), self.alloc(GW)]
        xsT = self.alloc(2 * T, BF16).rearrange("p (c t) -> p c t", c=2)
        pdt = self.bank("aux")
        for i in range(NTILE):
            self.proj_tm(pdt[:, i * 8:(i + 1) * 8], wB, 1024, 8, i)
        xr = ecs
        self.tt("dve", v3(xr), v3(pdt[:, 0:144]), self.dtb.unsqueeze(1).to_broadcast([128, NTILE, 8]), ALU.add)
        self.ts("dve", cs, xr, -1.0, None, ALU.mult)
        self.tt("dve", cs, cs, xr, ALU.max)
        self.act(cs, cs, AF.Exp, scale=-1.0)
        self.act(cs, cs, AF.Ln, bias=1.0)
        self.ts("dve", xr, xr, 0.0, None, ALU.max)
        self.tt("dve", dt, xr, cs, ALU.add)
        self.tt("dve", v3(da), v3(dt), self.arow.unsqueeze(1).to_broadcast([128, NTILE, 8]), ALU.mult)
        if BSTOP == 1:
            self.top = m
            return
        for ip in range(NTILE // 2):
            pz = self.bank("pj")
            for ii in range(2):
                self.proj_tm(pz[:, ii * 256:(ii + 1) * 256], wB, 0, 256, ip * 2 + ii)
            self.act(z_tok[:, ip * 2:ip * 2 + 2, :], pz[:, :].rearrange("p (i c) -> p i c", c=256), AF.Silu)
        if BSTOP == 2:
            self.top = m
            return
        for Gp in Gpl:
            for c0 in (0, 2049, 2050, 2307):
                self.ms("pool", Gp[:, c0:c0 + 1], 0.0)
        self.bankmap = self.BANKS_PROJ
        W = 2306

        def silu_out(cidx):
            acc = accl[cidx % 2]
            dstT = xsT[:, cidx, :] if cidx < 2 else (BT[:, cidx - 2, :] if cidx < 4 else CT[:, cidx - 4, :])
            self.act(dstT[:, 0:NLAT], acc[:, 0:NLAT], AF.Silu)
            self.act(dstT[:, NLAT:T], acc[:, 2050:2306], AF.Silu)
        def tok_major(i0_, i1_):
            for i in range(i0_, i1_):
                ptr = self.pss[6 + i % 2].bitcast(BF16)
                for c in range(2):
                    self.tr(ptr[:, c * 128:(c + 1) * 128], xsT[:, c, i * 128:(i + 1) * 128], self.identB)
                    self.tr(ptr[:, 256 + c * 128:256 + (c + 1) * 128], BT[:, c, i * 128:(i + 1) * 128], self.identB)
                self.cp("act", xs_tok[:, i, :], ptr[:, 0:256])
                self.cp("dve", B_tok[:, i, :], ptr[:, 256:512])
        for cidx in range(7):
            if cidx < 6:
                Gp = Gpl[cidx % 2]
                acc = accl[cidx % 2]
                for ti, (t0, n) in enumerate(TBS):
                    pr = self.bank("pj")
                    self.proj_fm(pr[:, 0:n], wB, 256 + cidx * 128, 128, t0, n)
                    off = 1 + t0 if ti < 4 else 2051
                    self.cp("dve" if ti % 2 else "act", Gp[:, off:off + n], pr[:, 0:n])
                self.act(acc[:, 0:W], Gp[:, 1:1 + W], AF.Identity, bias=self.scb[:, cidx:cidx + 1],
                         scale=self.scw[:, cidx * 3 + 1:cidx * 3 + 2])
                self.stt(acc[:, 0:W], Gp[:, 0:W], self.scw[:, cidx * 3:cidx * 3 + 1], acc[:, 0:W], ALU.mult, ALU.add)
                self.stt(acc[:, 0:W], Gp[:, 2:2 + W], self.scw[:, cidx * 3 + 2:cidx * 3 + 3], acc[:, 0:W],
                         ALU.mult, ALU.add)
            if cidx > 0:
                silu_out(cidx - 1)
            if cidx == 4:
                tok_major(0, 9)
            if cidx == 5:
                tok_major(9, NTILE)
        self.bankmap = self.BANKS_DEFAULT
        if BSTOP == 3:
            self.top = m
            return
        self.release(m2)
        if BSTOP == 4:
            self.top = m
            return
        ybuf = self.alloc(NTILE * 256).rearrange("p (i c) -> p i c", i=NTILE)
        H = [self.alloc(256), self.alloc(256)]
        Hb = [self.alloc(256, BF16), self.alloc(256, BF16)]
        dtr4 = [[self.alloc(512) for _ in range(2)] for _ in range(2)]
        decs = [self.alloc(128) for _ in range(8)]
        scs = [self.alloc(128, BF16) for _ in range(8)]
        tmpos = [self.alloc(256) for _ in range(4)]
        xsws = [self.alloc(256, BF16) for _ in range(4)]
        t1s = [self.alloc(256) for _ in range(2)]
        t2s = [self.alloc(256) for _ in range(2)]
        yns = [self.alloc(256, BF16) for _ in range(2)]
        junk = self.alloc(256, BF16)
        ssq = self.alloc(NTILE)
        pcs = self.bank("aux")
        ptot = self.bank("aux2")
        dav = v3(da)
        for j in range(NTILE):
            self.mm(pcs[:, j * 8:j * 8 + 4], self.triF, dav[:, j, 0:4])
            self.mm(pcs[:, j * 8 + 4:j * 8 + 8], self.triB, dav[:, j, 4:8])
            self.mm(ptot[:, j * 8:(j + 1) * 8], self.onesF, dav[:, j, :])
        self.cp("dve", cs, pcs[:, 0:144])
        self.cp("dve", tot, ptot[:, 0:144])
        self.act(ecs, cs, AF.Exp)
        self.tt("dve", dtw, tot, cs, ALU.subtract)
        self.act(dtw, dtw, AF.Exp)
        self.tt("dve", dtw, dtw, dt, ALU.mult)
        self.act(cdec, tot, AF.Exp)
        self.ts("dve", ncs, cs, -1.0, None, ALU.mult)
        if BSTOP == 5:
            self.top = m
            return
        self.ms("pool", ybuf.rearrange("p i c -> p (i c)"), 0.0)
        for d in range(2):
            self.ms("pool", H[d], 0.0)
            self.ms("pool", Hb[d], 0.0)
        forder = [16, 17] + list(range(16))
        border = [17, 16] + list(range(15, -1, -1))
        tri = (self.triF, self.triB)
        h3 = lambda a: a.rearrange("p (h c) -> p h c", h=4)
        hd = [(h, d) for h in range(4) for d in range(2)]
        pEb = (((self.pss[0], self.pss[1])), ((self.pss[5], self.pss[7])))
        pGb = self.pss[2]
        pSb = self.pss[3]
        pYb = self.pss[4]
        pOb = self.pss[6]

        def stage_a(i):
            p = i % 2
            for d in range(2):
                for h in range(4):
                    col = i * 8 + d * 4 + h
                    self.ts("pool", dtr4[p][d][:, h * 128:(h + 1) * 128], tri[d], da[:, col:col + 1], 1.0,
                            ALU.mult, ALU.mult)
                self.mm(pEb[p][d][:, :], self.onesF, dtr4[p][d], start=True, stop=False)
                self.mm(pEb[p][d][:, :], self.identB, self.nmb4[d], start=False, stop=True)

        def stage_g(i):
            for g in range(2):
                self.mm(pGb[:, g * 128:(g + 1) * 128], BT[:, g, i * 128:(i + 1) * 128], CT[:, g, i * 128:(i + 1) * 128])

        def stage_bc(i):
            p = i % 2
            for q_, (h, d) in enumerate(hd):
                col = i * 8 + d * 4 + h
                self.act(decs[q_], pEb[p][d][:, h * 128:(h + 1) * 128], AF.Exp, bias=ncs[:, col:col + 1])
            for q_, (h, d) in enumerate(hd):
                col = i * 8 + d * 4 + h
                g = h // 2
                self.stt(scs[q_], decs[q_], dt[:, col:col + 1], pGb[:, g * 128:(g + 1) * 128], ALU.mult, ALU.mult)

        def stage_d(i):
            for q_, (h, d) in enumerate(hd):
                self.mm(pYb[:, h * 64:(h + 1) * 64], scs[q_], xs_tok[:, i, h * 64:(h + 1) * 64],
                        start=(d == 0), stop=(d == 1))

        def scan(i):
            dj = ((0, forder[i]), (1, border[i]))
            xsw = {}
            for d, jj in dj:
                c4 = jj * 8 + d * 4
                xsw[d] = self.rot("xsw", xsws)
                self.tt("pool", h3(xsw[d]), h3(xs_tok[:, jj, :]), dtw[:, c4:c4 + 4].unsqueeze(2).to_broadcast([128, 4, 64]),
                        ALU.mult)
            for d, jj in dj:
                for h in range(4):
                    self.mm(pOb[:, d * 256 + h * 64:d * 256 + (h + 1) * 64], CT[:, h // 2, jj * 128:(jj + 1) * 128],
                            Hb[d][:, h * 64:(h + 1) * 64])
            for d, jj in dj:
                for h in range(4):
                    g = h // 2
                    self.mm(pSb[:, d * 256 + h * 64:d * 256 + (h + 1) * 64], B_tok[:, jj, g * 128:(g + 1) * 128],
                            xsw[d][:, h * 64:(h + 1) * 64])
            tm = {}
            for d, jj in dj:
                c4 = jj * 8 + d * 4
                tm[d] = self.rot("tmpo", tmpos)
                self.tt("dve", h3(tm[d]), h3(pOb[:, d * 256:(d + 1) * 256]),
                        ecs[:, c4:c4 + 4].unsqueeze(2).to_broadcast([128, 4, 64]), ALU.mult)
            for d, jj in dj:
                c4 = jj * 8 + d * 4
                self.tt("dve", h3(H[d]), h3(H[d]), cdec[:, c4:c4 + 4].unsqueeze(2).to_broadcast([128, 4, 64]), ALU.mult)
                self.tt("dve", H[d], H[d], pSb[:, d * 256:(d + 1) * 256], ALU.add)
                self.cp("act", Hb[d], H[d])
            self.tt("dve", ybuf[:, i, :], ybuf[:, i, :], pYb[:, 0:256], ALU.add)
            for d, jj in dj:
                self.tt("pool", ybuf[:, jj, :], ybuf[:, jj, :], tm[d], ALU.add)
        stage_a(0)
        stage_g(0)
        for i in range(NTILE):
            stage_bc(i)
            if i + 1 < NTILE:
                stage_a(i + 1)
            stage_d(i)
            if i + 1 < NTILE:
                stage_g(i + 1)
            scan(i)
        if BSTOP == 6:
            self.top = m
            return
        for j in range(NTILE):
            t1 = self.rot("bt1", t1s)
            self.tt("dve", t1, xs_tok[:, j, :], self.Drow, ALU.mult)
            self.tt("dve", ybuf[:, j, :], t1, ybuf[:, j, :], ALU.add)
            self.tt("pool", ybuf[:, j, :], ybuf[:, j, :], z_tok[:, j, :], ALU.mult)
            self.act(junk, ybuf[:, j, :], AF.Square, accum=ssq[:, j:j + 1])
        self.rstd(ssq, ssq, 1.0 / 256)
        for j in range(NTILE):
            yn = self.rot("byn", yns)
            self.ts("dve", yn, ybuf[:, j, :], ssq[:, j:j + 1], None, ALU.mult)
            ptr = self.bank("acc").bitcast(BF16)
            for c in range(2):
                self.tr(ptr[:, c * 128:(c + 1) * 128], yn[:, c * 128:(c + 1) * 128], self.identB)
            self.act(self.Y[:, 2, j * 128:(j + 1) * 128], ptr[:, 0:128], AF.Identity, scale=self.snw[:, 0:1])
            self.ts("dve", self.Y[:, 3, j * 128:(j + 1) * 128], ptr[:, 128:256], self.snw[:, 1:2], None, ALU.mult)
        self.release(m)

    def pO(self, l, b):
        I = self.I
        m = self.mark()
        wo = self.alloc(8 * 1024, BF16).rearrange("p (k n) -> p k n", k=8)
        m2 = self.mark()
        self.wst = [self.alloc(4096), self.alloc(4096)]
        self.load_w(wo, I["w_out"][l], 8, 1024)
        self.release(m2)
        xbs = [self.alloc(4096), self.alloc(4096)]
        tmp = self.norm_tmp()
        def ldx(ti):
            t0_, n_ = TBS[ti]
            v = xbs[ti % 2].rearrange("p (k t) -> p k t", k=8)
            self.ld(v[:, :, 0:n_], self.xt_view(b, t0_, n_), "xb%d" % (ti % 2), rk=self.tb_keys(b, ti))
        ldx(0)
        for ti, (t0, n) in enumerate(TBS):
            s = ti % 2
            xb = xbs[s].rearrange("p (k t) -> p k t", k=8)
            if ti + 1 < len(TBS):
                ldx(ti + 1)
            r = b if ti < 4 else 2
            for dc in range(8):
                po = self.bank("pj")
                for k in range(8):
                    self.mm(po[:, 0:n], wo[:, k, dc * 128:(dc + 1) * 128], self.Y[:, k, t0:t0 + n],
                            start=(k == 0), stop=(k == 7))
                self.stt(xb[:, dc, 0:n], po[:, 0:n], self.modv[:, 2, dc, r:r + 1], xb[:, dc, 0:n], ALU.mult, ALU.add)
            self.st(self.xt_view(b, t0, n), xb[:, :, 0:n], "xst%d" % s, wk=self.tb_keys(b, ti))
            self.norm_mod_block(xb, n, t0, self.A2v, 3, r, tmp)
        self.release(m)

    def pF(self, l, b):
        I = self.I
        self.release(self.y_mark)
        m = self.mark()
        wd = self.alloc(NFC * 1024, BF16).rearrange("p (k n) -> p k n", k=NFC)
        wd_mark = self.mark()
        wup = [self.alloc(8 * 256, BF16).rearrange("p (k n) -> p k n", k=8) for _ in range(2)]
        stgs = [self.alloc(2048).rearrange("p (k n) -> p k n", k=8) for _ in range(2)]
        dstg = [self.alloc(1024) for _ in range(2)]
        Gps = [self.alloc(GW) for _ in range(2)]
        accs = [self.alloc(GW) for _ in range(2)]
        abufs = [self.alloc(T, BF16) for _ in range(2)]
        sgs = [self.alloc(T, BF16) for _ in range(2)]
        us = [self.alloc(T, BF16) for _ in range(2)]
        for Gp in Gps:
            for c0 in (0, 2049, 2050, 2307):
                self.ms("pool", Gp[:, c0:c0 + 1], 0.0)
        wupd = I["ffn_w_up"][l]
        wdd = I["ffn_w_down"][l]
        W = 2306

        def ldw(fc_):
            s_ = fc_ % 2
            self.ld(stgs[s_][:, :, 0:128], wupd[:, fc_ * 128:(fc_ + 1) * 128].rearrange("(k p) n -> p k n", p=128),
                    "wup%d" % s_)
            self.ld(stgs[s_][:, :, 128:256],
                    wupd[:, DFF + fc_ * 128:DFF + (fc_ + 1) * 128].rearrange("(k p) n -> p k n", p=128), "wup%d" % s_)
            self.ld(dstg[s_], wdd[fc_ * 128:(fc_ + 1) * 128, :], "wdn%d" % s_)

        def conv_stage(fc_, stage):
            s_ = fc_ % 2
            Gp, acc, abuf, sg, u = Gps[s_], accs[s_], abufs[s_], sgs[s_], us[s_]
            if stage == 0:
                self.act(acc[:, 0:W], Gp[:, 1:1 + W], AF.Identity, bias=self.fcb[:, fc_:fc_ + 1],
                         scale=self.fcw[:, fc_ * 3 + 1:fc_ * 3 + 2])
            elif stage == 1:
                self.stt(acc[:, 0:W], Gp[:, 0:W], self.fcw[:, fc_ * 3:fc_ * 3 + 1], acc[:, 0:W], ALU.mult, ALU.add)
            elif stage == 2:
                self.stt(acc[:, 0:W], Gp[:, 2:2 + W], self.fcw[:, fc_ * 3 + 2:fc_ * 3 + 3], acc[:, 0:W],
                         ALU.mult, ALU.add)
            elif stage == 3:
                self.act(sg[:, 0:NLAT], acc[:, 0:NLAT], AF.Silu)
                self.act(sg[:, NLAT:T], acc[:, 2050:2306], AF.Silu)
            else:
                self.tt("pool", u, abuf, sg, ALU.mult)
                self.st(self.usc[fc_], u, "ust%d" % s_, wk=[("usc", fc_)])
        ldw(0)
        ldw(1)
        self.cp("act", wup[0], stgs[0])
        self.cp("act", wd[:, 0, :], dstg[0])
        for fc in range(NFC + 1):
            s = fc % 2
            for ti, (t0, n) in enumerate(TBS):
                if fc < NFC:
                    w = wup[s]
                    pa = self.bank("pj")
                    for k_ in range(8):
                        self.mm(pa[:, 0:n], w[:, k_, 0:128], self.hT[:, k_, t0:t0 + n], start=(k_ == 0), stop=(k_ == 7))
                    pg = self.bank("s")
                    for k_ in range(8):
                        self.mm(pg[:, 0:n], w[:, k_, 128:256], self.hT[:, k_, t0:t0 + n], start=(k_ == 0), stop=(k_ == 7))
                    off = 1 + t0 if ti < 4 else 2051
                    self.cp("act", abufs[s][:, t0:t0 + n], pa[:, 0:n])
                    self.cp("dve", Gps[s][:, off:off + n], pg[:, 0:n])
                if fc > 0:
                    conv_stage(fc - 1, ti)
                if ti == 1 and fc + 1 < NFC:
                    self.cp("act", wup[1 - s], stgs[1 - s])
                    self.cp("act", wd[:, fc + 1, :], dstg[1 - s])
                if ti == 2 and fc + 2 < NFC:
                    ldw(fc + 2)
        self.release(wd_mark)
        ubs = [self.alloc(NFC * 512, BF16).rearrange("p (f t) -> p f t", f=NFC) for _ in range(2)]
        xbs = [self.alloc(4096), self.alloc(4096)]
        last = (l == 1)
        if last:
            tmp = self.norm_tmp()
            ots = [self.alloc(1024), self.alloc(1024)]
        ukeys = [("usc", fc) for fc in range(NFC)]
        nblk = 4 if last else 5

        def ldd(ti_):
            t0_, n_ = TBS[ti_]
            s_ = ti_ % 2
            self.ld(ubs[s_][:, :, 0:n_], self.usc[:, :, t0_:t0_ + n_].rearrange("f p t -> p f t"), "ub%d" % s_, rk=ukeys)
            v = xbs[s_].rearrange("p (k t) -> p k t", k=8)
            self.ld(v[:, :, 0:n_], self.xt_view(b, t0_, n_), "xb%d" % s_, rk=self.tb_keys(b, ti_))
        for ti, (t0, n) in enumerate(TBS):
            if last and ti == 4:
                break
            s = ti % 2
            ub = ubs[s]
            xb = xbs[s].rearrange("p (k t) -> p k t", k=8)
            if ti == 0:
                ldd(0)
            if ti + 1 < nblk:
                ldd(ti + 1)
            r = b if ti < 4 else 2
            for dc in range(8):
                po = self.bank("pj")
                for fc in range(NFC):
                    self.mm(po[:, 0:n], wd[:, fc, dc * 128:(dc + 1) * 128], ub[:, fc, 0:n],
                            start=(fc == 0), stop=(fc == NFC - 1))
                self.stt(xb[:, dc, 0:n], po[:, 0:n], self.modv[:, 5, dc, r:r + 1], xb[:, dc, 0:n], ALU.mult, ALU.add)
            if not last:
                self.st(self.xt_view(b, t0, n), xb[:, :, 0:n], "xst%d" % s, wk=self.tb_keys(b, ti))
                continue
            ss = self.bank("aux")
            for k in range(8):
                sq = self.rot("sqb", tmp["sq"])
                self.act(sq[:, 0:n], xb[:, k, 0:n], AF.Square)
                self.mm(ss[:, 0:n], self.onesB, sq[:, 0:n], start=(k == 0), stop=(k == 7))
            rs = self.rot("rsb", tmp["rs"])
            self.rstd(rs[:, 0:n], ss[:, 0:n], 1.0 / D)
            for k in range(8):
                self.stt(xb[:, k, 0:n], xb[:, k, 0:n], self.fnw[:, k:k + 1], rs[:, 0:n], ALU.mult, ALU.mult)
            for tt_ in range(n // 128):
                ot = self.rot("ot", ots)
                p0 = self.bank("s")
                p1 = self.bank("acc")
                for k in range(8):
                    pp = p0 if k < 4 else p1
                    self.tr(pp[:, (k % 4) * 128:(k % 4 + 1) * 128], xb[:, k, tt_ * 128:(tt_ + 1) * 128], self.identF)
                self.cp("act", ot[:, 0:512], p0[:, :])
                self.cp("dve", ot[:, 512:1024], p1[:, :])
                tok = t0 + tt_ * 128
                self.st(self.out[b, tok:tok + 128, :], ot, "ost%d" % (self.rotc["ot"] % 2), wk=[("out", b, tok)])
                self.outkeys.append(("out", b, tok))
        self.release(m)

    def build(self, stop=None):
        self.outkeys = []
        self.P.cur_tag = "setup"
        self.setup_consts()
        self.P.cur_tag = "x0"
        self.phase_x0()
        done = False
        for l in range(2):
            self.P.cur_tag = ("lsetup", l)
            self.layer_setup(l)
            for b in range(2):
                self.lb_mark = self.mark()
                self.hT = self.alloc(8 * T, BF16).rearrange("p (k t) -> p k t", k=8)
                self.y_mark = self.mark()
                self.Y = self.alloc(8 * T, BF16).rearrange("p (k t) -> p k t", k=8)
                for pi, ph in enumerate(("p1", "pA", "pB", "pC", "pD", "pO", "pF")):
                    self.P.cur_tag = (l, b, ph)
                    if MARK:
                        self.act(self.mk, self.mk, AF.Abs)
                    getattr(self, ph)(l, b)
                    if stop == (l, b, ph):
                        done = True
                        break
                if done:
                    break
                self.release(self.lb_mark)
            if done:
                break
        if done:
            self.dump("hT", self.hT.rearrange("p k t -> p (k t)"), 8 * T)
            self.dump("Y", self.Y.rearrange("p k t -> p (k t)"), 8 * T)
        keys = list(self.outkeys) + [("dbg", n, c) for n in self.dump_aps for c in range(0, 8 * T, 512)]
        self.P.op("sp", lambda e: e.nop(), reads=keys, writes=[])


def build_program(stop=None, dumps=()):
    nc = bass.Bass("TRN2", target_bir_lowering=False)
    with ExitStack() as es:
        kb = KB(nc, es, dumps)
        kb.build(stop)
        kb.P.finalize(kb.sem)
        print("ops", len(kb.P.allops), "waits", kb.P.nwaits, "sems", kb.nsem, flush=True)
        with nc.Block() as block:
            kb.P.emit(block)
    return nc


def make_in_maps(inputs):
    cst = host_consts()
    maps = []
    for core in range(8):
        m = {}
        for name, shp in WSPEC:
            if name in cst:
                m[name] = cst[name]
            elif name in ("x", "c", "ctx"):
                m[name] = np.ascontiguousarray(np.asarray(inputs[name], dtype=np.float32)[2 * core:2 * core + 2])
            else:
                m[name] = np.ascontiguousarray(np.asarray(inputs[name], dtype=np.float32))
        maps.append(m)
    return maps


_NC_CACHE = {}


def kernel(**inputs):
    if "nc" not in _NC_CACHE:
        _NC_CACHE["nc"] = build_program()
    nc = _NC_CACHE["nc"]
    maps = make_in_maps(inputs)
    res = run_bass_kernel_spmd(nc, maps, core_ids=list(range(8)))
    return np.concatenate([np.asarray(r["out"]) for r in res.results], axis=0).astype(np.float32)
```
